# Optimizing a Trainium2 kernel written in Bass

```python
import math, functools
import jax, jax.numpy as jnp
from jax import lax
import numpy as np

D_MODEL = 1024
BATCH = 4
SEQ = 4096
DEPTH = 1
DEC_BATCH = 128
DEC_SEQ = 8
PAST_LEN = 16384
PAGE_SIZE = 128

MLA_HEADS = 16
QK_NOPE = 64
QK_ROPE = 32
V_HEAD = 64
Q_LORA = 512
KV_LORA = 256
ROPE_THETA = 10000.0
ATTN_SCALE = (QK_NOPE + QK_ROPE) ** -0.5
Q_BLOCK = 128
SSM_INNER = 2 * D_MODEL
SSM_HEAD_DIM = 64
SSM_HEADS = SSM_INNER // SSM_HEAD_DIM
SSM_GROUPS = 4
SSM_STATE = 128
CONV_WIDTH = 4
SSM_CHUNK = 128
CONV_DIM = SSM_INNER + 2 * SSM_GROUPS * SSM_STATE
D_FF = 2816
FFN_RES = 0.5
N_SUB = 3
EPS = 1e-6
IN_SIZES = (Q_LORA, KV_LORA, QK_ROPE, SSM_INNER, CONV_DIM, SSM_HEADS, D_MODEL, D_MODEL)
IN_OFFSETS = tuple(int(v) for v in np.cumsum(IN_SIZES)[:-1])
D_IN = int(sum(IN_SIZES))

kernel_name = 'hybrid_mla_ssd_macaron_step'


def rms_norm(x, g):
    xf = x.astype(jnp.float32)
    y = xf * lax.rsqrt(jnp.mean(xf * xf, axis=-1, keepdims=True) + EPS)
    return (y * g.astype(jnp.float32)).astype(x.dtype)


def rope(x, pos):
    half = QK_ROPE // 2
    inv = ROPE_THETA ** (-jnp.arange(half, dtype=jnp.float32) / half)
    ang = pos.astype(jnp.float32)[:, None] * inv[None, :]
    shape = (1, pos.shape[0]) + (1,) * (x.ndim - 3) + (half,)
    cos, sin = jnp.cos(ang).reshape(shape), jnp.sin(ang).reshape(shape)
    xf = x.astype(jnp.float32)
    x1, x2 = xf[..., :half], xf[..., half:]
    return jnp.concatenate([x1 * cos - x2 * sin, x1 * sin + x2 * cos], axis=-1).astype(x.dtype)


def _modulate(x, g, shift, scale):
    return rms_norm(x, g) * (1 + scale[:, None, :]) + shift[:, None, :]


def _swiglu(h, w_gate, w_up, w_down):
    return (jax.nn.silu(h @ w_gate) * (h @ w_up)) @ w_down


def _causal_conv(u, w, b):
    out = lax.conv_general_dilated(u, w[:, None, :], window_strides=(1,), padding='VALID',
                                   dimension_numbers=('NWC', 'WIO', 'NWC'),
                                   feature_group_count=u.shape[-1])
    return out + b


def _mla_project(q_lat, kv_lat, k_pe_raw, pos, g_q_lat, w_uq, g_kv_lat):
    bsz, T = q_lat.shape[:2]
    q = (rms_norm(q_lat, g_q_lat) @ w_uq).reshape(bsz, T, MLA_HEADS, QK_NOPE + QK_ROPE)
    q_nope, q_pe = q[..., :QK_NOPE], rope(q[..., QK_NOPE:], pos)
    c_kv = rms_norm(kv_lat, g_kv_lat)
    k_pe = rope(k_pe_raw, pos)
    return q_nope, q_pe, c_kv, k_pe


def _mla_prompt_attention(q_nope, q_pe, c_kv, k_pe, w_ukv):
    bsz, T = q_nope.shape[:2]
    w = w_ukv.reshape(KV_LORA, MLA_HEADS, QK_NOPE + V_HEAD)
    kv = jnp.einsum('btc,chd->bthd', c_kv, w)
    k_nope, v = kv[..., :QK_NOPE], kv[..., QK_NOPE:]
    q = jnp.concatenate([q_nope, q_pe], axis=-1)
    k = jnp.concatenate([k_nope, jnp.broadcast_to(k_pe[:, :, None, :], (bsz, T, MLA_HEADS, QK_ROPE))], axis=-1)
    n_blocks = T // Q_BLOCK
    q_blocks = jnp.moveaxis(q.reshape(bsz, n_blocks, Q_BLOCK, MLA_HEADS, QK_NOPE + QK_ROPE), 1, 0)
    key_pos = jnp.arange(T)

    def one_block(args):
        q_blk, blk = args
        s = jnp.einsum('bqhd,bkhd->bhqk', q_blk, k).astype(jnp.float32) * ATTN_SCALE
        q_pos = blk * Q_BLOCK + jnp.arange(Q_BLOCK)
        s = jnp.where(key_pos[None, :] <= q_pos[:, None], s, -jnp.inf)
        p = jax.nn.softmax(s, axis=-1).astype(v.dtype)
        return jnp.einsum('bhqk,bkhd->bqhd', p, v)

    out = lax.map(one_block, (q_blocks, jnp.arange(n_blocks)))
    return jnp.moveaxis(out, 0, 1).reshape(bsz, T, MLA_HEADS * V_HEAD)


def _mla_sample_attention(q_nope, q_pe, c_kv, k_pe, w_ukv, past_c, past_pe):
    bsz, T = q_nope.shape[:2]
    w = w_ukv.reshape(KV_LORA, MLA_HEADS, QK_NOPE + V_HEAD)
    w_uk, w_uv = w[..., :QK_NOPE], w[..., QK_NOPE:]
    q_abs = jnp.einsum('bthd,chd->bthc', q_nope, w_uk)
    s_past = (jnp.einsum('bthc,bpc->bhtp', q_abs, past_c).astype(jnp.float32)
              + jnp.einsum('bthr,bpr->bhtp', q_pe, past_pe).astype(jnp.float32))
    s_new = (jnp.einsum('bthc,bsc->bhts', q_abs, c_kv).astype(jnp.float32)
             + jnp.einsum('bthr,bsr->bhts', q_pe, k_pe).astype(jnp.float32))
    s_new = jnp.where(jnp.tril(jnp.ones((T, T), dtype=bool)), s_new, -jnp.inf)
    p = jax.nn.softmax(jnp.concatenate([s_past, s_new], axis=-1) * ATTN_SCALE, axis=-1).astype(q_nope.dtype)
    n_past = past_c.shape[1]
    o_lat = (jnp.einsum('bhtp,bpc->bthc', p[..., :n_past], past_c)
             + jnp.einsum('bhts,bsc->bthc', p[..., n_past:], c_kv))
    o = jnp.einsum('bthc,chd->bthd', o_lat, w_uv)
    return o.reshape(bsz, T, MLA_HEADS * V_HEAD)


def _ssd_chunked_scan(xs, dt, a, b_in, c_in, h0):
    bsz, T = xs.shape[:2]
    L = min(SSM_CHUNK, T)
    pad = (-T) % L
    R = SSM_HEADS // SSM_GROUPS
    xdt = xs.astype(jnp.float32) * dt[..., None]
    la = dt * a
    bf, cf = b_in.astype(jnp.float32), c_in.astype(jnp.float32)
    if pad:
        pad_t = lambda t: jnp.pad(t, [(0, 0), (0, pad)] + [(0, 0)] * (t.ndim - 2))
        xdt, la, bf, cf = pad_t(xdt), pad_t(la), pad_t(bf), pad_t(cf)
    nc = (T + pad) // L
    xdt = xdt.reshape(bsz, nc, L, SSM_GROUPS, R, SSM_HEAD_DIM)
    la = la.reshape(bsz, nc, L, SSM_GROUPS, R)
    bf = bf.reshape(bsz, nc, L, SSM_GROUPS, SSM_STATE)
    cf = cf.reshape(bsz, nc, L, SSM_GROUPS, SSM_STATE)
    cum = jnp.cumsum(la, axis=2)
    causal = jnp.tril(jnp.ones((L, L), dtype=bool))[:, :, None, None]
    seg = cum[:, :, :, None] - cum[:, :, None, :]
    decay = jnp.exp(jnp.where(causal, seg, -jnp.inf))
    cb = jnp.einsum('bclgn,bcsgn->bclsg', cf, bf)
    y_diag = jnp.einsum('bclsgr,bcsgrp->bclgrp', cb[..., None] * decay, xdt)
    to_end = jnp.exp(cum[:, :, -1:] - cum)
    chunk_states = jnp.einsum('bclgn,bclgr,bclgrp->bcgrpn', bf, to_end, xdt)
    chunk_decay = jnp.exp(cum[:, :, -1])

    def step(h, inp):
        s_c, d_c = inp
        return h * d_c[..., None, None] + s_c, h

    h_init = h0.astype(jnp.float32).reshape(bsz, SSM_GROUPS, R, SSM_HEAD_DIM, SSM_STATE)
    h_final, h_enter = lax.scan(step, h_init, (jnp.moveaxis(chunk_states, 1, 0), jnp.moveaxis(chunk_decay, 1, 0)))
    h_enter = jnp.moveaxis(h_enter, 0, 1)
    y_off = jnp.einsum('bclgn,bcgrpn,bclgr->bclgrp', cf, h_enter, jnp.exp(cum))
    y = (y_diag + y_off).reshape(bsz, nc * L, SSM_HEADS, SSM_HEAD_DIM)[:, :T]
    return y, h_final.reshape(bsz, SSM_HEADS, SSM_HEAD_DIM, SSM_STATE)


def _ssd_branch(z, xbc_raw, dt_raw, conv_prev, ssm_prev, conv_w, conv_b, dt_bias, a_log, d_skip, g_ssm_norm, w_o_ssm):
    bsz, T = z.shape[:2]
    gn = SSM_GROUPS * SSM_STATE
    u = jnp.concatenate([conv_prev, xbc_raw], axis=1)
    new_conv = u[:, u.shape[1] - (CONV_WIDTH - 1):]
    xbc = jax.nn.silu(_causal_conv(u, conv_w, conv_b))
    xs = xbc[..., :SSM_INNER].reshape(bsz, T, SSM_HEADS, SSM_HEAD_DIM)
    b_in = xbc[..., SSM_INNER:SSM_INNER + gn].reshape(bsz, T, SSM_GROUPS, SSM_STATE)
    c_in = xbc[..., SSM_INNER + gn:].reshape(bsz, T, SSM_GROUPS, SSM_STATE)
    dt = jax.nn.softplus(dt_raw.astype(jnp.float32) + dt_bias.astype(jnp.float32))
    a = -jnp.exp(a_log.astype(jnp.float32))
    y, h_final = _ssd_chunked_scan(xs, dt, a, b_in, c_in, ssm_prev)
    y = (y + d_skip.astype(jnp.float32)[:, None] * xs.astype(jnp.float32)).astype(z.dtype)
    y = rms_norm(y.reshape(bsz, T, SSM_INNER) * jax.nn.silu(z), g_ssm_norm)
    return y @ w_o_ssm, new_conv, h_final.astype(ssm_prev.dtype)


def _layer(x, c, pos, attend, conv_prev, ssm_prev, w_ada, b_ada, g_pre, g_post, w_ffn_gate, w_ffn_up,
           w_ffn_down, w_in, g_q_lat, w_uq, g_kv_lat, w_ukv, w_o_attn, conv_w, conv_b, dt_bias, a_log,
           d_skip, g_ssm_norm, w_o_ssm, w_out):
    bsz = x.shape[0]
    mod = (jax.nn.silu(c) @ w_ada + b_ada).reshape(bsz, N_SUB, 3, D_MODEL)
    shift, scale, gate = mod[:, :, 0], mod[:, :, 1], mod[:, :, 2]
    h = _modulate(x, g_pre[0], shift[:, 0], scale[:, 0])
    f = _swiglu(h, w_ffn_gate[0], w_ffn_up[0], w_ffn_down[0])
    x = x + FFN_RES * gate[:, 0][:, None] * rms_norm(f, g_post[0])
    h = _modulate(x, g_pre[1], shift[:, 1], scale[:, 1])
    q_lat, kv_lat, k_pe_raw, z, xbc_raw, dt_raw, g_attn, g_ssm = jnp.split(h @ w_in, IN_OFFSETS, axis=-1)
    q_nope, q_pe, c_kv, k_pe = _mla_project(q_lat, kv_lat, k_pe_raw, pos, g_q_lat, w_uq, g_kv_lat)
    o_attn = attend(q_nope, q_pe, c_kv, k_pe, w_ukv) @ w_o_attn
    o_ssm, new_conv, new_ssm = _ssd_branch(z, xbc_raw, dt_raw, conv_prev, ssm_prev, conv_w, conv_b,
                                           dt_bias, a_log, d_skip, g_ssm_norm, w_o_ssm)
    mixed = (jax.nn.sigmoid(g_attn) * o_attn + jax.nn.sigmoid(g_ssm) * o_ssm) @ w_out
    x = x + gate[:, 1][:, None] * rms_norm(mixed, g_post[1])
    h = _modulate(x, g_pre[2], shift[:, 2], scale[:, 2])
    f = _swiglu(h, w_ffn_gate[1], w_ffn_up[1], w_ffn_down[1])
    x = x + FFN_RES * gate[:, 2][:, None] * rms_norm(f, g_post[2])
    return x, c_kv, k_pe, new_conv, new_ssm


def _normal(k, shape, scale):
    return jax.random.normal(k, shape, jnp.float32) * scale


def setup_inputs(seed: int = 0) -> dict:
    key = jax.random.key(seed)
    ks = jax.random.split(key, 32)
    n_pages = PAST_LEN // PAGE_SIZE
    n_used = DEC_BATCH * n_pages
    n_phys = (5 * n_used + 3) // 4
    page_table = jax.random.permutation(ks[0], n_phys)[:n_used].reshape(DEC_BATCH, n_pages).astype(jnp.int32)
    dt0 = jnp.exp(jax.random.uniform(ks[25], (DEPTH, SSM_HEADS), jnp.float32, math.log(1e-3), math.log(1e-1)))
    return {
        'x_prompt': _normal(ks[1], (BATCH, SEQ, D_MODEL), 1.0),
        'x_sample': _normal(ks[2], (DEC_BATCH, DEC_SEQ, D_MODEL), 1.0),
        'cache_kv_latent': _normal(ks[3], (DEPTH, n_phys, PAGE_SIZE, KV_LORA), 1.0),
        'cache_k_rope': _normal(ks[4], (DEPTH, n_phys, PAGE_SIZE, QK_ROPE), 1.0),
        'state_conv': _normal(ks[5], (DEPTH, DEC_BATCH, CONV_WIDTH - 1, CONV_DIM), 1.0),
        'state_ssm': _normal(ks[6], (DEPTH, DEC_BATCH, SSM_HEADS, SSM_HEAD_DIM, SSM_STATE), 0.1),
        'page_table': page_table,
        'c_prompt': _normal(ks[7], (BATCH, D_MODEL), 1.0),
        'c_sample': _normal(ks[8], (DEC_BATCH, D_MODEL), 1.0),
        'w_ada': _normal(ks[9], (DEPTH, D_MODEL, N_SUB * 3 * D_MODEL), 0.5 * D_MODEL ** -0.5),
        'b_ada': _normal(ks[10], (DEPTH, N_SUB * 3 * D_MODEL), 0.01),
        'g_pre': 1.0 + _normal(ks[11], (DEPTH, N_SUB, D_MODEL), 0.01),
        'g_post': 1.0 + _normal(ks[12], (DEPTH, N_SUB, D_MODEL), 0.01),
        'w_ffn_gate': _normal(ks[13], (DEPTH, 2, D_MODEL, D_FF), D_MODEL ** -0.5),
        'w_ffn_up': _normal(ks[14], (DEPTH, 2, D_MODEL, D_FF), D_MODEL ** -0.5),
        'w_ffn_down': _normal(ks[15], (DEPTH, 2, D_FF, D_MODEL), D_FF ** -0.5),
        'w_in': _normal(ks[16], (DEPTH, D_MODEL, D_IN), D_MODEL ** -0.5),
        'g_q_lat': 1.0 + _normal(ks[17], (DEPTH, Q_LORA), 0.01),
        'w_uq': _normal(ks[18], (DEPTH, Q_LORA, MLA_HEADS * (QK_NOPE + QK_ROPE)), Q_LORA ** -0.5),
        'g_kv_lat': 1.0 + _normal(ks[19], (DEPTH, KV_LORA), 0.01),
        'w_ukv': _normal(ks[20], (DEPTH, KV_LORA, MLA_HEADS * (QK_NOPE + V_HEAD)), KV_LORA ** -0.5),
        'w_o_attn': _normal(ks[21], (DEPTH, MLA_HEADS * V_HEAD, D_MODEL), (MLA_HEADS * V_HEAD) ** -0.5),
        'conv_w': _normal(ks[22], (DEPTH, CONV_WIDTH, CONV_DIM), CONV_WIDTH ** -0.5),
        'conv_b': _normal(ks[23], (DEPTH, CONV_DIM), 0.01),
        'dt_bias': dt0 + jnp.log(-jnp.expm1(-dt0)),
        'a_log': jnp.log(jax.random.uniform(ks[24], (DEPTH, SSM_HEADS), jnp.float32, 1.0, 16.0)),
        'd_skip': 1.0 + _normal(ks[26], (DEPTH, SSM_HEADS), 0.01),
        'g_ssm_norm': 1.0 + _normal(ks[27], (DEPTH, SSM_INNER), 0.01),
        'w_o_ssm': _normal(ks[28], (DEPTH, SSM_INNER, D_MODEL), SSM_INNER ** -0.5),
        'w_out': _normal(ks[29], (DEPTH, D_MODEL, D_MODEL), D_MODEL ** -0.5),
    }


def reference(x_prompt, x_sample, cache_kv_latent, cache_k_rope, state_conv, state_ssm, page_table,
              c_prompt, c_sample, w_ada, b_ada, g_pre, g_post, w_ffn_gate, w_ffn_up, w_ffn_down, w_in,
              g_q_lat, w_uq, g_kv_lat, w_ukv, w_o_attn, conv_w, conv_b, dt_bias, a_log, d_skip,
              g_ssm_norm, w_o_ssm, w_out):
    bp, t_prompt = x_prompt.shape[:2]
    bs, t_sample = x_sample.shape[:2]
    past_len = page_table.shape[1] * cache_kv_latent.shape[2]
    pos_p = jnp.arange(t_prompt, dtype=jnp.int32)
    pos_s = past_len + jnp.arange(t_sample, dtype=jnp.int32)
    yp, ys = x_prompt, x_sample
    kv_p, pe_p, cv_p, ss_p, kv_s, pe_s, cv_s, ss_s = [], [], [], [], [], [], [], []
    for l in range(DEPTH):
        lw = (w_ada[l], b_ada[l], g_pre[l], g_post[l], w_ffn_gate[l], w_ffn_up[l], w_ffn_down[l], w_in[l],
              g_q_lat[l], w_uq[l], g_kv_lat[l], w_ukv[l], w_o_attn[l], conv_w[l], conv_b[l], dt_bias[l],
              a_log[l], d_skip[l], g_ssm_norm[l], w_o_ssm[l], w_out[l])
        conv0 = jnp.zeros((bp, CONV_WIDTH - 1, CONV_DIM), x_prompt.dtype)
        ssm0 = jnp.zeros((bp, SSM_HEADS, SSM_HEAD_DIM, SSM_STATE), state_ssm.dtype)
        yp, a1, a2, a3, a4 = _layer(yp, c_prompt, pos_p, _mla_prompt_attention, conv0, ssm0, *lw)
        past_c = cache_kv_latent[l][page_table].reshape(bs, past_len, KV_LORA)
        past_pe = cache_k_rope[l][page_table].reshape(bs, past_len, QK_ROPE)
        attend_s = functools.partial(_mla_sample_attention, past_c=past_c, past_pe=past_pe)
        ys, b1, b2, b3, b4 = _layer(ys, c_sample, pos_s, attend_s, state_conv[l], state_ssm[l], *lw)
        kv_p.append(a1); pe_p.append(a2); cv_p.append(a3); ss_p.append(a4)
        kv_s.append(b1); pe_s.append(b2); cv_s.append(b3); ss_s.append(b4)
    return (yp, ys, jnp.stack(kv_p), jnp.stack(pe_p), jnp.stack(cv_p), jnp.stack(ss_p),
            jnp.stack(kv_s), jnp.stack(pe_s), jnp.stack(cv_s), jnp.stack(ss_s))
```

```python
from contextlib import ExitStack
import math
import numpy as np
import concourse.bass as bass
import concourse.mybir as mybir
from concourse.bass_utils import run_bass_kernel_spmd

F32 = mybir.dt.float32
BF16 = mybir.dt.bfloat16
I32 = mybir.dt.int32
ALU = mybir.AluOpType
AF = mybir.ActivationFunctionType

D = 1024
SEQ = 4096
NSS = 16
TS = 8
DFF = 2816
DIN = 8000
QL, KVL, ROPE = 512, 256, 32
NH, NOPE, VH = 16, 64, 64
SI, SHD, SH, SG, SN = 2048, 64, 32, 4, 128
CONV = 3072
EPS = 1e-6
ATT_SCALE = (NOPE + ROPE) ** -0.5
PAST = 16384
O_Q, O_KV, O_PE, O_Z, O_XBC, O_DT, O_GA, O_GS = 0, 512, 768, 800, 2848, 5920, 5952, 6976


class Prog:
    COMPUTE = ("pe", "act", "dve", "pool")
    NDMA = {"sp": 24, "pool": 16, "act": 8}

    def __init__(self, nc):
        self.nc = nc
        self.ops = []
        self.stack = ExitStack()
        self.dry = False

    def sb(self, name, shape, dt):
        return self.stack.enter_context(self.nc.sbuf_tensor(name, list(shape), dt))

    def ps(self, name, shape, dt=F32):
        return self.stack.enter_context(self.nc.psum_tensor(name, list(shape), dt))

    @staticmethod
    def _k(x):
        if isinstance(x, str):
            return x
        if isinstance(x, tuple):
            return tuple(Prog._k(i) if not isinstance(i, int) else i for i in x)
        return "T:" + x.name

    def op(self, eng, fn, r=(), w=()):
        if self.dry:
            return
        r = [self._k(i) for i in r]
        w = [self._k(i) for i in w]
        w += [i for i in r if isinstance(i, str) and i.startswith("T:ps") and i not in w]
        self.ops.append(dict(eng=eng, fn=fn, r=r, w=w, dma=False))

    def dma(self, q, fn, r=(), w=()):
        if self.dry:
            return
        self.ops.append(dict(eng=q, fn=fn, r=[self._k(i) for i in r], w=[self._k(i) for i in w], dma=True))

    def plan(self):
        cnt = {e: 0 for e in self.COMPUTE}
        dcount = {}
        rr = {q: 0 for q in self.NDMA}
        lastw = {}
        readers = {}
        waited = {e: {} for e in ("pe", "act", "dve", "pool", "sp")}
        for op in self.ops:
            E = op["eng"]
            need = {}

            def req(t, kind):
                if t is None:
                    return
                s, v, te = t
                if te is not None and te == E:
                    if E == "pe" or kind == "war":
                        return
                if v > need.get(s, 0):
                    need[s] = v

            for r in op["r"]:
                req(lastw.get(r), "raw")
            for w in op["w"]:
                req(lastw.get(w), "waw")
                for s, (v, te) in readers.get(w, {}).items():
                    req((s, v, te), "war")
            if op["dma"]:
                s = "d_%s_%d" % (E, rr[E])
                rr[E] = (rr[E] + 1) % self.NDMA[E]
                prev = dcount.get(s, 0)
                if prev and 16 * prev > need.get(s, 0):
                    need[s] = 16 * prev
                dcount[s] = prev + 1
                tick = (s, 16 * (prev + 1), None)
                op["inc"] = 16
            else:
                cnt[E] += 1
                tick = (E, cnt[E], E)
                op["inc"] = 1
            wl = []
            for s, v in need.items():
                if v > waited[E].get(s, 0):
                    waited[E][s] = v
                    wl.append((s, v))
            op["waits"] = wl
            op["tick"] = tick
            for r in op["r"]:
                d = readers.setdefault(r, {})
                if tick[1] > d.get(tick[0], (0, None))[0]:
                    d[tick[0]] = (tick[1], tick[2])
            for w in op["w"]:
                lastw[w] = tick
                readers[w] = {}
        self.final = {k: v for k, v in cnt.items() if v}
        self.final.update({s: 16 * c for s, c in dcount.items()})
        self.waited = waited

    def emit(self):
        self.plan()
        nc = self.nc
        sems = {}
        with ExitStack() as st:
            for s in self.final:
                sems[s] = st.enter_context(nc.semaphore("s_" + s))
            block = st.enter_context(nc.Block())

            def replay(name, e):
                for op in self.ops:
                    if op["eng"] != name:
                        continue
                    for s, v in op["waits"]:
                        e.wait_ge(sems[s], v)
                    ins = op["fn"](e)
                    ins.then_inc(sems[op["tick"][0]], op["inc"])
                if name == "sp":
                    for s, v in self.final.items():
                        if v > self.waited["sp"].get(s, 0):
                            e.wait_ge(sems[s], v)

            @block.tensor
            def _(e):
                replay("pe", e)

            @block.scalar
            def _(e):
                replay("act", e)

            @block.vector
            def _(e):
                replay("dve", e)

            @block.gpsimd
            def _(e):
                replay("pool", e)

            @block.sync
            def _(e):
                replay("sp", e)
        self.stack.close()


def build(n_pblk=8, do_sample=True, stage=9, dbg=False, n_phys=20480):
    nc = bass.Bass("TRN2", target_bir_lowering=False)
    P = Prog(nc)

    def din(name, shape, dt=F32):
        return nc.dram_tensor(name, list(shape), dt, kind="ExternalInput").ap()

    def dout(name, shape, dt=F32):
        return nc.dram_tensor(name, list(shape), dt, kind="ExternalOutput").ap()

    xp = din("xp", [SEQ, D])
    xs = din("xs", [NSS * TS, D])
    cc = din("cc", [1 + NSS, D])
    invf = din("invf", [128, 1])
    w_ada = din("w_ada", [D, 9 * D])
    vecA = din("vecA", [120, 128])
    vecB = din("vecB", [126, 128])
    vecC = din("vecC", [32, 128])
    hvec = din("hvec", [128, 96])
    w_gate = din("w_gate", [2 * D, DFF])
    w_up = din("w_up", [2 * D, DFF])
    w_down = din("w_down", [2 * DFF, D])
    w_in = din("w_in", [D, DIN])
    w_uq = din("w_uq", [QL, NH * 96])
    w_ukv = din("w_ukv", [KVL, NH * 128])
    w_oa = din("w_oa", [D, D])
    w_os = din("w_os", [SI, D])
    w_out = din("w_out", [D, D])
    cache_kv = din("cache_kv", [n_phys, 128 * KVL])
    cache_pe = din("cache_pe", [n_phys, 128 * ROPE])
    ptT = din("ptT", [128, NSS], I32)
    st_conv = din("st_conv", [NSS * 3, CONV])
    st_ssm = din("st_ssm", [NSS, SI, SN])
    yp = dout("yp", [SEQ, D])
    ys = dout("ys", [NSS * TS, D])
    kvp = dout("kvp", [SEQ, KVL])
    pep = dout("pep", [SEQ, ROPE])
    convp = dout("convp", [3, CONV])
    ssmp = dout("ssmp", [SI, SN])
    kvs = dout("kvs", [NSS * TS, KVL])
    pes = dout("pes", [NSS * TS, ROPE])
    convs = dout("convs", [NSS * 3, CONV])
    ssms = dout("ssms", [NSS, SI, SN])
    dbg1 = dout("dbg1", [512, D]) if dbg else None
    dbg2 = dout("dbg2", [512, SI]) if dbg else None
    Ksc = nc.dram_tensor("Ksc", [NH, 96, SEQ], BF16).ap()
    Vsc = nc.dram_tensor("Vsc", [NH, 128, SEQ // 128, 66], BF16).ap()

    ident_f = P.sb("ident_f", [128, 128], F32)
    ident_b = P.sb("ident_b", [128, 128], BF16)
    ones_b = P.sb("ones_b", [128, 128], BF16)
    ones_f = P.sb("ones_f", [128, 128], F32)
    triU = P.sb("triU", [128, 128], F32)
    triS = P.sb("triS", [128, 128], F32)
    blkS = P.sb("blkS", [128, 128], F32)
    triU_b = P.sb("triU_b", [128, 128], BF16)
    triS_b = P.sb("triS_b", [128, 128], BF16)
    Esel = P.sb("Esel", [16, 128], F32)
    eps_t = P.sb("eps_t", [128, 1], F32)
    one_t = P.sb("one_t", [128, 1], F32)
    npi_t = P.sb("npi_t", [128, 1], F32)
    invf_t = P.sb("invf_t", [128, 1], F32)
    Rt = P.sb("Rt", [128, 96], BF16)
    Rtmp = P.sb("Rtmp", [32, 64], F32)
    vA = P.sb("vA", [128, 120], F32)
    vB = P.sb("vB", [128, 126], F32)
    vC = P.sb("vC", [128, 32], F32)
    hv = P.sb("hv", [128, 96], F32)
    a_bc = P.sb("a_bc", [128, 32], F32)
    stg = P.sb("stg", [128, 1024], F32)
    cT = P.sb("cT", [128, 8, 17], F32)
    modT = P.sb("modT", [128, 72, 17], F32)
    At = P.sb("At", [128, 24, 17], F32)
    Gt = P.sb("Gt", [128, 24, 17], F32)
    NWB = 4
    WBE = 4096
    WB = [P.sb("wb%d" % i, [128, WBE], BF16) for i in range(NWB)]
    xT = P.sb("xT", [128, 8, 512], F32)
    fT = P.sb("fT", [128, 8, 512], F32)
    wada = [xT, fT]
    hT = P.sb("hT", [128, 8, 512], BF16)
    big = P.sb("big", [128, 24, 512], BF16)
    hid = big
    R1 = P.sb("R1", [128, 16, 512], BF16)
    rstd = P.sb("rstd", [128, 512], F32)
    tA = [P.sb("tA%d" % i, [128, 512], F32) for i in range(2)]
    sq = [P.sb("sq%d" % i, [128, 520], BF16) for i in range(2)]
    ckv = P.sb("ckv", [128, 2, 512], F32)
    ckv_b = P.sb("ckv_b", [128, 2, 512], BF16)
    kpe = P.sb("kpe", [32, 512], F32)
    kpe_b = P.sb("kpe_b", [32, 512], BF16)
    cos_t = P.sb("cos_t", [128, 512], F32)
    sin_t = P.sb("sin_t", [128, 512], F32)
    pos_i = P.sb("pos_i", [128, 512], I32)
    qn = big[:, 20:24, :]
    otm = [P.sb("otm%d" % i, [128, 288], F32) for i in range(2)]
    hstate = P.sb("hstate", [128, SI], F32)
    hstate_b = P.sb("hstate_b", [128, SI], BF16)
    histb = P.sb("histb", [128, 24, 3], BF16)
    histf = P.sb("histf", [128, 24, 48], F32)
    dtt = P.sb("dtt", [128, 4, 32], F32)
    lat = P.sb("lat", [128, 32], F32)
    cumt = P.sb("cumt", [128, 32], F32)
    ncum = P.sb("ncum", [128, 32], F32)
    wdec = P.sb("wdec", [128, 32], F32)
    edch = P.sb("edch", [128, 32, 16], F32)
    Gm = P.sb("Gm", [128, 4, 128], F32)
    dgh = P.sb("dgh", [128, 128], F32)
    segt = P.sb("segt", [128, 128], F32)
    ecr = P.sb("ecr", [128, 128], F32)
    rec = P.sb("rec", [128, 4], F32)
    pti = P.sb("pti", [128, NSS], I32)
    pti2 = P.sb("pti2", [128, NSS * 16], I32)
    arena = P.sb("arena", [128, 13312], BF16)
    PS = [P.ps("ps%d" % i, [128, 512], F32) for i in range(7)]
    PSB = P.ps("psb", [128, 1024], BF16)
    PSB2 = PS[5][:, 0:512].bitcast(BF16)

    Kbuf = arena[:, 0:4096]
    Vbuf = arena[:, 4096:4096 + 32 * 66].rearrange("p (t x) -> p t x", x=66)[:, :, 0:65]
    o_tm = arena[:, 6400:6400 + 4096].rearrange("p (t x) -> p t x", x=1024)
    PTs = [arena[:, 10496 + i * 512:10496 + (i + 1) * 512] for i in range(2)]
    vst = arena[:, 0:4 * 16 * 66].rearrange("p (t h x) -> p t h x", t=4, h=16)[:, :, :, 0:65]
    kst = arena[:, 4224:4224 + 2048].rearrange("p (h x) -> p h x", x=512)
    MTs = arena[:, 0:4096].rearrange("p (h l) -> p h l", l=128)
    Css = arena[:, 4096:8192].rearrange("p (h l) -> p h l", l=128)
    xdt = arena[:, 8192:10240]
    xdtw = arena[:, 10240:12288]
    Btm = arena[:, 12288:12800].rearrange("p (g n) -> p g n", n=128)
    Bms = arena[:, 12800:13312].rearrange("p (g n) -> p g n", n=128)
    Kbs = [arena[:, i * 2432:(i + 1) * 2432].rearrange("p (r x) -> p r x", x=304) for i in range(2)]
    KTs = [arena[:, 4864 + i * 384:4864 + (i + 1) * 384].rearrange("p (c k) -> p c k", k=128) for i in range(2)]
    Knew = arena[:, 5632:5632 + 304]
    qabs = arena[:, 5936:5936 + 3 * 2048].rearrange("p (c q) -> p c q", q=2048)
    wukT = arena[:, 12080:12080 + 256]
    olat = arena[:, 12336:12336 + 256]
    olatT = arena[:, 12592:12592 + 256].rearrange("p (c q) -> p c q", q=128)
    PTq = [arena[:, 12848 + i * 128:12848 + (i + 1) * 128] for i in range(2)]
    Gk = [fT[:, 4 * i:4 * i + 4, :].rearrange("p a b -> p (a b)") for i in range(2)]
    GkK = [("fTg", 0), ("fTg", 1)]
    Gp = [tA[i][:, 0:256] for i in range(2)]

    cnt = {}

    def rot(name, lst):
        cnt[name] = cnt.get(name, -1) + 1
        return lst[cnt[name] % len(lst)]

    AR = "arena"

    def barrier(old, new):
        P.op("dve", lambda e: e.memset(rec[:, 0:1], 0.0), r=list(old), w=list(new) + [rec])

    wsched = []
    wstate = {"i": 0, "issued": 0}
    wuniq = {}
    wbf_holder = {}

    def pview(buf, KC, w):
        return buf[:, 0:KC * w].rearrange("p (kc n) -> p kc n", n=w)

    def pkey(Wd, K, c0, w):
        return (repr(Wd), K, c0, w)

    def convert_weights():
        off = 0
        for ent in wsched:
            k = pkey(*ent)
            if k not in wuniq:
                Wd, K, c0, w = ent
                wuniq[k] = (len(wuniq), off, ent)
                off += (K // 128) * w
        Wbf = nc.dram_tensor("Wbf", [128, off], BF16).ap()
        wbf_holder["ap"] = Wbf
        for k, (idx, o, ent) in wuniq.items():
            Wd, K, c0, w = ent
            KC = K // 128
            src = Wd[0:K, c0:c0 + w].rearrange("(kc p) n -> p kc n", p=128)
            dst = Wbf[:, o:o + KC * w].rearrange("p (kc n) -> p kc n", n=w)
            P.dma("pool", lambda e, src=src, dst=dst: e.dma_start(out=dst, in_=src), w=[("wbf", idx)])

    def issue_panel(i):
        ent = wsched[i]
        Wd, K, c0, w = ent
        idx, o, _ = wuniq[pkey(*ent)]
        buf = WB[i % NWB]
        n = (K // 128) * w
        Wbf = wbf_holder["ap"]
        P.dma("sp", lambda e: e.dma_start(out=buf[:, 0:n], in_=Wbf[:, o:o + n]), r=[("wbf", idx)], w=[buf])

    def panel(Wd, K, c0, w):
        i = wstate["i"]
        wstate["i"] += 1
        assert (K // 128) * w <= WBE
        if P.dry:
            wsched.append((Wd, K, c0, w))
            return WB[i % NWB], pview(WB[i % NWB], K // 128, w)
        while wstate["issued"] <= min(i + NWB - 1, len(wsched) - 1):
            issue_panel(wstate["issued"])
            wstate["issued"] += 1
        return WB[i % NWB], pview(WB[i % NWB], K // 128, w)

    def constants():
        P.op("pool", lambda e: e.memset(ident_f[:], 1.0), w=[ident_f])
        P.op("pool", lambda e: e.affine_select(out=ident_f[:], in_=ident_f[:], pattern=[[-1, 128]],
                                               compare_op=ALU.is_equal, fill=0.0, base=0, channel_multiplier=1),
             r=[ident_f], w=[ident_f])
        P.op("dve", lambda e: e.tensor_copy(out=ident_b[:], in_=ident_f[:]), r=[ident_f], w=[ident_b])
        P.op("dve", lambda e: e.memset(ones_b[:], 1.0), w=[ones_b])
        P.op("dve", lambda e: e.memset(ones_f[:], 1.0), w=[ones_f])
        P.op("dve", lambda e: e.memset(eps_t[:], EPS), w=[eps_t])
        P.op("dve", lambda e: e.memset(one_t[:], 1.0), w=[one_t])
        P.op("dve", lambda e: e.memset(npi_t[:], -math.pi), w=[npi_t])
        P.dma("sp", lambda e: e.dma_start(out=invf_t[:], in_=invf), w=[invf_t])
        P.dma("sp", lambda e: e.dma_start(out=hv[:], in_=hvec), w=[hv])
        P.dma("sp", lambda e: e.dma_start(out=pti[:], in_=ptT), w=[pti])
        P.op("act", lambda e: e.activation(out=a_bc[:], in_=hv[:, 32:64], func=AF.Exp), r=[hv], w=[a_bc])
        P.op("dve", lambda e: e.tensor_scalar(out=a_bc[:], in0=a_bc[:], scalar1=-1.0, scalar2=None, op0=ALU.mult),
             r=[a_bc], w=[a_bc])
        P.op("pool", lambda e: e.memset(triU[:], 1.0), w=[triU])
        P.op("pool", lambda e: e.affine_select(out=triU[:], in_=triU[:], pattern=[[1, 128]],
                                               compare_op=ALU.is_ge, fill=0.0, base=0, channel_multiplier=-1),
             r=[triU], w=[triU])
        P.op("pool", lambda e: e.memset(Esel[:], 1.0), w=[Esel])
        P.op("pool", lambda e: e.affine_select(out=Esel[:], in_=Esel[:], pattern=[[1, 128]],
                                               compare_op=ALU.is_ge, fill=0.0, base=0, channel_multiplier=-8),
             r=[Esel], w=[Esel])
        P.op("pool", lambda e: e.affine_select(out=Esel[:], in_=Esel[:], pattern=[[-1, 128]],
                                               compare_op=ALU.is_ge, fill=0.0, base=7, channel_multiplier=8),
             r=[Esel], w=[Esel])
        P.op("pe", lambda e: e.matmul(PS[5][:, 0:128], lhsT=Esel[:], rhs=Esel[:], start=True, stop=True), r=[Esel], w=[PS[5]])
        P.op("dve", lambda e: e.tensor_copy(out=blkS[:], in_=PS[5][:, 0:128]), r=[PS[5]], w=[blkS])
        P.op("dve", lambda e: e.tensor_tensor(out=triS[:], in0=triU[:], in1=blkS[:], op=ALU.mult), r=[triU, blkS], w=[triS])
        P.op("dve", lambda e: e.tensor_copy(out=triU_b[:], in_=triU[:]), r=[triU], w=[triU_b])
        P.op("dve", lambda e: e.tensor_copy(out=triS_b[:], in_=triS[:]), r=[triS], w=[triS_b])
        P.op("pool", lambda e: e.memset(Rtmp[:, 0:32], 1.0), w=[Rtmp])
        P.op("pool", lambda e: e.affine_select(out=Rtmp[:, 0:32], in_=Rtmp[:, 0:32], pattern=[[-1, 32]],
                                               compare_op=ALU.is_equal, fill=0.0, base=16, channel_multiplier=1),
             r=[Rtmp], w=[Rtmp])
        P.op("pool", lambda e: e.memset(Rtmp[:, 32:64], -1.0), r=[Rtmp], w=[Rtmp])
        P.op("pool", lambda e: e.affine_select(out=Rtmp[:, 32:64], in_=Rtmp[:, 32:64], pattern=[[-1, 32]],
                                               compare_op=ALU.is_equal, fill=0.0, base=-16, channel_multiplier=1),
             r=[Rtmp], w=[Rtmp])
        P.op("dve", lambda e: e.memset(Rt[:], 0.0), w=[Rt])
        P.op("dve", lambda e: e.tensor_tensor(out=Rt[0:32, 0:32], in0=Rtmp[:, 0:32], in1=Rtmp[:, 32:64], op=ALU.add),
             r=[Rtmp, Rt], w=[Rt])
        P.dma("sp", lambda e: e.dma_start(out=Rt[64:96, 64:96], in_=Rt[0:32, 0:32]), r=[Rt], w=[Rt])
        for (src, n, dst) in ((vecA, 120, vA), (vecB, 126, vB), (vecC, 32, vC)):
            P.dma("sp", lambda e, src=src, n=n: e.dma_start(out=stg[0:n, 0:128], in_=src), w=[stg])
            P.op("pe", lambda e, n=n: e.transpose(out=PS[5][:, 0:n], in_=stg[0:n, 0:128], identity=ident_f[0:n, 0:n]),
                 r=[stg, ident_f], w=[PS[5]])
            P.op("dve", lambda e, n=n, dst=dst: e.tensor_copy(out=dst[:, 0:n], in_=PS[5][:, 0:n]), r=[PS[5]], w=[dst])
        P.op("dve", lambda e: e.memset(hstate[:], 0.0), w=[hstate])
        P.op("dve", lambda e: e.memset(hstate_b[:], 0.0), w=[hstate_b])
        P.op("dve", lambda e: e.memset(histb[:], 0.0), w=[histb])

    def adaln():
        P.dma("sp", lambda e: e.dma_start(out=stg[0:17, :], in_=cc), w=[stg])
        P.op("act", lambda e: e.activation(out=stg[0:17, :], in_=stg[0:17, :], func=AF.Silu), r=[stg], w=[stg])
        for kc in range(8):
            P.op("pe", lambda e, kc=kc: e.transpose(out=PS[5][:, kc * 17:(kc + 1) * 17], in_=stg[0:17, kc * 128:(kc + 1) * 128],
                                                    identity=ident_f[0:17, 0:17]), r=[stg, ident_f], w=[PS[5]])
        P.op("dve", lambda e: e.tensor_copy(out=cT[:].rearrange("p a b -> p (a b)"), in_=PS[5][:, 0:136]), r=[PS[5]], w=[cT])
        for pn in range(18):
            wb = wada[pn % 2]
            P.dma("sp", lambda e, pn=pn, wb=wb: e.dma_start(
                out=wb[:], in_=w_ada[:, pn * 512:(pn + 1) * 512].rearrange("(kc p) n -> p kc n", p=128)), w=[wb])
            pt = PS[pn % 2]
            for o in range(4):
                for kc in range(8):
                    P.op("pe", lambda e, o=o, kc=kc, wb=wb, pt=pt: e.matmul(
                        pt[:, o * 17:(o + 1) * 17], lhsT=wb[:, kc, o * 128:(o + 1) * 128], rhs=cT[:, kc, :],
                        start=(kc == 0), stop=(kc == 7)), r=[wb, cT], w=[pt])
            for o in range(4):
                oc = pn * 4 + o
                P.op("act", lambda e, o=o, oc=oc, pt=pt: e.activation(
                    out=modT[:, oc, :], in_=pt[:, o * 17:(o + 1) * 17], func=AF.Identity, bias=vA[:, oc:oc + 1], scale=1.0),
                    r=[pt, vA], w=[modT])
        for j in range(3):
            for c in range(8):
                i = j * 8 + c
                P.op("dve", lambda e, j=j, c=c, i=i: e.tensor_scalar(
                    out=At[:, i, :], in0=modT[:, (3 * j + 1) * 8 + c, :], scalar1=1.0, scalar2=vA[:, 72 + i:73 + i],
                    op0=ALU.add, op1=ALU.mult), r=[modT, vA], w=[At])
                P.op("dve", lambda e, j=j, c=c, i=i: e.tensor_scalar(
                    out=Gt[:, i, :], in0=modT[:, (3 * j + 2) * 8 + c, :], scalar1=vA[:, 96 + i:97 + i],
                    scalar2=(1.0 if j == 1 else 0.5), op0=ALU.mult, op1=ALU.mult), r=[modT, vA], w=[Gt])

    def load_xT(src, T):
        for t in range(T // 128):
            P.dma("sp", lambda e, t=t: e.dma_start(out=stg[:], in_=src[t * 128:(t + 1) * 128, :]), w=[stg])
            for half in range(2):
                pt = PS[5]
                for q in range(4):
                    c = half * 4 + q
                    P.op("pe", lambda e, c=c, q=q, pt=pt: e.transpose(
                        out=pt[:, q * 128:(q + 1) * 128], in_=stg[:, c * 128:(c + 1) * 128], identity=ident_f[:]),
                        r=[stg, ident_f], w=[pt])
                P.op("act", lambda e, half=half, t=t, pt=pt: e.activation(
                    out=xT[:, half * 4:half * 4 + 4, t * 128:(t + 1) * 128],
                    in_=pt[:].rearrange("p (q n) -> p q n", q=4), func=AF.Copy), r=[pt], w=[xT])

    def store_xT(dst, T, dname):
        for t in range(T // 128):
            for half in range(2):
                pt = PS[5]
                for q in range(4):
                    c = half * 4 + q
                    P.op("pe", lambda e, c=c, q=q, pt=pt, t=t: e.transpose(
                        out=pt[:, q * 128:(q + 1) * 128], in_=xT[:, c, t * 128:(t + 1) * 128], identity=ident_f[:]),
                        r=[xT, ident_f], w=[pt])
                P.op("act", lambda e, half=half, pt=pt: e.activation(
                    out=stg[:, half * 512:(half + 1) * 512], in_=pt[:], func=AF.Copy), r=[pt], w=[stg])
            P.dma("sp", lambda e, t=t: e.dma_start(out=dst[t * 128:(t + 1) * 128, :], in_=stg[:]), r=[stg], w=[dname])

    def rms_rstd(src, nchunks, T, n_feat, dst, src_res=None):
        for c in range(nchunks):
            s_ = rot("sq", sq)
            P.op("act", lambda e, c=c, s_=s_: e.activation(out=s_[:, 0:T], in_=src[:, c, 0:T], func=AF.Square),
                 r=[src_res or src], w=[s_])
            P.op("pe", lambda e, c=c, s_=s_: e.matmul(PS[4][:, 0:T], lhsT=ones_b[:], rhs=s_[:, 0:T],
                                                     start=(c == 0), stop=(c == nchunks - 1)), r=[s_, ones_b], w=[PS[4]])
        P.op("act", lambda e: e.activation(out=dst[:, 0:T], in_=PS[4][:, 0:T], func=AF.Sqrt, bias=eps_t[:, 0:1],
                                           scale=1.0 / n_feat), r=[PS[4], eps_t], w=[dst])
        P.op("dve", lambda e: e.reciprocal(out=dst[:, 0:T], in_=dst[:, 0:T]), r=[dst], w=[dst])

    def modulate(j, T, segs):
        rms_rstd(xT, 8, T, D, rstd)
        for c in range(8):
            t_ = rot("tA", tA)
            P.op("dve", lambda e, c=c, t_=t_: e.tensor_tensor(out=t_[:, 0:T], in0=xT[:, c, 0:T], in1=rstd[:, 0:T], op=ALU.mult),
                 r=[xT, rstd], w=[t_])
            for (s, c0, n) in segs:
                P.op("act", lambda e, c=c, t_=t_, s=s, c0=c0, n=n: e.activation(
                    out=hT[:, c, c0:c0 + n], in_=t_[:, c0:c0 + n], func=AF.Identity,
                    bias=modT[:, (3 * j) * 8 + c, s:s + 1], scale=At[:, j * 8 + c, s:s + 1]),
                    r=[t_, modT, At], w=[hT])

    def linear(Wd, K, col0, ncols, rhs, rhs_res, T, evac, chunk=128, pw=None):
        KC = K // 128
        if pw is None:
            pw = 512 if KC * 512 <= WBE else (256 if KC * 256 <= WBE else 128)
        done = 0
        idx = 0
        while done < ncols:
            w = min(pw, ncols - done)
            bres, buf = panel(Wd, K, col0 + done, w)
            o = 0
            while o < w:
                m = min(chunk, w - o)
                pt = rot("pa", [PS[0], PS[1]])
                for kc in range(KC):
                    P.op("pe", lambda e, kc=kc, o=o, m=m, pt=pt, buf=buf: e.matmul(
                        pt[0:m, 0:T], lhsT=buf[:, kc, o:o + m], rhs=rhs(kc), start=(kc == 0), stop=(kc == KC - 1)),
                        r=[bres, rhs_res], w=[pt])
                evac(idx, m, pt)
                idx += 1
                o += m
            done += w

    def residual(j, T, segs):
        rms_rstd(fT, 8, T, D, rstd)
        for c in range(8):
            t_ = rot("tA", tA)
            P.op("dve", lambda e, c=c, t_=t_: e.tensor_tensor(out=t_[:, 0:T], in0=fT[:, c, 0:T], in1=rstd[:, 0:T], op=ALU.mult),
                 r=[fT, rstd], w=[t_])
            for (s, c0, n) in segs:
                P.op("dve", lambda e, c=c, t_=t_, s=s, c0=c0, n=n: e.scalar_tensor_tensor(
                    out=xT[:, c, c0:c0 + n], in0=t_[:, c0:c0 + n], scalar=Gt[:, j * 8 + c, s:s + 1],
                    in1=xT[:, c, c0:c0 + n], op0=ALU.mult, op1=ALU.add), r=[t_, Gt, xT], w=[xT])

    def ffn(j, T, segs):
        fi = 0 if j == 0 else 1
        modulate(j, T, segs)
        for pn in range(6):
            c0 = pn * 512
            w = min(512, DFF - c0)
            nch = w // 128
            gres, gbuf = panel(w_gate[fi * D:(fi + 1) * D, :], D, c0, w)
            for o in range(nch):
                pt = rot("pa", [PS[0], PS[1]])
                for kc in range(8):
                    P.op("pe", lambda e, kc=kc, o=o, pt=pt, gbuf=gbuf: e.matmul(
                        pt[:, 0:T], lhsT=gbuf[:, kc, o * 128:(o + 1) * 128], rhs=hT[:, kc, 0:T],
                        start=(kc == 0), stop=(kc == 7)), r=[gres, hT], w=[pt])
                P.op("act", lambda e, o=o, pt=pt, pn=pn: e.activation(
                    out=hid[:, pn * 4 + o, 0:T], in_=pt[:, 0:T], func=AF.Silu), r=[pt], w=[hid])
            ures, ubuf = panel(w_up[fi * D:(fi + 1) * D, :], D, c0, w)
            for o in range(nch):
                pt = rot("pb", [PS[2], PS[3]])
                for kc in range(8):
                    P.op("pe", lambda e, kc=kc, o=o, pt=pt, ubuf=ubuf: e.matmul(
                        pt[:, 0:T], lhsT=ubuf[:, kc, o * 128:(o + 1) * 128], rhs=hT[:, kc, 0:T],
                        start=(kc == 0), stop=(kc == 7)), r=[ures, hT], w=[pt])
                P.op("dve", lambda e, o=o, pt=pt, pn=pn: e.tensor_tensor(
                    out=hid[:, pn * 4 + o, 0:T], in0=hid[:, pn * 4 + o, 0:T], in1=pt[:, 0:T], op=ALU.mult),
                    r=[pt, hid], w=[hid])

        def ev(i, m, pt):
            P.op("act", lambda e: e.activation(out=fT[:, i, 0:T], in_=pt[:, 0:T], func=AF.Copy), r=[pt], w=[fT])
        linear(w_down[fi * DFF:(fi + 1) * DFF, :], DFF, 0, D, lambda kc: hid[:, kc, 0:T], hid, T, ev)
        residual(j, T, segs)

    def rope_tables(pos0, T):
        ang, frac = tA[0], tA[1]
        if pos0 is None:
            P.op("pool", lambda e: e.iota(pos_i[:, 0:T].rearrange("p (a b) -> p a b", b=TS), pattern=[[0, NSS], [1, TS]],
                                          base=PAST, channel_multiplier=0), w=[pos_i])
        else:
            P.op("pool", lambda e: e.iota(pos_i[:, 0:T], pattern=[[1, T]], base=pos0, channel_multiplier=0), w=[pos_i])
        P.op("dve", lambda e: e.tensor_copy(out=ang[:, 0:T], in_=pos_i[:, 0:T]), r=[pos_i], w=[ang])
        P.op("dve", lambda e: e.tensor_scalar(out=ang[:, 0:T], in0=ang[:, 0:T], scalar1=invf_t[:, 0:1], scalar2=None,
                                              op0=ALU.mult), r=[ang, invf_t], w=[ang])
        for (dst, off) in ((sin_t, 0.5), (cos_t, 0.75)):
            P.op("dve", lambda e, dst=dst, off=off: e.tensor_scalar(
                out=dst[:, 0:T], in0=ang[:, 0:T], scalar1=1.0 / (2 * math.pi), scalar2=off, op0=ALU.mult, op1=ALU.add),
                r=[ang], w=[dst])
            P.op("dve", lambda e, dst=dst: e.tensor_copy(out=pos_i[:, 0:T], in_=dst[:, 0:T]), r=[dst], w=[pos_i])
            P.op("dve", lambda e, dst=dst: e.tensor_copy(out=frac[:, 0:T], in_=pos_i[:, 0:T]), r=[pos_i], w=[frac])
            P.op("dve", lambda e, dst=dst: e.tensor_tensor(out=dst[:, 0:T], in0=dst[:, 0:T], in1=frac[:, 0:T], op=ALU.subtract),
                 r=[dst, frac], w=[dst])
            P.op("dve", lambda e, dst=dst: e.scalar_tensor_tensor(out=dst[:, 0:T], in0=dst[:, 0:T], scalar=0.0, in1=dst[:, 0:T],
                                                                  op0=ALU.is_lt, op1=ALU.add), r=[dst], w=[dst])
            P.op("act", lambda e, dst=dst: e.activation(out=dst[:, 0:T], in_=dst[:, 0:T], func=AF.Sin,
                                                        bias=npi_t[:, 0:1], scale=2 * math.pi), r=[dst, npi_t], w=[dst])

    def tm_out(srcs, T, dst, row0, dname, keep=None):
        ncol = sum(m for _, m in srcs)
        for t in range(T // 128):
            o_ = rot("otm", otm)
            c0 = 0
            for (fn, m) in srcs:
                P.op("pe", lambda e, fn=fn, m=m, c0=c0, t=t: e.transpose(
                    out=PS[5][:, c0:c0 + m], in_=fn(t), identity=ident_f[0:m, 0:m]), r=[fn.res, ident_f], w=[PS[5]])
                c0 += m
            P.op("dve", lambda e, o_=o_: e.tensor_copy(out=o_[:, 0:ncol], in_=PS[5][:, 0:ncol]), r=[PS[5]], w=[o_])
            if keep is not None:
                keep(o_)
            P.dma("sp", lambda e, o_=o_, t=t: e.dma_start(out=dst[row0 + t * 128:row0 + (t + 1) * 128, :], in_=o_[:, 0:ncol]),
                  r=[o_], w=[dname])

    def mixer_kvq(T, segs, pos0, grp, row0):
        modulate(1, T, segs)

        def ev_kv(i, m, pt):
            P.op("act", lambda e: e.activation(out=ckv[:, i, 0:T], in_=pt[:, 0:T], func=AF.Copy), r=[pt], w=[ckv])

        def ev_pe(i, m, pt):
            P.op("act", lambda e: e.activation(out=kpe[:, 0:T], in_=pt[0:32, 0:T], func=AF.Copy), r=[pt], w=[kpe])
        linear(w_in, D, O_KV, KVL, lambda kc: hT[:, kc, 0:T], hT, T, ev_kv)
        linear(w_in, D, O_PE, ROPE, lambda kc: hT[:, kc, 0:T], hT, T, ev_pe)
        rms_rstd(ckv, 2, T, KVL, rstd)
        for c in range(2):
            P.op("dve", lambda e, c=c: e.scalar_tensor_tensor(
                out=ckv[:, c, 0:T], in0=ckv[:, c, 0:T], scalar=vB[:, 4 + c:5 + c], in1=rstd[:, 0:T],
                op0=ALU.mult, op1=ALU.mult), r=[ckv, vB, rstd], w=[ckv])
        P.op("act", lambda e: e.activation(out=ckv_b[:, :, 0:T], in_=ckv[:, :, 0:T], func=AF.Copy), r=[ckv], w=[ckv_b])
        rope_tables(pos0, T)
        P.op("dve", lambda e: e.tensor_copy(out=kpe_b[:, 0:T], in_=kpe[:, 0:T]), r=[kpe], w=[kpe_b])
        P.op("pe", lambda e: e.matmul(PS[6][0:32, 0:T], lhsT=Rt[0:32, 0:32], rhs=kpe_b[:, 0:T], start=True, stop=True),
             r=[Rt, kpe_b], w=[PS[6]])
        kpe_r = tA[0][0:32, :]
        P.op("dve", lambda e: e.tensor_tensor(out=kpe_r[:, 0:T], in0=PS[6][0:32, 0:T], in1=sin_t[0:32, 0:T], op=ALU.mult),
             r=[PS[6], sin_t], w=[tA[0]])
        P.op("dve", lambda e: e.tensor_tensor(out=kpe[:, 0:T], in0=kpe[:, 0:T], in1=cos_t[0:32, 0:T], op=ALU.mult),
             r=[kpe, cos_t], w=[kpe])
        P.op("dve", lambda e: e.tensor_tensor(out=kpe[:, 0:T], in0=kpe[:, 0:T], in1=kpe_r[:, 0:T], op=ALU.add),
             r=[kpe, tA[0]], w=[kpe])
        P.op("dve", lambda e: e.tensor_copy(out=kpe_b[:, 0:T], in_=kpe[:, 0:T]), r=[kpe], w=[kpe_b])
        f0 = lambda t: ckv[:, 0, t * 128:(t + 1) * 128]
        f0.res = ckv
        f1 = lambda t: ckv[:, 1, t * 128:(t + 1) * 128]
        f1.res = ckv
        f2 = lambda t: kpe[:, t * 128:(t + 1) * 128]
        f2.res = kpe
        if grp == "s":
            barrier([(AR, "K"), (AR, "V"), (AR, "otm"), (AR, "PT0"), (AR, "PT1"), (AR, "ssd"), (AR, "vst"), (AR, "kst")],
                    [(AR, "Knew")])
        tm_out([(f0, 128), (f1, 128)], T, kvp if grp == "p" else kvs, row0, "kv" + grp,
               keep=(lambda o_: (P.op("act", lambda e: e.activation(out=Knew[:, 0:256], in_=o_[:, 0:256], func=AF.Copy),
                                      r=[o_], w=[(AR, "Knew")]),
                                 P.op("dve", lambda e: e.memset(Knew[:, 256:257], 1.0), w=[(AR, "Knew")]))) if grp == "s" else None)
        tm_out([(f2, 32)], T, pep if grp == "p" else pes, row0, "pe" + grp,
               keep=(lambda o_: P.op("act", lambda e: e.activation(out=Knew[:, 257:289], in_=o_[:, 0:32], func=AF.Copy),
                                     r=[o_], w=[(AR, "Knew")])) if grp == "s" else None)
        def ev_q(i, m, pt):
            P.op("act", lambda e: e.activation(out=fT[:, i, 0:T], in_=pt[:, 0:T], func=AF.Copy), r=[pt], w=[fT])
        linear(w_in, D, O_Q, QL, lambda kc: hT[:, kc, 0:T], hT, T, ev_q)
        rms_rstd(fT, 4, T, QL, rstd)
        for c in range(4):
            P.op("dve", lambda e, c=c: e.scalar_tensor_tensor(
                out=qn[:, c, 0:T], in0=fT[:, c, 0:T], scalar=vB[:, c:c + 1], in1=rstd[:, 0:T],
                op0=ALU.mult, op1=ALU.mult), r=[fT, vB, rstd], w=[big])

        def ev_qh(h, m, pt):
            P.op("act", lambda e: e.activation(out=R1[0:96, h, 0:T], in_=pt[0:96, 0:T], func=AF.Copy, scale=ATT_SCALE),
                 r=[pt], w=[R1])
            P.op("pe", lambda e: e.matmul(PS[6][0:96, 0:T], lhsT=Rt[64:96, 0:96], rhs=R1[64:96, h, 0:T], start=True, stop=True),
                 r=[Rt, R1], w=[PS[6]])
            t_ = rot("tA", tA)
            P.op("dve", lambda e: e.tensor_tensor(out=t_[64:96, 0:T], in0=PS[6][64:96, 0:T], in1=sin_t[64:96, 0:T], op=ALU.mult),
                 r=[PS[6], sin_t], w=[t_])
            P.op("dve", lambda e: e.tensor_tensor(out=R1[64:96, h, 0:T], in0=R1[64:96, h, 0:T], in1=cos_t[64:96, 0:T], op=ALU.mult),
                 r=[R1, cos_t], w=[R1])
            P.op("dve", lambda e: e.tensor_tensor(out=R1[64:96, h, 0:T], in0=R1[64:96, h, 0:T], in1=t_[64:96, 0:T], op=ALU.add),
                 r=[R1, t_], w=[R1])
        linear(w_uq, QL, 0, NH * 96, lambda kc: qn[:, kc, 0:T], big, T, ev_qh, chunk=96, pw=384)

    def kv_gen(b):
        T = 512
        barrier([(AR, "K"), (AR, "V"), (AR, "otm"), (AR, "PT0"), (AR, "PT1"), (AR, "ssd"), (AR, "sa"), (AR, "Knew")],
                [(AR, "vst"), (AR, "kst")])
        P.op("dve", lambda e: e.memset(vst[:, :, :, 64:65], 1.0), w=[(AR, "vst")])
        for pn in range(4):
            bres, buf = panel(w_ukv, KVL, pn * 512, 512)
            for hh in range(4):
                h = pn * 4 + hh
                pt = rot("pa", [PS[0], PS[1]])
                for kc in range(2):
                    P.op("pe", lambda e, kc=kc, hh=hh, pt=pt, buf=buf: e.matmul(
                        pt[0:64, 0:T], lhsT=buf[:, kc, hh * 128:hh * 128 + 64], rhs=ckv_b[:, kc, 0:T],
                        start=(kc == 0), stop=(kc == 1)), r=[bres, ckv_b], w=[pt])
                P.op("act", lambda e, hh=hh, pt=pt: e.activation(out=kst[0:64, hh, :], in_=pt[0:64, 0:T], func=AF.Copy),
                     r=[pt], w=[(AR, "kst")])
                P.dma("sp", lambda e, h=h, hh=hh: e.dma_start(out=Ksc[h, 0:64, b * 512:(b + 1) * 512], in_=kst[0:64, hh, :]),
                      r=[(AR, "kst")], w=["Ksc"])
                P.dma("sp", lambda e, h=h: e.dma_start(out=Ksc[h, 64:96, b * 512:(b + 1) * 512], in_=kpe_b[:, 0:T]),
                      r=[kpe_b], w=["Ksc"])
            for t in range(4):
                pt = rot("pb", [PS[2], PS[3]])
                for kc in range(2):
                    P.op("pe", lambda e, kc=kc, t=t, pt=pt, buf=buf: e.matmul(
                        pt[:, 0:256].rearrange("p (h x) -> p h x", x=64), lhsT=ckv_b[:, kc, t * 128:(t + 1) * 128],
                        rhs=buf[:, kc, :].rearrange("p (h x) -> p h x", x=128)[:, :, 64:128],
                        start=(kc == 0), stop=(kc == 1)), r=[bres, ckv_b], w=[pt])
                P.op("act", lambda e, t=t, pn=pn, pt=pt: e.activation(
                    out=vst[:, t, pn * 4:(pn + 1) * 4, 0:64], in_=pt[:, 0:256].rearrange("p (h x) -> p h x", x=64),
                    func=AF.Copy), r=[pt], w=[(AR, "vst")])
        for h in range(NH):
            P.dma("sp", lambda e, h=h: e.dma_start(out=Vsc[h, :, b * 4:(b + 1) * 4, 0:65], in_=vst[:, :, h, :]),
                  r=[(AR, "vst")], w=["Vsc"])

    def attn_prompt(b):
        T = 512
        nk = (b + 1) * 512
        nkt = nk // 128
        barrier([(AR, "vst"), (AR, "kst")], [(AR, "K"), (AR, "V"), (AR, "otm"), (AR, "PT0"), (AR, "PT1")])
        for h in range(NH):
            P.dma("sp", lambda e, h=h: e.dma_start(out=Kbuf[0:96, 0:nk], in_=Ksc[h, :, 0:nk]), r=["Ksc"], w=[(AR, "K")])
            P.dma("sp", lambda e, h=h: e.dma_start(out=Vbuf[:, 0:nkt, :], in_=Vsc[h, :, 0:nkt, 0:65]), r=["Vsc"], w=[(AR, "V")])
            def QX(kt, h=h):
                j = kt - 4 * b
                q0 = 0 if j < 0 else j * 128
                nq = 512 - q0
                pS = PS[kt % 2]
                P.op("pe", lambda e, kt=kt, q0=q0, nq=nq, pS=pS, h=h: e.matmul(
                    pS[:, 0:nq], lhsT=Kbuf[0:96, kt * 128:(kt + 1) * 128], rhs=R1[0:96, h, q0:512], start=True, stop=True),
                    r=[(AR, "K"), R1], w=[pS])
                pi = kt % 2
                PT = PTs[pi]
                pk = (AR, "PT%d" % pi)
                P.op("act", lambda e, nq=nq, pS=pS, PT=PT: e.activation(out=PT[:, 0:nq], in_=pS[:, 0:nq], func=AF.Exp),
                     r=[pS], w=[pk])
                if j >= 0:
                    P.op("dve", lambda e, PT=PT: e.tensor_tensor(out=PT[:, 0:128], in0=PT[:, 0:128], in1=triU_b[:], op=ALU.mult),
                         r=[pk, triU_b], w=[pk])
                return PT, pk, q0

            def PV(kt, st):
                PT, pk, q0 = st
                for qt in range(q0 // 128, 4):
                    last_kt = 4 * b + qt
                    P.op("pe", lambda e, kt=kt, qt=qt, q0=q0, PT=PT, last_kt=last_kt: e.matmul(
                        PS[6][:, qt * 65:(qt + 1) * 65], lhsT=PT[:, qt * 128 - q0:qt * 128 - q0 + 128], rhs=Vbuf[:, kt, :],
                        start=(kt == 0 and qt == 0), stop=(kt == last_kt), skip_group_check=True),
                        r=[pk, (AR, "V")], w=[PS[6]])
            cur = QX(0)
            for kt in range(nkt):
                nxt = QX(kt + 1) if kt + 1 < nkt else None
                PV(kt, cur)
                cur = nxt
            P.op("dve", lambda e: e.reciprocal(out=rec[:, 0:4], in_=PS[6][:, 0:260].rearrange("p (t x) -> p t x", x=65)[:, :, 64]),
                 r=[PS[6]], w=[rec])
            P.op("dve", lambda e, h=h: e.tensor_tensor(
                out=o_tm[:, :, h * 64:(h + 1) * 64], in0=PS[6][:, 0:260].rearrange("p (t x) -> p t x", x=65)[:, :, 0:64],
                in1=rec[:, 0:4].unsqueeze(2).to_broadcast([128, 4, 64]), op=ALU.mult), r=[PS[6], rec], w=[(AR, "otm")])
        for qt in range(4):
            for c in range(8):
                P.op("pe", lambda e, qt=qt, c=c: e.transpose(out=PSB[:, c * 128:(c + 1) * 128], in_=o_tm[:, qt, c * 128:(c + 1) * 128],
                                                             identity=ident_b[:]), r=[(AR, "otm"), ident_b], w=[PSB])
            P.op("act", lambda e, qt=qt: e.activation(out=big[:, 0:8, qt * 128:(qt + 1) * 128],
                                                      in_=PSB[:].rearrange("p (c n) -> p c n", n=128), func=AF.Copy),
                 r=[PSB], w=[big])

    def attn_sample():
        T = 128
        barrier([(AR, "vst"), (AR, "kst"), (AR, "K"), (AR, "V"), (AR, "otm"), (AR, "PT0"), (AR, "PT1"), (AR, "ssd"), fT],
                [(AR, "sa"), (AR, "Kb0"), (AR, "Kb1"), (AR, "KT0"), (AR, "KT1"), (AR, "PQ0"), (AR, "PQ1"), (AR, "qabs"), ("fTg", 0), ("fTg", 1)])
        for pn in range(4):
            bres, buf = panel(w_ukv, KVL, pn * 512, 512)
            for hh in range(4):
                h = pn * 4 + hh
                for kc in range(2):
                    P.op("pe", lambda e, kc=kc, hh=hh, buf=buf: e.transpose(
                        out=PSB[0:64, kc * 128:(kc + 1) * 128], in_=buf[:, kc, hh * 128:hh * 128 + 64], identity=ident_b[:]),
                        r=[bres, ident_b], w=[PSB])
                P.op("act", lambda e: e.activation(out=wukT[0:64, :], in_=PSB[0:64, 0:256], func=AF.Copy), r=[PSB], w=[(AR, "sa")])
                for kc in range(2):
                    pt = rot("pa", [PS[0], PS[1]])
                    P.op("pe", lambda e, kc=kc, h=h, pt=pt: e.matmul(
                        pt[:, 0:T], lhsT=wukT[0:64, kc * 128:(kc + 1) * 128], rhs=R1[0:64, h, 0:T], start=True, stop=True),
                        r=[(AR, "sa"), R1], w=[pt])
                    P.op("act", lambda e, kc=kc, h=h, pt=pt: e.activation(
                        out=qabs[:, kc, :].rearrange("p (st h) -> p st h", h=NH)[:, :, h], in_=pt[:, 0:T], func=AF.Copy),
                        r=[pt], w=[(AR, "qabs")])
                pt = rot("pa", [PS[0], PS[1]])
                P.op("pe", lambda e, h=h, pt=pt: e.matmul(pt[0:32, 0:T], lhsT=ident_b[64:96, 64:96], rhs=R1[64:96, h, 0:T],
                                                         start=True, stop=True), r=[ident_b, R1], w=[pt])
                P.op("act", lambda e, h=h, pt=pt: e.activation(
                    out=qabs[0:32, 2, :].rearrange("p (st h) -> p st h", h=NH)[:, :, h], in_=pt[0:32, 0:T], func=AF.Copy),
                    r=[pt], w=[(AR, "qabs")])
        gf, pf = tA[0][:, 0:256], tA[1][:, 0:256]
        P.op("pool", lambda e: e.iota(pti2[:].rearrange("p (a b) -> p a b", b=16), pattern=[[0, NSS], [1, 16]], base=0,
                                      channel_multiplier=0), w=[pti2])
        P.op("dve", lambda e: e.tensor_copy(out=gf, in_=pti2[:]), r=[pti2], w=[tA[0]])
        P.op("dve", lambda e: e.tensor_copy(out=pf[:, 0:NSS], in_=pti[:]), r=[pti], w=[tA[1]])
        P.op("dve", lambda e: e.scalar_tensor_tensor(
            out=gf.rearrange("p (a b) -> p a b", b=16), in0=pf[:, 0:NSS].unsqueeze(2).to_broadcast([128, NSS, 16]), scalar=16.0,
            in1=gf.rearrange("p (a b) -> p a b", b=16), op0=ALU.mult, op1=ALU.add), r=[tA[0], tA[1]], w=[tA[0]])
        P.op("dve", lambda e: e.tensor_copy(out=pti2[:], in_=gf), r=[tA[0]], w=[pti2])
        ckv_view = cache_kv.rearrange("n (g x) -> (n g) x", x=2048)
        cpe_view = cache_pe.rearrange("n (g x) -> (n g) x", x=256)
        for s in range(NSS):
            qs = lambda kc, K, s=s: qabs[0:K, kc, s * 128:(s + 1) * 128]
            def gather(g, s=s):
                gi = g % 2
                P.dma("pool", lambda e, g=g, s=s, gi=gi: e.indirect_dma_start(
                    out=Gk[gi][:], out_offset=None, in_=ckv_view,
                    in_offset=bass.IndirectOffsetOnAxis(ap=pti2[:, s * 16 + g:s * 16 + g + 1], axis=0)), r=[pti2], w=[GkK[gi]])
                P.dma("pool", lambda e, g=g, s=s, gi=gi: e.indirect_dma_start(
                    out=Gp[gi][:], out_offset=None, in_=cpe_view,
                    in_offset=bass.IndirectOffsetOnAxis(ap=pti2[:, s * 16 + g:s * 16 + g + 1], axis=0)), r=[pti2], w=[tA[gi]])

            def cast(g):
                gi = g % 2
                Kb = Kbs[gi]
                kbk = (AR, "Kb%d" % gi)
                P.op("act", lambda e, gi=gi, Kb=Kb: e.activation(out=Kb[:, :, 0:256], in_=Gk[gi][:].rearrange("p (r x) -> p r x", x=256),
                                                                 func=AF.Copy), r=[GkK[gi]], w=[kbk])
                P.op("dve", lambda e, gi=gi, Kb=Kb: e.tensor_copy(out=Kb[:, :, 257:289], in_=Gp[gi][:].rearrange("p (r x) -> p r x", x=32)),
                     r=[tA[gi]], w=[kbk])
                P.op("dve", lambda e, Kb=Kb: e.memset(Kb[:, :, 256:257], 1.0), w=[kbk])

            def stA(t):
                g, r_ = t // 8, t % 8
                if r_ == 0:
                    if g + 1 < 16:
                        gather(g + 1)
                    cast(g)
                gi, ti = g % 2, t % 2
                Kb, kbk = Kbs[gi], (AR, "Kb%d" % gi)
                TB, tbk = (PSB, PSB) if ti == 0 else (PSB2, PS[5])
                KT, ktk = KTs[ti], (AR, "KT%d" % ti)
                P.op("pe", lambda e: e.transpose(out=TB[:, 0:128], in_=Kb[:, r_, 0:128], identity=ident_b[:]), r=[kbk, ident_b], w=[tbk])
                P.op("pe", lambda e: e.transpose(out=TB[:, 128:256], in_=Kb[:, r_, 128:256], identity=ident_b[:]), r=[kbk, ident_b], w=[tbk])
                P.op("pe", lambda e: e.transpose(out=TB[0:32, 256:384], in_=Kb[:, r_, 257:289], identity=ident_b[:]), r=[kbk, ident_b], w=[tbk])
                P.op("act", lambda e: e.activation(out=KT[:, 0:2, :], in_=TB[:, 0:256].rearrange("p (c k) -> p c k", k=128), func=AF.Copy),
                     r=[tbk], w=[ktk])
                P.op("dve", lambda e: e.tensor_copy(out=KT[0:32, 2, :], in_=TB[0:32, 256:384]), r=[tbk], w=[ktk])

            def stC(t, qs=qs):
                ti = t % 2
                KT, ktk = KTs[ti], (AR, "KT%d" % ti)
                pS = PS[ti]
                for kc, K in ((0, 128), (1, 128), (2, 32)):
                    P.op("pe", lambda e, kc=kc, K=K: e.matmul(pS[:, 0:128], lhsT=KT[0:K, kc, :], rhs=qs(kc, K), start=(kc == 0), stop=(kc == 2)),
                         r=[ktk, (AR, "qabs")], w=[pS])
                PQ = PTq[ti]
                P.op("act", lambda e: e.activation(out=PQ[:], in_=pS[:, 0:128], func=AF.Exp), r=[pS], w=[(AR, "PQ%d" % ti)])

            def stE(t):
                g, r_, ti = t // 8, t % 8, t % 2
                Kb, kbk = Kbs[g % 2], (AR, "Kb%d" % (g % 2))
                PQ = PTq[ti]
                P.op("pe", lambda e: e.matmul(PS[6][:, 0:257], lhsT=PQ[:], rhs=Kb[:, r_, 0:257], start=(t == 0), stop=False),
                     r=[(AR, "PQ%d" % ti), kbk], w=[PS[6]])
            gather(0)
            NT_ = 128
            for i in range(-2, NT_):
                if i + 2 < NT_:
                    stA(i + 2)
                if 0 <= i + 1 < NT_:
                    stC(i + 1)
                if i >= 0:
                    stE(i)
            pS = PS[0]
            P.op("pe", lambda e, pS=pS, qs=qs: e.matmul(pS[:, 0:128], lhsT=ckv_b[:, 0, 0:128], rhs=qs(0, 128), start=True, stop=False),
                 r=[ckv_b, (AR, "qabs")], w=[pS])
            P.op("pe", lambda e, pS=pS, qs=qs: e.matmul(pS[:, 0:128], lhsT=ckv_b[:, 1, 0:128], rhs=qs(1, 128), start=False, stop=False),
                 r=[ckv_b, (AR, "qabs")], w=[pS])
            P.op("pe", lambda e, pS=pS, qs=qs: e.matmul(pS[:, 0:128], lhsT=kpe_b[0:32, 0:128], rhs=qs(2, 32), start=False, stop=True),
                 r=[kpe_b, (AR, "qabs")], w=[pS])
            pi = 0
            PQ = PTq[pi]
            P.op("act", lambda e, pS=pS, PQ=PQ: e.activation(out=PQ[:], in_=pS[:, 0:128], func=AF.Exp), r=[pS], w=[(AR, "PQ%d" % pi)])
            P.op("dve", lambda e, PQ=PQ, s=s: e.tensor_tensor(
                out=PQ[:].rearrange("p (t h) -> p t h", h=NH), in0=PQ[:].rearrange("p (t h) -> p t h", h=NH),
                in1=triS_b[:, s * 8:(s + 1) * 8].unsqueeze(2).to_broadcast([128, 8, NH]), op=ALU.mult),
                r=[(AR, "PQ%d" % pi), triS_b], w=[(AR, "PQ%d" % pi)])
            P.op("pe", lambda e, PQ=PQ: e.matmul(PS[6][:, 0:257], lhsT=PQ[:], rhs=Knew[:, 0:257], start=False, stop=True),
                 r=[(AR, "PQ%d" % pi), (AR, "Knew")], w=[PS[6]])
            P.op("dve", lambda e: e.reciprocal(out=rec[:, 0:1], in_=PS[6][:, 256:257]), r=[PS[6]], w=[rec])
            P.op("dve", lambda e: e.tensor_scalar(out=olat[:], in0=PS[6][:, 0:256], scalar1=rec[:, 0:1], scalar2=None, op0=ALU.mult),
                 r=[PS[6], rec], w=[(AR, "sa")])
            for kc in range(2):
                P.op("pe", lambda e, kc=kc: e.transpose(out=PSB[:, kc * 128:(kc + 1) * 128], in_=olat[:, kc * 128:(kc + 1) * 128],
                                                        identity=ident_b[:]), r=[(AR, "sa"), ident_b], w=[PSB])
            P.op("act", lambda e: e.activation(out=olatT[:], in_=PSB[:, 0:256].rearrange("p (c q) -> p c q", q=128), func=AF.Copy),
                 r=[PSB], w=[(AR, "sa")])
            for h in range(NH):
                for kc in range(2):
                    c0 = h * 128 + 64 if h % 2 == 0 else h * 128
                    P.op("pe", lambda e, kc=kc, h=h, c0=c0: e.matmul(
                        PS[3][:, h * 8:(h + 1) * 8], lhsT=wuv_res[:, kc, c0:c0 + 128],
                        rhs=olatT[:, kc, :].rearrange("p (t h) -> p t h", h=NH)[:, :, h], start=(kc == 0), stop=(kc == 1),
                        skip_group_check=True), r=[wuv_key, (AR, "sa")], w=[PS[3]])
            pv = PS[3][:, 0:128].rearrange("p (f two t) -> p f two t", two=2, t=8)
            P.op("dve", lambda e, s=s, pv=pv: e.tensor_copy(out=big[0:64, 0:8, s * 8:(s + 1) * 8], in_=pv[0:64, :, 0, :]),
                 r=[PS[3]], w=[big])
            P.op("dve", lambda e, s=s, pv=pv: e.tensor_copy(out=big[64:128, 0:8, s * 8:(s + 1) * 8], in_=pv[64:128, :, 1, :]),
                 r=[PS[3]], w=[big])

    wuv_res = big[:, 16:24, :].rearrange("p a b -> p (a b)").rearrange("p (kc n) -> p kc n", n=2048)
    wuv_key = big

    def load_wuv():
        src = w_ukv.rearrange("(kc p) n -> p kc n", p=128)
        P.dma("pool", lambda e: e.dma_start(out=wuv_res, in_=src), w=[big])

    def ssd(T, grp):
        nseq = 1 if grp == "p" else NSS
        LS = 128 // nseq
        TRI = triU if grp == "p" else triS
        BLK = ones_f if grp == "p" else blkS
        nch = T // 128
        xc = big
        barrier([(AR, "K"), (AR, "V"), (AR, "otm"), (AR, "PT0"), (AR, "PT1"), (AR, "sa"), (AR, "Kb0"), (AR, "Kb1"), (AR, "KT0"), (AR, "KT1"),
                 (AR, "PQ0"), (AR, "PQ1"), (AR, "qabs"), (AR, "Knew"), (AR, "vst"), (AR, "kst")], [(AR, "ssd")])
        SK = (AR, "ssd")
        bres, buf = panel(w_in, D, O_DT, 32)
        for t in range(nch):
            for kc in range(8):
                P.op("pe", lambda e, kc=kc, t=t, buf=buf: e.matmul(PS[2][:, t * 32:(t + 1) * 32], lhsT=hT[:, kc, t * 128:(t + 1) * 128],
                                                                  rhs=buf[:, kc, 0:32], start=(kc == 0), stop=(kc == 7)),
                     r=[bres, hT], w=[PS[2]])
            P.op("dve", lambda e, t=t: e.tensor_tensor(out=dtt[:, t, :], in0=PS[2][:, t * 32:(t + 1) * 32], in1=hv[:, 0:32], op=ALU.add),
                 r=[PS[2], hv], w=[dtt])
        P.op("act", lambda e: e.activation(out=dtt[:, 0:nch, :], in_=dtt[:, 0:nch, :], func=AF.Exp), r=[dtt], w=[dtt])
        P.op("act", lambda e: e.activation(out=dtt[:, 0:nch, :], in_=dtt[:, 0:nch, :], func=AF.Ln, bias=one_t[:, 0:1], scale=1.0),
             r=[dtt, one_t], w=[dtt])
        if grp == "s":
            P.dma("sp", lambda e: e.dma_start(out=stg[0:48, 0:1024], in_=st_conv[:, 0:1024]), w=[stg])
            for part in range(3):
                if part:
                    P.dma("sp", lambda e, part=part: e.dma_start(out=stg[0:48, 0:1024], in_=st_conv[:, part * 1024:(part + 1) * 1024]),
                          w=[stg])
                for q in range(8):
                    P.op("pe", lambda e, q=q: e.transpose(out=PS[5][:, q * 48:(q + 1) * 48], in_=stg[0:48, q * 128:(q + 1) * 128],
                                                          identity=ident_f[0:48, 0:48]), r=[stg, ident_f], w=[PS[5]])
                P.op("dve", lambda e, part=part: e.tensor_copy(out=histf[:, part * 8:(part + 1) * 8, :],
                                                               in_=PS[5][:, 0:384].rearrange("p (q x) -> p q x", x=48)),
                     r=[PS[5]], w=[histf])

        def ev_x(fc, m, pt):
            raw = rot("raw", rawt)
            acc = rot("tA", tA)
            if grp == "p":
                P.op("act", lambda e: e.activation(out=raw[:, 3:3 + T], in_=pt[:, 0:T], func=AF.Copy), r=[pt], w=[raw])
                P.op("dve", lambda e: e.tensor_copy(out=raw[:, 0:3], in_=histb[:, fc, :]), r=[histb], w=[raw])
                P.op("dve", lambda e: e.tensor_copy(out=histf[:, fc, 0:3], in_=pt[:, T - 3:T]), r=[pt], w=[histf])
                P.op("dve", lambda e: e.tensor_copy(out=histb[:, fc, :], in_=raw[:, T:T + 3]), r=[raw], w=[histb])
                win = lambda k: raw[:, k:k + T]
                accv = acc[:, 0:T]
                outv = xc[:, fc, 0:T]
            else:
                r3 = raw[:, 0:NSS * 11].rearrange("p (s x) -> p s x", x=11)
                P.op("act", lambda e: e.activation(out=r3[:, :, 3:11], in_=pt[:, 0:T].rearrange("p (s t) -> p s t", t=TS), func=AF.Copy),
                     r=[pt], w=[raw])
                P.op("dve", lambda e: e.tensor_copy(out=r3[:, :, 0:3], in_=histf[:, fc, :].rearrange("p (s j) -> p s j", j=3)),
                     r=[histf], w=[raw])
                P.op("dve", lambda e: e.tensor_copy(out=histf[:, fc, :].rearrange("p (s j) -> p s j", j=3),
                                                    in_=pt[:, 0:T].rearrange("p (s t) -> p s t", t=TS)[:, :, 5:8]),
                     r=[pt, raw], w=[histf])
                win = lambda k: r3[:, :, k:k + TS]
                accv = acc[:, 0:T].rearrange("p (s t) -> p s t", t=TS)
                outv = xc[:, fc, 0:T].rearrange("p (s t) -> p s t", t=TS)
            P.op("dve", lambda e: e.tensor_scalar(out=accv, in0=win(0), scalar1=vB[:, 6 + fc:7 + fc], scalar2=None, op0=ALU.mult),
                 r=[raw, vB], w=[acc])
            for k in (1, 2, 3):
                P.op("dve", lambda e, k=k: e.scalar_tensor_tensor(out=accv, in0=win(k), scalar=vB[:, 6 + k * 24 + fc:7 + k * 24 + fc],
                                                                  in1=accv, op0=ALU.mult, op1=ALU.add), r=[raw, vB, acc], w=[acc])
            P.op("act", lambda e: e.activation(out=outv, in_=accv, func=AF.Silu, bias=vB[:, 102 + fc:103 + fc], scale=1.0),
                 r=[acc, vB], w=[xc])
        linear(w_in, D, O_XBC, CONV, lambda kc: hT[:, kc, 0:T], hT, T, ev_x)

        for c in range(nch):
            tok = slice(c * 128, (c + 1) * 128)
            P.op("dve", lambda e, c=c: e.tensor_tensor(out=lat[:], in0=dtt[:, c, :], in1=a_bc[:], op=ALU.mult), r=[dtt, a_bc], w=[lat])
            P.op("pe", lambda e: e.matmul(PS[2][:, 0:32], lhsT=TRI[:], rhs=lat[:], start=True, stop=True), r=[TRI, lat], w=[PS[2]])
            P.op("pe", lambda e: e.matmul(PS[2][:, 32:64], lhsT=BLK[:], rhs=lat[:], start=True, stop=True), r=[BLK, lat], w=[PS[2]])
            P.op("dve", lambda e: e.tensor_copy(out=cumt[:], in_=PS[2][:, 0:32]), r=[PS[2]], w=[cumt])
            P.op("dve", lambda e: e.tensor_scalar(out=ncum[:], in0=PS[2][:, 0:32], scalar1=-1.0, scalar2=None, op0=ALU.mult),
                 r=[PS[2]], w=[ncum])
            P.op("dve", lambda e: e.tensor_tensor(out=wdec[:], in0=PS[2][:, 32:64], in1=cumt[:], op=ALU.subtract),
                 r=[PS[2], cumt], w=[wdec])
            P.op("act", lambda e: e.activation(out=wdec[:], in_=wdec[:], func=AF.Exp), r=[wdec], w=[wdec])
            for half in range(2):
                for q in range(8):
                    fc = half * 8 + q
                    P.op("pe", lambda e, fc=fc, q=q, tok=tok: e.transpose(out=PSB[:, q * 128:(q + 1) * 128], in_=xc[:, fc, tok],
                                                                        identity=ident_b[:]), r=[xc, ident_b], w=[PSB])
                P.op("dve", lambda e, half=half, c=c: e.tensor_tensor(
                    out=xdt[:, half * 1024:(half + 1) * 1024].rearrange("p (h x) -> p h x", x=64),
                    in0=PSB[:].rearrange("p (h x) -> p h x", x=64),
                    in1=dtt[:, c, half * 16:(half + 1) * 16].unsqueeze(2).to_broadcast([128, 16, 64]), op=ALU.mult),
                    r=[PSB, dtt], w=[SK])
            P.op("dve", lambda e: e.tensor_tensor(out=xdtw.rearrange("p (h x) -> p h x", x=64), in0=xdt.rearrange("p (h x) -> p h x", x=64),
                                                  in1=wdec[:].unsqueeze(2).to_broadcast([128, 32, 64]), op=ALU.mult),
                 r=[SK, wdec], w=[SK])
            for g in range(4):
                P.op("pe", lambda e, g=g, tok=tok: e.transpose(out=PSB[:, g * 128:(g + 1) * 128], in_=xc[:, 16 + g, tok], identity=ident_b[:]),
                     r=[xc, ident_b], w=[PSB])
            P.op("act", lambda e: e.activation(out=Btm, in_=PSB[:, 0:512].rearrange("p (g n) -> p g n", n=128), func=AF.Copy),
                 r=[PSB], w=[SK])
            for g in range(4):
                P.op("pe", lambda e, g=g, tok=tok: e.matmul(PS[3][:, g * 128:(g + 1) * 128], lhsT=xc[:, 16 + g, tok], rhs=xc[:, 20 + g, tok],
                                                          start=True, stop=True), r=[xc], w=[PS[3]])
            P.op("dve", lambda e: e.tensor_tensor(out=Gm[:], in0=PS[3][:].rearrange("p (g l) -> p g l", l=128),
                                                  in1=TRI[:].unsqueeze(1).to_broadcast([128, 4, 128]), op=ALU.mult),
                 r=[PS[3], TRI], w=[Gm])
            for h in range(SH):
                g = h // 8
                P.op("dve", lambda e, h=h: e.tensor_scalar(out=dgh[:], in0=ident_f[:], scalar1=cumt[:, h:h + 1], scalar2=None, op0=ALU.mult),
                     r=[ident_f, cumt], w=[dgh])
                pc = rot("pa", [PS[0], PS[1]])
                P.op("pe", lambda e, pc=pc: e.matmul(pc[:, 0:128], lhsT=ones_f[:], rhs=dgh[:], start=True, stop=True),
                     r=[ones_f, dgh], w=[pc])
                P.op("dve", lambda e, h=h, pc=pc: e.tensor_scalar(out=segt[:], in0=pc[:, 0:128], scalar1=ncum[:, h:h + 1], scalar2=0.0,
                                                                 op0=ALU.add, op1=ALU.min), r=[pc, ncum], w=[segt])
                P.op("act", lambda e, pc=pc: e.activation(out=ecr[:], in_=pc[:, 0:128], func=AF.Exp), r=[pc], w=[ecr])
                P.op("act", lambda e: e.activation(out=segt[:], in_=segt[:], func=AF.Exp), r=[segt], w=[segt])
                P.op("dve", lambda e, h=h, g=g: e.tensor_tensor(out=MTs[:, h, :], in0=segt[:], in1=Gm[:, g, :], op=ALU.mult),
                     r=[segt, Gm], w=[SK])
                P.op("dve", lambda e, h=h, g=g, tok=tok: e.tensor_tensor(out=Css[:, h, :], in0=ecr[:], in1=xc[:, 20 + g, tok], op=ALU.mult),
                     r=[ecr, xc], w=[SK])
                P.op("dve", lambda e, h=h: e.tensor_copy(out=edch[:, h, 0:nseq], in_=ecr[:, LS - 1::LS]), r=[ecr], w=[edch])
            HG = 4 if grp == "p" else 32
            for s in range(nseq):
                cols = slice(s * LS, (s + 1) * LS)
                if grp == "s":
                    for hf in range(2):
                        P.dma("sp", lambda e, s=s, hf=hf: e.dma_start(
                            out=stg[:].rearrange("p (f n) -> p f n", n=128),
                            in_=st_ssm[s, hf * 1024:(hf + 1) * 1024, :].rearrange("(f p) n -> p f n", p=128)), w=[stg])
                        for q4 in range(2):
                            for q in range(4):
                                f = q4 * 4 + q
                                P.op("pe", lambda e, f=f, q=q: e.transpose(out=PS[5][:, q * 128:(q + 1) * 128],
                                                                           in_=stg[:, f * 128:(f + 1) * 128], identity=ident_f[:]),
                                     r=[stg, ident_f], w=[PS[5]])
                            o0 = (hf * 2 + q4) * 512
                            P.op("act", lambda e, o0=o0: e.activation(out=hstate[:, o0:o0 + 512], in_=PS[5][:], func=AF.Copy),
                                 r=[PS[5]], w=[hstate])
                            P.op("dve", lambda e, o0=o0: e.tensor_copy(out=hstate_b[:, o0:o0 + 512], in_=PS[5][:]),
                                 r=[PS[5]], w=[hstate_b])
                for hg in range(SH // HG):
                    py = rot("py", [PS[2], PS[3]])
                    for hh in range(HG):
                        h = hg * HG + hh
                        pr = (h // 2) * 128
                        reg = py[:, hh * LS:(hh + 1) * LS]
                        P.op("pe", lambda e, h=h, pr=pr, reg=reg, cols=cols, hh=hh: e.matmul(
                            reg, lhsT=xdt[:, pr:pr + 128], rhs=MTs[:, h, cols], start=(hh == 0), stop=False, skip_group_check=True),
                            r=[SK], w=[py])
                        P.op("pe", lambda e, h=h, pr=pr, reg=reg, cols=cols: e.matmul(
                            reg, lhsT=hstate_b[:, pr:pr + 128], rhs=Css[:, h, cols], start=False, stop=True, skip_group_check=True),
                            r=[SK, hstate_b], w=[py])
                    nf = HG // 2
                    f0 = hg * nf
                    pv = py[:, 0:HG * LS].rearrange("p (f two l) -> p f two l", two=2, l=LS)
                    for (rows, two) in ((slice(0, 64), 0), (slice(64, 128), 1)):
                        t_ = rot("tA", tA)
                        tv = t_[:, 0:nf * LS].rearrange("p (f l) -> p f l", l=LS)
                        xv = xc[rows, f0:f0 + nf, c * 128 + s * LS:c * 128 + (s + 1) * LS]
                        P.op("dve", lambda e, rows=rows, tv=tv, xv=xv, f0=f0, nf=nf: e.tensor_tensor(
                            out=tv[rows], in0=xv, in1=vC[rows, 16 + f0:16 + f0 + nf].unsqueeze(2).to_broadcast([64, nf, LS]), op=ALU.mult),
                            r=[xc, vC], w=[t_])
                        P.op("dve", lambda e, rows=rows, tv=tv, two=two, pv=pv, f0=f0, nf=nf, c=c, s=s: e.tensor_tensor(
                            out=R1[rows, f0:f0 + nf, c * 128 + s * LS:c * 128 + (s + 1) * LS], in0=tv[rows], in1=pv[rows, :, two, :],
                            op=ALU.add), r=[t_, py], w=[R1])
                if grp == "s":
                    P.op("dve", lambda e, s=s: e.tensor_scalar(out=Bms, in0=Btm, scalar1=blkS[:, s * 8:s * 8 + 1], scalar2=None, op0=ALU.mult),
                         r=[SK, blkS], w=[SK])
                    Bl = Bms
                else:
                    Bl = Btm
                for g in range(4):
                    pu = rot("pa", [PS[0], PS[1]])
                    P.op("pe", lambda e, g=g, pu=pu, Bl=Bl: e.matmul(pu[:, 0:512], lhsT=Bl[:, g, :], rhs=xdtw[:, g * 512:(g + 1) * 512],
                                                                  start=True, stop=True), r=[SK], w=[pu])
                    hsv = hstate[:, g * 512:(g + 1) * 512].rearrange("p (h x) -> p h x", x=64)
                    P.op("dve", lambda e, g=g, hsv=hsv, s=s: e.tensor_tensor(
                        out=hsv, in0=hsv, in1=edch[:, g * 8:(g + 1) * 8, s:s + 1].to_broadcast([128, 8, 64]), op=ALU.mult),
                        r=[hstate, edch], w=[hstate])
                    P.op("dve", lambda e, g=g, pu=pu: e.tensor_tensor(out=hstate[:, g * 512:(g + 1) * 512], in0=hstate[:, g * 512:(g + 1) * 512],
                                                                      in1=pu[:, 0:512], op=ALU.add), r=[hstate, pu], w=[hstate])
                if grp == "s":
                    store_state(ssms[s], "ssms")
                else:
                    P.op("act", lambda e: e.activation(out=hstate_b[:], in_=hstate[:], func=AF.Copy), r=[hstate], w=[hstate_b])

    rawt = sq

    def store_state(dst, dname):
        for q4 in range(4):
            for q in range(4):
                f = q4 * 4 + q
                P.op("pe", lambda e, f=f, q=q: e.transpose(out=PS[5][:, q * 128:(q + 1) * 128], in_=hstate[:, f * 128:(f + 1) * 128],
                                                           identity=ident_f[:]), r=[hstate, ident_f], w=[PS[5]])
            P.op("act", lambda e, q4=q4: e.activation(out=stg[:, 0:512], in_=PS[5][:], func=AF.Copy), r=[PS[5]], w=[stg])
            P.dma("sp", lambda e, q4=q4: e.dma_start(out=dst[q4 * 512:(q4 + 1) * 512, :].rearrange("(f p) n -> p f n", p=128),
                                                     in_=stg[:, 0:512].rearrange("p (f n) -> p f n", n=128)), r=[stg], w=[dname])

    def store_conv(grp):
        n = 3 if grp == "p" else 48
        dst = convp if grp == "p" else convs
        for part in range(6):
            for q in range(4):
                fc = part * 4 + q
                P.op("pe", lambda e, fc=fc, q=q: e.transpose(out=PS[5][0:n, q * 128:(q + 1) * 128], in_=histf[:, fc, 0:n],
                                                             identity=ident_f[:]), r=[histf, ident_f], w=[PS[5]])
            P.op("act", lambda e: e.activation(out=stg[0:n, 0:512], in_=PS[5][0:n, :], func=AF.Copy), r=[PS[5]], w=[stg])
            P.dma("sp", lambda e, part=part: e.dma_start(out=dst[:, part * 512:(part + 1) * 512], in_=stg[0:n, 0:512]),
                  r=[stg], w=["conv" + grp])

    def mix_out(T, segs):
        def ev_oa(i, m, pt):
            P.op("act", lambda e: e.activation(out=fT[:, i, 0:T], in_=pt[:, 0:T], func=AF.Copy), r=[pt], w=[fT])
        linear(w_oa, D, 0, D, lambda kc: big[:, kc, 0:T], big, T, ev_oa)

    def mix_out2(T, segs):
        yT = R1
        def ev_z(i, m, pt):
            t_ = rot("sq", sq)
            P.op("act", lambda e: e.activation(out=t_[:, 0:T], in_=pt[:, 0:T], func=AF.Silu), r=[pt], w=[t_])
            P.op("dve", lambda e: e.tensor_tensor(out=yT[:, i, 0:T], in0=yT[:, i, 0:T], in1=t_[:, 0:T], op=ALU.mult),
                 r=[yT, t_], w=[yT])
        linear(w_in, D, O_Z, SI, lambda kc: hT[:, kc, 0:T], hT, T, ev_z)
        rms_rstd(yT, 16, T, SI, rstd)
        for c in range(16):
            P.op("dve", lambda e, c=c: e.tensor_scalar(out=yT[:, c, 0:T], in0=yT[:, c, 0:T], scalar1=vC[:, c:c + 1], scalar2=None,
                                                       op0=ALU.mult), r=[yT, vC], w=[yT])
        def ev_ga(i, m, pt):
            t_ = rot("tA", tA)
            P.op("act", lambda e: e.activation(out=t_[:, 0:T], in_=pt[:, 0:T], func=AF.Sigmoid), r=[pt], w=[t_])
            P.op("dve", lambda e: e.tensor_tensor(out=fT[:, i, 0:T], in0=fT[:, i, 0:T], in1=t_[:, 0:T], op=ALU.mult),
                 r=[fT, t_], w=[fT])
        linear(w_in, D, O_GA, D, lambda kc: hT[:, kc, 0:T], hT, T, ev_ga)
        gs = big
        def ev_gs(i, m, pt):
            P.op("act", lambda e: e.activation(out=big[:, 8 + i, 0:T], in_=pt[:, 0:T], func=AF.Sigmoid), r=[pt], w=[big])
        linear(w_in, D, O_GS, D, lambda kc: hT[:, kc, 0:T], hT, T, ev_gs)
        def ev_os(i, m, pt):
            t_ = rot("tA", tA)
            P.op("dve", lambda e: e.tensor_tensor(out=t_[:, 0:T], in0=pt[:, 0:T], in1=rstd[:, 0:T], op=ALU.mult), r=[pt, rstd], w=[t_])
            P.op("dve", lambda e: e.tensor_tensor(out=t_[:, 0:T], in0=t_[:, 0:T], in1=big[:, 8 + i, 0:T], op=ALU.mult), r=[t_, big], w=[t_])
            P.op("dve", lambda e: e.tensor_tensor(out=fT[:, i, 0:T], in0=fT[:, i, 0:T], in1=t_[:, 0:T], op=ALU.add), r=[fT, t_], w=[fT])
        linear(w_os, SI, 0, D, lambda kc: yT[:, kc, 0:T], yT, T, ev_os)
        P.op("act", lambda e: e.activation(out=hT[:, :, 0:T], in_=fT[:, :, 0:T], func=AF.Copy), r=[fT], w=[hT])

        def ev_o(i, m, pt):
            P.op("act", lambda e: e.activation(out=fT[:, i, 0:T], in_=pt[:, 0:T], func=AF.Copy), r=[pt], w=[fT])
        linear(w_out, D, 0, D, lambda kc: hT[:, kc, 0:T], hT, T, ev_o)
        residual(1, T, segs)

    def dump_fm(src, nchunk, T, dst, dname, bf=False):
        for t in range(T // 128):
            for c0 in range(0, nchunk, 4):
                for q in range(4):
                    if bf:
                        P.op("pe", lambda e, c0=c0, q=q, t=t: e.transpose(out=PSB[:, q * 128:(q + 1) * 128],
                                                                          in_=src[:, c0 + q, t * 128:(t + 1) * 128], identity=ident_b[:]),
                             r=[src, ident_b], w=[PSB])
                    else:
                        P.op("pe", lambda e, c0=c0, q=q, t=t: e.transpose(out=PS[5][:, q * 128:(q + 1) * 128],
                                                                          in_=src[:, c0 + q, t * 128:(t + 1) * 128], identity=ident_f[:]),
                             r=[src, ident_f], w=[PS[5]])
                P.op("act", lambda e: e.activation(out=stg[:, 0:512], in_=(PSB[:, 0:512] if bf else PS[5][:]), func=AF.Copy),
                     r=[PSB if bf else PS[5]], w=[stg])
                P.dma("sp", lambda e, c0=c0, t=t: e.dma_start(out=dst[t * 128:(t + 1) * 128, c0 * 128:(c0 + 4) * 128], in_=stg[:, 0:512]),
                      r=[stg], w=[dname])

    def program():
        wstate["i"] = 0
        cnt.clear()
        constants()
        if not P.dry:
            convert_weights()
        adaln()
        for b in range(n_pblk):
            T = 512
            segs = [(0, 0, T)]
            load_xT(xp[b * 512:(b + 1) * 512, :], T)
            ffn(0, T, segs)
            mixer_kvq(T, segs, b * 512, "p", b * 512)
            if stage >= 5:
                kv_gen(b)
                attn_prompt(b)
                mix_out(T, segs)
                if dbg and dbg == "p%d" % b:
                    dump_fm(fT, 8, T, dbg1, "dbg1")
            if stage >= 6:
                ssd(T, "p")
                if dbg and dbg == "p%d" % b:
                    dump_fm(R1, 16, T, dbg2, "dbg2", bf=True)
            if stage >= 7:
                mix_out2(T, segs)
                ffn(2, T, segs)
                store_xT(yp[b * 512:(b + 1) * 512, :], T, "yp")
        if n_pblk and stage >= 6:
            store_state(ssmp, "ssmp")
            store_conv("p")
        if do_sample:
            T = NSS * TS
            segs = [(1 + i, i * TS, TS) for i in range(NSS)]
            load_xT(xs, T)
            ffn(0, T, segs)
            mixer_kvq(T, segs, None, "s", 0)
            if stage >= 5:
                load_wuv()
                attn_sample()
                barrier([("fTg", 0), ("fTg", 1)], [fT])
                mix_out(T, segs)
                if dbg == "s":
                    dump_fm(fT, 8, T, dbg1, "dbg1")
            if stage >= 6:
                ssd(T, "s")
                store_conv("s")
                if dbg == "s":
                    dump_fm(R1, 16, T, dbg2, "dbg2", bf=True)
            if stage >= 7:
                mix_out2(T, segs)
                ffn(2, T, segs)
                store_xT(ys, T, "ys")

    P.dry = True
    program()
    P.dry = False
    wstate["issued"] = 0
    program()
    P.emit()
    return nc


def _prep_inputs(inp, need_cache=True):
    f = lambda k: np.ascontiguousarray(np.asarray(inp[k], dtype=np.float32))
    half = ROPE // 2
    inv = (10000.0 ** (-np.arange(half, dtype=np.float32) / half)).astype(np.float32)
    invf = np.tile(inv, 8).reshape(128, 1).astype(np.float32)
    vecA = np.concatenate([f("b_ada").reshape(72, 128), f("g_pre").reshape(24, 128), f("g_post").reshape(24, 128)], 0)
    vecB = np.concatenate([f("g_q_lat").reshape(4, 128), f("g_kv_lat").reshape(2, 128),
                           f("conv_w").reshape(96, 128), f("conv_b").reshape(24, 128)], 0)
    vecC = np.concatenate([f("g_ssm_norm").reshape(16, 128), np.repeat(f("d_skip").reshape(32), 64).reshape(16, 128)], 0)
    hvec = np.tile(np.concatenate([f("dt_bias").reshape(32), f("a_log").reshape(32), f("d_skip").reshape(32)])[None, :], (128, 1))
    shared = dict(
        invf=invf, w_ada=f("w_ada")[0], vecA=np.ascontiguousarray(vecA), vecB=np.ascontiguousarray(vecB),
        vecC=np.ascontiguousarray(vecC), hvec=np.ascontiguousarray(hvec.astype(np.float32)),
        w_gate=f("w_ffn_gate")[0].reshape(2 * D, DFF), w_up=f("w_ffn_up")[0].reshape(2 * D, DFF),
        w_down=f("w_ffn_down")[0].reshape(2 * DFF, D), w_in=f("w_in")[0],
        w_uq=f("w_uq")[0], w_ukv=f("w_ukv")[0], w_oa=f("w_o_attn")[0], w_os=f("w_o_ssm")[0], w_out=f("w_out")[0],
    )
    if need_cache:
        shared["cache_kv"] = np.asarray(inp["cache_kv_latent"], dtype=np.float32).reshape(20480, 128 * KVL)
        shared["cache_pe"] = np.asarray(inp["cache_k_rope"], dtype=np.float32).reshape(20480, 128 * ROPE)
    else:
        shared["cache_kv"] = np.zeros((1, 1), np.float32)
        shared["cache_pe"] = np.zeros((1, 1), np.float32)
    xpa, xsa, cpa, csa = f("x_prompt"), f("x_sample"), f("c_prompt"), f("c_sample")
    pt = np.asarray(inp["page_table"], dtype=np.int32)
    stc, sts = f("state_conv")[0], np.asarray(inp["state_ssm"], dtype=np.float32)[0]
    maps = []
    for c in range(8):
        m = dict(shared)
        sl = slice(c * NSS, (c + 1) * NSS)
        m["xp"] = xpa[c % 4]
        m["xs"] = xsa[sl].reshape(NSS * TS, D)
        m["cc"] = np.concatenate([cpa[c % 4:c % 4 + 1], csa[sl]], 0)
        m["ptT"] = np.ascontiguousarray(pt[sl].T)
        m["st_conv"] = np.ascontiguousarray(stc[sl].reshape(NSS * 3, CONV))
        m["st_ssm"] = np.ascontiguousarray(sts[sl].reshape(NSS, SI, SN))
        maps.append(m)
    return maps


def run(inp, need_cache=True, dev_small=None, trace=False, **bk):
    nc = build(**bk)
    maps = _prep_inputs(inp, need_cache)
    if dev_small is not None:
        ckv_s, cpe_s = dev_small
        for m in maps:
            for k in ("xs", "cc", "st_conv", "st_ssm"):
                m[k] = maps[0][k] if k != "cc" else np.concatenate([m["cc"][0:1], maps[0]["cc"][1:]], 0)
            m["cache_kv"], m["cache_pe"] = ckv_s, cpe_s
            m["ptT"] = np.ascontiguousarray(np.arange(2048, dtype=np.int32).reshape(16, 128).T)
    if trace:
        res = run_bass_kernel_spmd(nc, maps, core_ids=list(range(8)), trace=True)
        print("EXEC_TIME_NS", res.exec_time_ns)
        return res.results
    res = run_bass_kernel_spmd(nc, maps, core_ids=list(range(8)))
    return res.results


def kernel(**inp):
    r = run(inp)
    cat = lambda k, cores: np.stack([r[c][k] for c in cores], 0)
    y_p = cat("yp", range(4))
    y_s = np.concatenate([r[c]["ys"].reshape(NSS, TS, D) for c in range(8)], 0)
    kv_p = cat("kvp", range(4))[None]
    pe_p = cat("pep", range(4))[None]
    cv_p = cat("convp", range(4))[None]
    ss_p = cat("ssmp", range(4)).reshape(1, 4, SH, SHD, SN)
    kv_s = np.concatenate([r[c]["kvs"].reshape(NSS, TS, KVL) for c in range(8)], 0)[None]
    pe_s = np.concatenate([r[c]["pes"].reshape(NSS, TS, ROPE) for c in range(8)], 0)[None]
    cv_s = np.concatenate([r[c]["convs"].reshape(NSS, 3, CONV) for c in range(8)], 0)[None]
    ss_s = np.concatenate([r[c]["ssms"].reshape(NSS, SH, SHD, SN) for c in range(8)], 0)[None]
    return tuple(np.ascontiguousarray(a, dtype=np.float32) for a in (y_p, y_s, kv_p, pe_p, cv_p, ss_p, kv_s, pe_s, cv_s, ss_s))
```

```python
from contextlib import ExitStack
import math
import numpy as np
import concourse.bass as bass
import concourse.mybir as mybir
from concourse.bass_utils import run_bass_kernel_spmd

F32 = mybir.dt.float32
BF16 = mybir.dt.bfloat16
I32 = mybir.dt.int32
ALU = mybir.AluOpType
AF = mybir.ActivationFunctionType

D = 1024
SEQ = 4096
NSS = 16
TS = 8
DFF = 2816
DIN = 8000
QL, KVL, ROPE = 512, 256, 32
NH, NOPE, VH = 16, 64, 64
SI, SHD, SH, SG, SN = 2048, 64, 32, 4, 128
CONV = 3072
EPS = 1e-6
ATT_SCALE = (NOPE + ROPE) ** -0.5
PAST = 16384
O_Q, O_KV, O_PE, O_Z, O_XBC, O_DT, O_GA, O_GS = 0, 512, 768, 800, 2848, 5920, 5952, 6976


class Prog:
    COMPUTE = ("pe", "act", "dve", "pool")
    NDMA = {"sp": 24, "pool": 16, "act": 8}

    def __init__(self, nc):
        self.nc = nc
        self.ops = []
        self.stack = ExitStack()
        self.dry = False

    def sb(self, name, shape, dt):
        return self.stack.enter_context(self.nc.sbuf_tensor(name, list(shape), dt))

    def ps(self, name, shape, dt=F32):
        return self.stack.enter_context(self.nc.psum_tensor(name, list(shape), dt))

    @staticmethod
    def _k(x):
        if isinstance(x, str):
            return x
        if isinstance(x, tuple):
            return tuple(Prog._k(i) if not isinstance(i, int) else i for i in x)
        return "T:" + x.name

    def op(self, eng, fn, r=(), w=()):
        if self.dry:
            return
        r = [self._k(i) for i in r]
        w = [self._k(i) for i in w]
        w += [i for i in r if isinstance(i, str) and i.startswith("T:ps") and i not in w]
        self.ops.append(dict(eng=eng, fn=fn, r=r, w=w, dma=False))

    def dma(self, q, fn, r=(), w=()):
        if self.dry:
            return
        self.ops.append(dict(eng=q, fn=fn, r=[self._k(i) for i in r], w=[self._k(i) for i in w], dma=True))

    def plan(self):
        cnt = {e: 0 for e in self.COMPUTE}
        dcount = {}
        rr = {q: 0 for q in self.NDMA}
        lastw = {}
        readers = {}
        waited = {e: {} for e in ("pe", "act", "dve", "pool", "sp")}
        for op in self.ops:
            E = op["eng"]
            need = {}

            def req(t, kind):
                if t is None:
                    return
                s, v, te = t
                if te is not None and te == E:
                    if E == "pe" or kind == "war":
                        return
                if v > need.get(s, 0):
                    need[s] = v

            for r in op["r"]:
                req(lastw.get(r), "raw")
            for w in op["w"]:
                req(lastw.get(w), "waw")
                for s, (v, te) in readers.get(w, {}).items():
                    req((s, v, te), "war")
            if op["dma"]:
                s = "d_%s_%d" % (E, rr[E])
                rr[E] = (rr[E] + 1) % self.NDMA[E]
                prev = dcount.get(s, 0)
                if prev and 16 * prev > need.get(s, 0):
                    need[s] = 16 * prev
                dcount[s] = prev + 1
                tick = (s, 16 * (prev + 1), None)
                op["inc"] = 16
            else:
                cnt[E] += 1
                tick = (E, cnt[E], E)
                op["inc"] = 1
            wl = []
            for s, v in need.items():
                if v > waited[E].get(s, 0):
                    waited[E][s] = v
                    wl.append((s, v))
            op["waits"] = wl
            op["tick"] = tick
            for r in op["r"]:
                d = readers.setdefault(r, {})
                if tick[1] > d.get(tick[0], (0, None))[0]:
                    d[tick[0]] = (tick[1], tick[2])
            for w in op["w"]:
                lastw[w] = tick
                readers[w] = {}
        self.final = {k: v for k, v in cnt.items() if v}
        self.final.update({s: 16 * c for s, c in dcount.items()})
        self.waited = waited

    def emit(self):
        self.plan()
        nc = self.nc
        sems = {}
        with ExitStack() as st:
            for s in self.final:
                sems[s] = st.enter_context(nc.semaphore("s_" + s))
            block = st.enter_context(nc.Block())

            def replay(name, e):
                for op in self.ops:
                    if op["eng"] != name:
                        continue
                    for s, v in op["waits"]:
                        e.wait_ge(sems[s], v)
                    ins = op["fn"](e)
                    ins.then_inc(sems[op["tick"][0]], op["inc"])
                if name == "sp":
                    for s, v in self.final.items():
                        if v > self.waited["sp"].get(s, 0):
                            e.wait_ge(sems[s], v)

            @block.tensor
            def _(e):
                replay("pe", e)

            @block.scalar
            def _(e):
                replay("act", e)

            @block.vector
            def _(e):
                replay("dve", e)

            @block.gpsimd
            def _(e):
                replay("pool", e)

            @block.sync
            def _(e):
                replay("sp", e)
        self.stack.close()


def build(n_pblk=8, do_sample=True, stage=9, dbg=False, n_phys=20480):
    nc = bass.Bass("TRN2", target_bir_lowering=False)
    P = Prog(nc)

    def din(name, shape, dt=F32):
        return nc.dram_tensor(name, list(shape), dt, kind="ExternalInput").ap()

    def dout(name, shape, dt=F32):
        return nc.dram_tensor(name, list(shape), dt, kind="ExternalOutput").ap()

    xp = din("xp", [SEQ, D])
    xs = din("xs", [NSS * TS, D])
    cc = din("cc", [1 + NSS, D])
    invf = din("invf", [128, 1])
    w_ada = din("w_ada", [D, 9 * D])
    vecA = din("vecA", [120, 128])
    vecB = din("vecB", [126, 128])
    vecC = din("vecC", [32, 128])
    hvec = din("hvec", [128, 96])
    w_gate = din("w_gate", [2 * D, DFF])
    w_up = din("w_up", [2 * D, DFF])
    w_down = din("w_down", [2 * DFF, D])
    w_in = din("w_in", [D, DIN])
    w_uq = din("w_uq", [QL, NH * 96])
    w_ukv = din("w_ukv", [KVL, NH * 128])
    w_oa = din("w_oa", [D, D])
    w_os = din("w_os", [SI, D])
    w_out = din("w_out", [D, D])
    cache_kv = din("cache_kv", [n_phys, 128 * KVL])
    cache_pe = din("cache_pe", [n_phys, 128 * ROPE])
    ptT = din("ptT", [128, NSS], I32)
    st_conv = din("st_conv", [NSS * 3, CONV])
    st_ssm = din("st_ssm", [NSS, SI, SN])
    yp = dout("yp", [SEQ, D])
    ys = dout("ys", [NSS * TS, D])
    kvp = dout("kvp", [SEQ, KVL])
    pep = dout("pep", [SEQ, ROPE])
    convp = dout("convp", [3, CONV])
    ssmp = dout("ssmp", [SI, SN])
    kvs = dout("kvs", [NSS * TS, KVL])
    pes = dout("pes", [NSS * TS, ROPE])
    convs = dout("convs", [NSS * 3, CONV])
    ssms = dout("ssms", [NSS, SI, SN])
    dbg1 = dout("dbg1", [512, D]) if dbg else None
    dbg2 = dout("dbg2", [512, SI]) if dbg else None
    Ksc = nc.dram_tensor("Ksc", [NH, 96, SEQ], BF16).ap()
    Vsc = nc.dram_tensor("Vsc", [NH, 128, SEQ // 128, 66], BF16).ap()

    ident_f = P.sb("ident_f", [128, 128], F32)
    ident_b = P.sb("ident_b", [128, 128], BF16)
    ones_b = P.sb("ones_b", [128, 128], BF16)
    ones_f = P.sb("ones_f", [128, 128], F32)
    triU = P.sb("triU", [128, 128], F32)
    triS = P.sb("triS", [128, 128], F32)
    blkS = P.sb("blkS", [128, 128], F32)
    triU_b = P.sb("triU_b", [128, 128], BF16)
    triS_b = P.sb("triS_b", [128, 128], BF16)
    Esel = P.sb("Esel", [16, 128], F32)
    eps_t = P.sb("eps_t", [128, 1], F32)
    one_t = P.sb("one_t", [128, 1], F32)
    npi_t = P.sb("npi_t", [128, 1], F32)
    invf_t = P.sb("invf_t", [128, 1], F32)
    Rt = P.sb("Rt", [128, 96], BF16)
    Rtmp = P.sb("Rtmp", [32, 64], F32)
    vA = P.sb("vA", [128, 120], F32)
    vB = P.sb("vB", [128, 126], F32)
    vC = P.sb("vC", [128, 32], F32)
    hv = P.sb("hv", [128, 96], F32)
    a_bc = P.sb("a_bc", [128, 32], F32)
    stg = P.sb("stg", [128, 1024], F32)
    cT = P.sb("cT", [128, 8, 17], F32)
    modT = P.sb("modT", [128, 72, 17], F32)
    At = P.sb("At", [128, 24, 17], F32)
    Gt = P.sb("Gt", [128, 24, 17], F32)
    NWB = 4
    WBE = 4096
    WB = [P.sb("wb%d" % i, [128, WBE], BF16) for i in range(NWB)]
    xT = P.sb("xT", [128, 8, 512], F32)
    fT = P.sb("fT", [128, 8, 512], F32)
    wada = [xT, fT]
    hT = P.sb("hT", [128, 8, 512], BF16)
    big = P.sb("big", [128, 24, 512], BF16)
    hid = big
    R1 = P.sb("R1", [128, 16, 512], BF16)
    rstd = P.sb("rstd", [128, 512], F32)
    tA = [P.sb("tA%d" % i, [128, 512], F32) for i in range(2)]
    sq = [P.sb("sq%d" % i, [128, 520], BF16) for i in range(2)]
    ckv = P.sb("ckv", [128, 2, 512], F32)
    ckv_b = P.sb("ckv_b", [128, 2, 512], BF16)
    kpe = P.sb("kpe", [32, 512], F32)
    kpe_b = P.sb("kpe_b", [32, 512], BF16)
    cos_t = P.sb("cos_t", [128, 512], F32)
    sin_t = P.sb("sin_t", [128, 512], F32)
    pos_i = P.sb("pos_i", [128, 512], I32)
    qn = big[:, 20:24, :]
    otm = [P.sb("otm%d" % i, [128, 288], F32) for i in range(2)]
    hstate = P.sb("hstate", [128, SI], F32)
    hstate_b = P.sb("hstate_b", [128, SI], BF16)
    histb = P.sb("histb", [128, 24, 3], BF16)
    histf = P.sb("histf", [128, 24, 48], F32)
    dtt = P.sb("dtt", [128, 4, 32], F32)
    lat = P.sb("lat", [128, 32], F32)
    cumt = P.sb("cumt", [128, 32], F32)
    ncum = P.sb("ncum", [128, 32], F32)
    wdec = P.sb("wdec", [128, 32], F32)
    edch = P.sb("edch", [128, 32, 16], F32)
    Gm = P.sb("Gm", [128, 4, 128], F32)
    dghs = [P.sb("dgh%d" % i, [128, 128], F32) for i in range(2)]
    segts = [P.sb("segt%d" % i, [128, 128], F32) for i in range(2)]
    ecrs = [P.sb("ecr%d" % i, [128, 128], F32) for i in range(2)]
    rec = P.sb("rec", [128, 4], F32)
    pti = P.sb("pti", [128, NSS], I32)
    pti2 = P.sb("pti2", [128, NSS * 16], I32)
    arena = P.sb("arena", [128, 13312], BF16)
    PS = [P.ps("ps%d" % i, [128, 512], F32) for i in range(7)]
    PSB = P.ps("psb", [128, 1024], BF16)
    PSB2 = PS[5][:, 0:512].bitcast(BF16)

    Kbuf = arena[:, 0:4096]
    Vbuf = arena[:, 4096:4096 + 32 * 66].rearrange("p (t x) -> p t x", x=66)[:, :, 0:65]
    o_tm = arena[:, 6400:6400 + 4096].rearrange("p (t x) -> p t x", x=1024)
    PTs = [arena[:, 10496 + i * 512:10496 + (i + 1) * 512] for i in range(2)]
    vst = arena[:, 0:4 * 16 * 66].rearrange("p (t h x) -> p t h x", t=4, h=16)[:, :, :, 0:65]
    kst = arena[:, 4224:4224 + 2048].rearrange("p (h x) -> p h x", x=512)
    MTs = arena[:, 0:4096].rearrange("p (h l) -> p h l", l=128)
    Css = arena[:, 4096:8192].rearrange("p (h l) -> p h l", l=128)
    xdt = arena[:, 8192:10240]
    xdtw = arena[:, 10240:12288]
    Btm = arena[:, 12288:12800].rearrange("p (g n) -> p g n", n=128)
    Bms = arena[:, 12800:13312].rearrange("p (g n) -> p g n", n=128)
    Kbs = [arena[:, i * 2432:(i + 1) * 2432].rearrange("p (r x) -> p r x", x=304) for i in range(2)]
    KTs = [arena[:, 4864 + i * 384:4864 + (i + 1) * 384].rearrange("p (c k) -> p c k", k=128) for i in range(2)]
    Knew = arena[:, 5632:5632 + 304]
    qabs = arena[:, 5936:5936 + 3 * 2048].rearrange("p (c q) -> p c q", q=2048)
    wukT = arena[:, 12080:12080 + 256]
    olat = arena[:, 12336:12336 + 256]
    olatT = arena[:, 12592:12592 + 256].rearrange("p (c q) -> p c q", q=128)
    PTq = [arena[:, 12848 + i * 128:12848 + (i + 1) * 128] for i in range(2)]
    Gk = [fT[:, 4 * i:4 * i + 4, :].rearrange("p a b -> p (a b)") for i in range(2)]
    GkK = [("fTg", 0), ("fTg", 1)]
    Gp = [tA[i][:, 0:256] for i in range(2)]

    cnt = {}

    def rot(name, lst):
        cnt[name] = cnt.get(name, -1) + 1
        return lst[cnt[name] % len(lst)]

    AR = "arena"

    def barrier(old, new):
        P.op("dve", lambda e: e.memset(rec[:, 0:1], 0.0), r=list(old), w=list(new) + [rec])

    wsched = []
    wstate = {"i": 0, "issued": 0}
    wuniq = {}
    wbf_holder = {}

    def pview(buf, KC, w):
        return buf[:, 0:KC * w].rearrange("p (kc n) -> p kc n", n=w)

    def pkey(Wd, K, c0, w):
        return (repr(Wd), K, c0, w)

    def convert_weights():
        off = 0
        for ent in wsched:
            k = pkey(*ent)
            if k not in wuniq:
                Wd, K, c0, w = ent
                wuniq[k] = (len(wuniq), off, ent)
                off += (K // 128) * w
        Wbf = nc.dram_tensor("Wbf", [128, off], BF16).ap()
        wbf_holder["ap"] = Wbf
        for k, (idx, o, ent) in wuniq.items():
            Wd, K, c0, w = ent
            KC = K // 128
            src = Wd[0:K, c0:c0 + w].rearrange("(kc p) n -> p kc n", p=128)
            dst = Wbf[:, o:o + KC * w].rearrange("p (kc n) -> p kc n", n=w)
            P.dma("pool", lambda e, src=src, dst=dst: e.dma_start(out=dst, in_=src), w=[("wbf", idx)])

    def issue_panel(i):
        ent = wsched[i]
        Wd, K, c0, w = ent
        idx, o, _ = wuniq[pkey(*ent)]
        buf = WB[i % NWB]
        n = (K // 128) * w
        Wbf = wbf_holder["ap"]
        P.dma("sp", lambda e: e.dma_start(out=buf[:, 0:n], in_=Wbf[:, o:o + n]), r=[("wbf", idx)], w=[buf])

    def panel(Wd, K, c0, w):
        i = wstate["i"]
        wstate["i"] += 1
        assert (K // 128) * w <= WBE
        if P.dry:
            wsched.append((Wd, K, c0, w))
            return WB[i % NWB], pview(WB[i % NWB], K // 128, w)
        while wstate["issued"] <= min(i + NWB - 1, len(wsched) - 1):
            issue_panel(wstate["issued"])
            wstate["issued"] += 1
        return WB[i % NWB], pview(WB[i % NWB], K // 128, w)

    def constants():
        P.op("pool", lambda e: e.memset(ident_f[:], 1.0), w=[ident_f])
        P.op("pool", lambda e: e.affine_select(out=ident_f[:], in_=ident_f[:], pattern=[[-1, 128]],
                                               compare_op=ALU.is_equal, fill=0.0, base=0, channel_multiplier=1),
             r=[ident_f], w=[ident_f])
        P.op("dve", lambda e: e.tensor_copy(out=ident_b[:], in_=ident_f[:]), r=[ident_f], w=[ident_b])
        P.op("dve", lambda e: e.memset(ones_b[:], 1.0), w=[ones_b])
        P.op("dve", lambda e: e.memset(ones_f[:], 1.0), w=[ones_f])
        P.op("dve", lambda e: e.memset(eps_t[:], EPS), w=[eps_t])
        P.op("dve", lambda e: e.memset(one_t[:], 1.0), w=[one_t])
        P.op("dve", lambda e: e.memset(npi_t[:], -math.pi), w=[npi_t])
        P.dma("sp", lambda e: e.dma_start(out=invf_t[:], in_=invf), w=[invf_t])
        P.dma("sp", lambda e: e.dma_start(out=hv[:], in_=hvec), w=[hv])
        P.dma("sp", lambda e: e.dma_start(out=pti[:], in_=ptT), w=[pti])
        P.op("act", lambda e: e.activation(out=a_bc[:], in_=hv[:, 32:64], func=AF.Exp), r=[hv], w=[a_bc])
        P.op("dve", lambda e: e.tensor_scalar(out=a_bc[:], in0=a_bc[:], scalar1=-1.0, scalar2=None, op0=ALU.mult),
             r=[a_bc], w=[a_bc])
        P.op("pool", lambda e: e.memset(triU[:], 1.0), w=[triU])
        P.op("pool", lambda e: e.affine_select(out=triU[:], in_=triU[:], pattern=[[1, 128]],
                                               compare_op=ALU.is_ge, fill=0.0, base=0, channel_multiplier=-1),
             r=[triU], w=[triU])
        P.op("pool", lambda e: e.memset(Esel[:], 1.0), w=[Esel])
        P.op("pool", lambda e: e.affine_select(out=Esel[:], in_=Esel[:], pattern=[[1, 128]],
                                               compare_op=ALU.is_ge, fill=0.0, base=0, channel_multiplier=-8),
             r=[Esel], w=[Esel])
        P.op("pool", lambda e: e.affine_select(out=Esel[:], in_=Esel[:], pattern=[[-1, 128]],
                                               compare_op=ALU.is_ge, fill=0.0, base=7, channel_multiplier=8),
             r=[Esel], w=[Esel])
        P.op("pe", lambda e: e.matmul(PS[5][:, 0:128], lhsT=Esel[:], rhs=Esel[:], start=True, stop=True), r=[Esel], w=[PS[5]])
        P.op("dve", lambda e: e.tensor_copy(out=blkS[:], in_=PS[5][:, 0:128]), r=[PS[5]], w=[blkS])
        P.op("dve", lambda e: e.tensor_tensor(out=triS[:], in0=triU[:], in1=blkS[:], op=ALU.mult), r=[triU, blkS], w=[triS])
        P.op("dve", lambda e: e.tensor_copy(out=triU_b[:], in_=triU[:]), r=[triU], w=[triU_b])
        P.op("dve", lambda e: e.tensor_copy(out=triS_b[:], in_=triS[:]), r=[triS], w=[triS_b])
        P.op("pool", lambda e: e.memset(Rtmp[:, 0:32], 1.0), w=[Rtmp])
        P.op("pool", lambda e: e.affine_select(out=Rtmp[:, 0:32], in_=Rtmp[:, 0:32], pattern=[[-1, 32]],
                                               compare_op=ALU.is_equal, fill=0.0, base=16, channel_multiplier=1),
             r=[Rtmp], w=[Rtmp])
        P.op("pool", lambda e: e.memset(Rtmp[:, 32:64], -1.0), r=[Rtmp], w=[Rtmp])
        P.op("pool", lambda e: e.affine_select(out=Rtmp[:, 32:64], in_=Rtmp[:, 32:64], pattern=[[-1, 32]],
                                               compare_op=ALU.is_equal, fill=0.0, base=-16, channel_multiplier=1),
             r=[Rtmp], w=[Rtmp])
        P.op("dve", lambda e: e.memset(Rt[:], 0.0), w=[Rt])
        P.op("dve", lambda e: e.tensor_tensor(out=Rt[0:32, 0:32], in0=Rtmp[:, 0:32], in1=Rtmp[:, 32:64], op=ALU.add),
             r=[Rtmp, Rt], w=[Rt])
        P.dma("sp", lambda e: e.dma_start(out=Rt[64:96, 64:96], in_=Rt[0:32, 0:32]), r=[Rt], w=[Rt])
        for (src, n, dst) in ((vecA, 120, vA), (vecB, 126, vB), (vecC, 32, vC)):
            P.dma("sp", lambda e, src=src, n=n: e.dma_start(out=stg[0:n, 0:128], in_=src), w=[stg])
            P.op("pe", lambda e, n=n: e.transpose(out=PS[5][:, 0:n], in_=stg[0:n, 0:128], identity=ident_f[0:n, 0:n]),
                 r=[stg, ident_f], w=[PS[5]])
            P.op("dve", lambda e, n=n, dst=dst: e.tensor_copy(out=dst[:, 0:n], in_=PS[5][:, 0:n]), r=[PS[5]], w=[dst])
        P.op("dve", lambda e: e.memset(hstate[:], 0.0), w=[hstate])
        P.op("dve", lambda e: e.memset(hstate_b[:], 0.0), w=[hstate_b])
        P.op("dve", lambda e: e.memset(histb[:], 0.0), w=[histb])

    def adaln():
        P.dma("sp", lambda e: e.dma_start(out=stg[0:17, :], in_=cc), w=[stg])
        P.op("act", lambda e: e.activation(out=stg[0:17, :], in_=stg[0:17, :], func=AF.Silu), r=[stg], w=[stg])
        for kc in range(8):
            P.op("pe", lambda e, kc=kc: e.transpose(out=PS[5][:, kc * 17:(kc + 1) * 17], in_=stg[0:17, kc * 128:(kc + 1) * 128],
                                                    identity=ident_f[0:17, 0:17]), r=[stg, ident_f], w=[PS[5]])
        P.op("dve", lambda e: e.tensor_copy(out=cT[:].rearrange("p a b -> p (a b)"), in_=PS[5][:, 0:136]), r=[PS[5]], w=[cT])
        for pn in range(18):
            wb = wada[pn % 2]
            P.dma("sp", lambda e, pn=pn, wb=wb: e.dma_start(
                out=wb[:], in_=w_ada[:, pn * 512:(pn + 1) * 512].rearrange("(kc p) n -> p kc n", p=128)), w=[wb])
            pt = PS[pn % 2]
            for o in range(4):
                for kc in range(8):
                    P.op("pe", lambda e, o=o, kc=kc, wb=wb, pt=pt: e.matmul(
                        pt[:, o * 17:(o + 1) * 17], lhsT=wb[:, kc, o * 128:(o + 1) * 128], rhs=cT[:, kc, :],
                        start=(kc == 0), stop=(kc == 7)), r=[wb, cT], w=[pt])
            for o in range(4):
                oc = pn * 4 + o
                P.op("act", lambda e, o=o, oc=oc, pt=pt: e.activation(
                    out=modT[:, oc, :], in_=pt[:, o * 17:(o + 1) * 17], func=AF.Identity, bias=vA[:, oc:oc + 1], scale=1.0),
                    r=[pt, vA], w=[modT])
        for j in range(3):
            for c in range(8):
                i = j * 8 + c
                P.op("dve", lambda e, j=j, c=c, i=i: e.tensor_scalar(
                    out=At[:, i, :], in0=modT[:, (3 * j + 1) * 8 + c, :], scalar1=1.0, scalar2=vA[:, 72 + i:73 + i],
                    op0=ALU.add, op1=ALU.mult), r=[modT, vA], w=[At])
                P.op("dve", lambda e, j=j, c=c, i=i: e.tensor_scalar(
                    out=Gt[:, i, :], in0=modT[:, (3 * j + 2) * 8 + c, :], scalar1=vA[:, 96 + i:97 + i],
                    scalar2=(1.0 if j == 1 else 0.5), op0=ALU.mult, op1=ALU.mult), r=[modT, vA], w=[Gt])

    def load_xT(src, T):
        for t in range(T // 128):
            P.dma("sp", lambda e, t=t: e.dma_start(out=stg[:], in_=src[t * 128:(t + 1) * 128, :]), w=[stg])
            for half in range(2):
                pt = PS[5]
                for q in range(4):
                    c = half * 4 + q
                    P.op("pe", lambda e, c=c, q=q, pt=pt: e.transpose(
                        out=pt[:, q * 128:(q + 1) * 128], in_=stg[:, c * 128:(c + 1) * 128], identity=ident_f[:]),
                        r=[stg, ident_f], w=[pt])
                P.op("act", lambda e, half=half, t=t, pt=pt: e.activation(
                    out=xT[:, half * 4:half * 4 + 4, t * 128:(t + 1) * 128],
                    in_=pt[:].rearrange("p (q n) -> p q n", q=4), func=AF.Copy), r=[pt], w=[xT])

    def store_xT(dst, T, dname):
        for t in range(T // 128):
            for half in range(2):
                pt = PS[5]
                for q in range(4):
                    c = half * 4 + q
                    P.op("pe", lambda e, c=c, q=q, pt=pt, t=t: e.transpose(
                        out=pt[:, q * 128:(q + 1) * 128], in_=xT[:, c, t * 128:(t + 1) * 128], identity=ident_f[:]),
                        r=[xT, ident_f], w=[pt])
                P.op("act", lambda e, half=half, pt=pt: e.activation(
                    out=stg[:, half * 512:(half + 1) * 512], in_=pt[:], func=AF.Copy), r=[pt], w=[stg])
            P.dma("sp", lambda e, t=t: e.dma_start(out=dst[t * 128:(t + 1) * 128, :], in_=stg[:]), r=[stg], w=[dname])

    def rms_rstd(src, nchunks, T, n_feat, dst, src_res=None):
        for c in range(nchunks):
            s_ = rot("sq", sq)
            P.op("act", lambda e, c=c, s_=s_: e.activation(out=s_[:, 0:T], in_=src[:, c, 0:T], func=AF.Square),
                 r=[src_res or src], w=[s_])
            P.op("pe", lambda e, c=c, s_=s_: e.matmul(PS[4][:, 0:T], lhsT=ones_b[:], rhs=s_[:, 0:T],
                                                     start=(c == 0), stop=(c == nchunks - 1)), r=[s_, ones_b], w=[PS[4]])
        P.op("act", lambda e: e.activation(out=dst[:, 0:T], in_=PS[4][:, 0:T], func=AF.Sqrt, bias=eps_t[:, 0:1],
                                           scale=1.0 / n_feat), r=[PS[4], eps_t], w=[dst])
        P.op("dve", lambda e: e.reciprocal(out=dst[:, 0:T], in_=dst[:, 0:T]), r=[dst], w=[dst])

    def modulate(j, T, segs):
        rms_rstd(xT, 8, T, D, rstd)
        for c in range(8):
            t_ = rot("tA", tA)
            P.op("dve", lambda e, c=c, t_=t_: e.tensor_tensor(out=t_[:, 0:T], in0=xT[:, c, 0:T], in1=rstd[:, 0:T], op=ALU.mult),
                 r=[xT, rstd], w=[t_])
            for (s, c0, n) in segs:
                P.op("act", lambda e, c=c, t_=t_, s=s, c0=c0, n=n: e.activation(
                    out=hT[:, c, c0:c0 + n], in_=t_[:, c0:c0 + n], func=AF.Identity,
                    bias=modT[:, (3 * j) * 8 + c, s:s + 1], scale=At[:, j * 8 + c, s:s + 1]),
                    r=[t_, modT, At], w=[hT])

    def linear(Wd, K, col0, ncols, rhs, rhs_res, T, evac, chunk=128, pw=None):
        KC = K // 128
        if pw is None:
            pw = 512 if KC * 512 <= WBE else (256 if KC * 256 <= WBE else 128)
        done = 0
        idx = 0
        while done < ncols:
            w = min(pw, ncols - done)
            bres, buf = panel(Wd, K, col0 + done, w)
            o = 0
            while o < w:
                m = min(chunk, w - o)
                pt = rot("pa", [PS[0], PS[1]])
                for kc in range(KC):
                    P.op("pe", lambda e, kc=kc, o=o, m=m, pt=pt, buf=buf: e.matmul(
                        pt[0:m, 0:T], lhsT=buf[:, kc, o:o + m], rhs=rhs(kc), start=(kc == 0), stop=(kc == KC - 1)),
                        r=[bres, rhs_res], w=[pt])
                evac(idx, m, pt)
                idx += 1
                o += m
            done += w

    def residual(j, T, segs):
        rms_rstd(fT, 8, T, D, rstd)
        for c in range(8):
            t_ = rot("tA", tA)
            P.op("dve", lambda e, c=c, t_=t_: e.tensor_tensor(out=t_[:, 0:T], in0=fT[:, c, 0:T], in1=rstd[:, 0:T], op=ALU.mult),
                 r=[fT, rstd], w=[t_])
            for (s, c0, n) in segs:
                P.op("dve", lambda e, c=c, t_=t_, s=s, c0=c0, n=n: e.scalar_tensor_tensor(
                    out=xT[:, c, c0:c0 + n], in0=t_[:, c0:c0 + n], scalar=Gt[:, j * 8 + c, s:s + 1],
                    in1=xT[:, c, c0:c0 + n], op0=ALU.mult, op1=ALU.add), r=[t_, Gt, xT], w=[xT])

    def ffn(j, T, segs):
        fi = 0 if j == 0 else 1
        modulate(j, T, segs)
        for pn in range(6):
            c0 = pn * 512
            w = min(512, DFF - c0)
            nch = w // 128
            gres, gbuf = panel(w_gate[fi * D:(fi + 1) * D, :], D, c0, w)
            for o in range(nch):
                pt = rot("pa", [PS[0], PS[1]])
                for kc in range(8):
                    P.op("pe", lambda e, kc=kc, o=o, pt=pt, gbuf=gbuf: e.matmul(
                        pt[:, 0:T], lhsT=gbuf[:, kc, o * 128:(o + 1) * 128], rhs=hT[:, kc, 0:T],
                        start=(kc == 0), stop=(kc == 7)), r=[gres, hT], w=[pt])
                P.op("act", lambda e, o=o, pt=pt, pn=pn: e.activation(
                    out=hid[:, pn * 4 + o, 0:T], in_=pt[:, 0:T], func=AF.Silu), r=[pt], w=[hid])
            ures, ubuf = panel(w_up[fi * D:(fi + 1) * D, :], D, c0, w)
            for o in range(nch):
                pt = rot("pb", [PS[2], PS[3]])
                for kc in range(8):
                    P.op("pe", lambda e, kc=kc, o=o, pt=pt, ubuf=ubuf: e.matmul(
                        pt[:, 0:T], lhsT=ubuf[:, kc, o * 128:(o + 1) * 128], rhs=hT[:, kc, 0:T],
                        start=(kc == 0), stop=(kc == 7)), r=[ures, hT], w=[pt])
                P.op("dve", lambda e, o=o, pt=pt, pn=pn: e.tensor_tensor(
                    out=hid[:, pn * 4 + o, 0:T], in0=hid[:, pn * 4 + o, 0:T], in1=pt[:, 0:T], op=ALU.mult),
                    r=[pt, hid], w=[hid])

        def ev(i, m, pt):
            P.op("act", lambda e: e.activation(out=fT[:, i, 0:T], in_=pt[:, 0:T], func=AF.Copy), r=[pt], w=[fT])
        linear(w_down[fi * DFF:(fi + 1) * DFF, :], DFF, 0, D, lambda kc: hid[:, kc, 0:T], hid, T, ev)
        residual(j, T, segs)

    def rope_tables(pos0, T):
        ang, frac = tA[0], tA[1]
        if pos0 is None:
            P.op("pool", lambda e: e.iota(pos_i[:, 0:T].rearrange("p (a b) -> p a b", b=TS), pattern=[[0, NSS], [1, TS]],
                                          base=PAST, channel_multiplier=0), w=[pos_i])
        else:
            P.op("pool", lambda e: e.iota(pos_i[:, 0:T], pattern=[[1, T]], base=pos0, channel_multiplier=0), w=[pos_i])
        P.op("dve", lambda e: e.tensor_copy(out=ang[:, 0:T], in_=pos_i[:, 0:T]), r=[pos_i], w=[ang])
        P.op("dve", lambda e: e.tensor_scalar(out=ang[:, 0:T], in0=ang[:, 0:T], scalar1=invf_t[:, 0:1], scalar2=None,
                                              op0=ALU.mult), r=[ang, invf_t], w=[ang])
        for (dst, off) in ((sin_t, 0.5), (cos_t, 0.75)):
            P.op("dve", lambda e, dst=dst, off=off: e.tensor_scalar(
                out=dst[:, 0:T], in0=ang[:, 0:T], scalar1=1.0 / (2 * math.pi), scalar2=off, op0=ALU.mult, op1=ALU.add),
                r=[ang], w=[dst])
            P.op("dve", lambda e, dst=dst: e.tensor_copy(out=pos_i[:, 0:T], in_=dst[:, 0:T]), r=[dst], w=[pos_i])
            P.op("dve", lambda e, dst=dst: e.tensor_copy(out=frac[:, 0:T], in_=pos_i[:, 0:T]), r=[pos_i], w=[frac])
            P.op("dve", lambda e, dst=dst: e.tensor_tensor(out=dst[:, 0:T], in0=dst[:, 0:T], in1=frac[:, 0:T], op=ALU.subtract),
                 r=[dst, frac], w=[dst])
            P.op("dve", lambda e, dst=dst: e.scalar_tensor_tensor(out=dst[:, 0:T], in0=dst[:, 0:T], scalar=0.0, in1=dst[:, 0:T],
                                                                  op0=ALU.is_lt, op1=ALU.add), r=[dst], w=[dst])
            P.op("act", lambda e, dst=dst: e.activation(out=dst[:, 0:T], in_=dst[:, 0:T], func=AF.Sin,
                                                        bias=npi_t[:, 0:1], scale=2 * math.pi), r=[dst, npi_t], w=[dst])

    def tm_out(srcs, T, dst, row0, dname, keep=None):
        ncol = sum(m for _, m in srcs)
        for t in range(T // 128):
            o_ = rot("otm", otm)
            c0 = 0
            for (fn, m) in srcs:
                P.op("pe", lambda e, fn=fn, m=m, c0=c0, t=t: e.transpose(
                    out=PS[5][:, c0:c0 + m], in_=fn(t), identity=ident_f[0:m, 0:m]), r=[fn.res, ident_f], w=[PS[5]])
                c0 += m
            P.op("dve", lambda e, o_=o_: e.tensor_copy(out=o_[:, 0:ncol], in_=PS[5][:, 0:ncol]), r=[PS[5]], w=[o_])
            if keep is not None:
                keep(o_)
            P.dma("sp", lambda e, o_=o_, t=t: e.dma_start(out=dst[row0 + t * 128:row0 + (t + 1) * 128, :], in_=o_[:, 0:ncol]),
                  r=[o_], w=[dname])

    def mixer_kvq(T, segs, pos0, grp, row0):
        modulate(1, T, segs)

        def ev_kv(i, m, pt):
            P.op("act", lambda e: e.activation(out=ckv[:, i, 0:T], in_=pt[:, 0:T], func=AF.Copy), r=[pt], w=[ckv])

        def ev_pe(i, m, pt):
            P.op("act", lambda e: e.activation(out=kpe[:, 0:T], in_=pt[0:32, 0:T], func=AF.Copy), r=[pt], w=[kpe])
        linear(w_in, D, O_KV, KVL, lambda kc: hT[:, kc, 0:T], hT, T, ev_kv)
        linear(w_in, D, O_PE, ROPE, lambda kc: hT[:, kc, 0:T], hT, T, ev_pe)
        rms_rstd(ckv, 2, T, KVL, rstd)
        for c in range(2):
            P.op("dve", lambda e, c=c: e.scalar_tensor_tensor(
                out=ckv[:, c, 0:T], in0=ckv[:, c, 0:T], scalar=vB[:, 4 + c:5 + c], in1=rstd[:, 0:T],
                op0=ALU.mult, op1=ALU.mult), r=[ckv, vB, rstd], w=[ckv])
        P.op("act", lambda e: e.activation(out=ckv_b[:, :, 0:T], in_=ckv[:, :, 0:T], func=AF.Copy), r=[ckv], w=[ckv_b])
        rope_tables(pos0, T)
        P.op("dve", lambda e: e.tensor_copy(out=kpe_b[:, 0:T], in_=kpe[:, 0:T]), r=[kpe], w=[kpe_b])
        P.op("pe", lambda e: e.matmul(PS[6][0:32, 0:T], lhsT=Rt[0:32, 0:32], rhs=kpe_b[:, 0:T], start=True, stop=True),
             r=[Rt, kpe_b], w=[PS[6]])
        kpe_r = tA[0][0:32, :]
        P.op("dve", lambda e: e.tensor_tensor(out=kpe_r[:, 0:T], in0=PS[6][0:32, 0:T], in1=sin_t[0:32, 0:T], op=ALU.mult),
             r=[PS[6], sin_t], w=[tA[0]])
        P.op("dve", lambda e: e.tensor_tensor(out=kpe[:, 0:T], in0=kpe[:, 0:T], in1=cos_t[0:32, 0:T], op=ALU.mult),
             r=[kpe, cos_t], w=[kpe])
        P.op("dve", lambda e: e.tensor_tensor(out=kpe[:, 0:T], in0=kpe[:, 0:T], in1=kpe_r[:, 0:T], op=ALU.add),
             r=[kpe, tA[0]], w=[kpe])
        P.op("dve", lambda e: e.tensor_copy(out=kpe_b[:, 0:T], in_=kpe[:, 0:T]), r=[kpe], w=[kpe_b])
        f0 = lambda t: ckv[:, 0, t * 128:(t + 1) * 128]
        f0.res = ckv
        f1 = lambda t: ckv[:, 1, t * 128:(t + 1) * 128]
        f1.res = ckv
        f2 = lambda t: kpe[:, t * 128:(t + 1) * 128]
        f2.res = kpe
        if grp == "s":
            barrier([(AR, "K"), (AR, "V"), (AR, "otm"), (AR, "PT0"), (AR, "PT1"), (AR, "ssd"), (AR, "vst"), (AR, "kst")],
                    [(AR, "Knew")])
        tm_out([(f0, 128), (f1, 128)], T, kvp if grp == "p" else kvs, row0, "kv" + grp,
               keep=(lambda o_: (P.op("act", lambda e: e.activation(out=Knew[:, 0:256], in_=o_[:, 0:256], func=AF.Copy),
                                      r=[o_], w=[(AR, "Knew")]),
                                 P.op("dve", lambda e: e.memset(Knew[:, 256:257], 1.0), w=[(AR, "Knew")]))) if grp == "s" else None)
        tm_out([(f2, 32)], T, pep if grp == "p" else pes, row0, "pe" + grp,
               keep=(lambda o_: P.op("act", lambda e: e.activation(out=Knew[:, 257:289], in_=o_[:, 0:32], func=AF.Copy),
                                     r=[o_], w=[(AR, "Knew")])) if grp == "s" else None)
        def ev_q(i, m, pt):
            P.op("act", lambda e: e.activation(out=fT[:, i, 0:T], in_=pt[:, 0:T], func=AF.Copy), r=[pt], w=[fT])
        linear(w_in, D, O_Q, QL, lambda kc: hT[:, kc, 0:T], hT, T, ev_q)
        rms_rstd(fT, 4, T, QL, rstd)
        for c in range(4):
            P.op("dve", lambda e, c=c: e.scalar_tensor_tensor(
                out=qn[:, c, 0:T], in0=fT[:, c, 0:T], scalar=vB[:, c:c + 1], in1=rstd[:, 0:T],
                op0=ALU.mult, op1=ALU.mult), r=[fT, vB, rstd], w=[big])

        def ev_qh(h, m, pt):
            P.op("act", lambda e: e.activation(out=R1[0:96, h, 0:T], in_=pt[0:96, 0:T], func=AF.Copy, scale=ATT_SCALE),
                 r=[pt], w=[R1])
            P.op("pe", lambda e: e.matmul(PS[6][0:96, 0:T], lhsT=Rt[64:96, 0:96], rhs=R1[64:96, h, 0:T], start=True, stop=True),
                 r=[Rt, R1], w=[PS[6]])
            t_ = rot("tA", tA)
            P.op("dve", lambda e: e.tensor_tensor(out=t_[64:96, 0:T], in0=PS[6][64:96, 0:T], in1=sin_t[64:96, 0:T], op=ALU.mult),
                 r=[PS[6], sin_t], w=[t_])
            P.op("dve", lambda e: e.tensor_tensor(out=R1[64:96, h, 0:T], in0=R1[64:96, h, 0:T], in1=cos_t[64:96, 0:T], op=ALU.mult),
                 r=[R1, cos_t], w=[R1])
            P.op("dve", lambda e: e.tensor_tensor(out=R1[64:96, h, 0:T], in0=R1[64:96, h, 0:T], in1=t_[64:96, 0:T], op=ALU.add),
                 r=[R1, t_], w=[R1])
        linear(w_uq, QL, 0, NH * 96, lambda kc: qn[:, kc, 0:T], big, T, ev_qh, chunk=96, pw=384)

    def kv_gen(b):
        T = 512
        barrier([(AR, "K"), (AR, "V"), (AR, "otm"), (AR, "PT0"), (AR, "PT1"), (AR, "ssd"), (AR, "sa"), (AR, "Knew")],
                [(AR, "vst"), (AR, "kst")])
        P.op("dve", lambda e: e.memset(vst[:, :, :, 64:65], 1.0), w=[(AR, "vst")])
        for pn in range(4):
            bres, buf = panel(w_ukv, KVL, pn * 512, 512)
            for hh in range(4):
                h = pn * 4 + hh
                pt = rot("pa", [PS[0], PS[1]])
                for kc in range(2):
                    P.op("pe", lambda e, kc=kc, hh=hh, pt=pt, buf=buf: e.matmul(
                        pt[0:64, 0:T], lhsT=buf[:, kc, hh * 128:hh * 128 + 64], rhs=ckv_b[:, kc, 0:T],
                        start=(kc == 0), stop=(kc == 1)), r=[bres, ckv_b], w=[pt])
                P.op("act", lambda e, hh=hh, pt=pt: e.activation(out=kst[0:64, hh, :], in_=pt[0:64, 0:T], func=AF.Copy),
                     r=[pt], w=[(AR, "kst")])
                P.dma("sp", lambda e, h=h, hh=hh: e.dma_start(out=Ksc[h, 0:64, b * 512:(b + 1) * 512], in_=kst[0:64, hh, :]),
                      r=[(AR, "kst")], w=["Ksc"])
                P.dma("sp", lambda e, h=h: e.dma_start(out=Ksc[h, 64:96, b * 512:(b + 1) * 512], in_=kpe_b[:, 0:T]),
                      r=[kpe_b], w=["Ksc"])
            for t in range(4):
                pt = rot("pb", [PS[2], PS[3]])
                for kc in range(2):
                    P.op("pe", lambda e, kc=kc, t=t, pt=pt, buf=buf: e.matmul(
                        pt[:, 0:256].rearrange("p (h x) -> p h x", x=64), lhsT=ckv_b[:, kc, t * 128:(t + 1) * 128],
                        rhs=buf[:, kc, :].rearrange("p (h x) -> p h x", x=128)[:, :, 64:128],
                        start=(kc == 0), stop=(kc == 1)), r=[bres, ckv_b], w=[pt])
                P.op("act", lambda e, t=t, pn=pn, pt=pt: e.activation(
                    out=vst[:, t, pn * 4:(pn + 1) * 4, 0:64], in_=pt[:, 0:256].rearrange("p (h x) -> p h x", x=64),
                    func=AF.Copy), r=[pt], w=[(AR, "vst")])
        for h in range(NH):
            P.dma("sp", lambda e, h=h: e.dma_start(out=Vsc[h, :, b * 4:(b + 1) * 4, 0:65], in_=vst[:, :, h, :]),
                  r=[(AR, "vst")], w=["Vsc"])

    def attn_prompt(b):
        T = 512
        nk = (b + 1) * 512
        nkt = nk // 128
        barrier([(AR, "vst"), (AR, "kst")], [(AR, "K"), (AR, "V"), (AR, "otm"), (AR, "PT0"), (AR, "PT1")])
        for h in range(NH):
            P.dma("sp", lambda e, h=h: e.dma_start(out=Kbuf[0:96, 0:nk], in_=Ksc[h, :, 0:nk]), r=["Ksc"], w=[(AR, "K")])
            P.dma("sp", lambda e, h=h: e.dma_start(out=Vbuf[:, 0:nkt, :], in_=Vsc[h, :, 0:nkt, 0:65]), r=["Vsc"], w=[(AR, "V")])
            def QX(kt, h=h):
                j = kt - 4 * b
                q0 = 0 if j < 0 else j * 128
                nq = 512 - q0
                pS = PS[kt % 2]
                P.op("pe", lambda e, kt=kt, q0=q0, nq=nq, pS=pS, h=h: e.matmul(
                    pS[:, 0:nq], lhsT=Kbuf[0:96, kt * 128:(kt + 1) * 128], rhs=R1[0:96, h, q0:512], start=True, stop=True),
                    r=[(AR, "K"), R1], w=[pS])
                pi = kt % 2
                PT = PTs[pi]
                pk = (AR, "PT%d" % pi)
                P.op("act", lambda e, nq=nq, pS=pS, PT=PT: e.activation(out=PT[:, 0:nq], in_=pS[:, 0:nq], func=AF.Exp),
                     r=[pS], w=[pk])
                if j >= 0:
                    P.op("dve", lambda e, PT=PT: e.tensor_tensor(out=PT[:, 0:128], in0=PT[:, 0:128], in1=triU_b[:], op=ALU.mult),
                         r=[pk, triU_b], w=[pk])
                return PT, pk, q0

            def PV(kt, st):
                PT, pk, q0 = st
                for qt in range(q0 // 128, 4):
                    last_kt = 4 * b + qt
                    P.op("pe", lambda e, kt=kt, qt=qt, q0=q0, PT=PT, last_kt=last_kt: e.matmul(
                        PS[6][:, qt * 65:(qt + 1) * 65], lhsT=PT[:, qt * 128 - q0:qt * 128 - q0 + 128], rhs=Vbuf[:, kt, :],
                        start=(kt == 0 and qt == 0), stop=(kt == last_kt), skip_group_check=True),
                        r=[pk, (AR, "V")], w=[PS[6]])
            cur = QX(0)
            for kt in range(nkt):
                nxt = QX(kt + 1) if kt + 1 < nkt else None
                PV(kt, cur)
                cur = nxt
            P.op("dve", lambda e: e.reciprocal(out=rec[:, 0:4], in_=PS[6][:, 0:260].rearrange("p (t x) -> p t x", x=65)[:, :, 64]),
                 r=[PS[6]], w=[rec])
            P.op("dve", lambda e, h=h: e.tensor_tensor(
                out=o_tm[:, :, h * 64:(h + 1) * 64], in0=PS[6][:, 0:260].rearrange("p (t x) -> p t x", x=65)[:, :, 0:64],
                in1=rec[:, 0:4].unsqueeze(2).to_broadcast([128, 4, 64]), op=ALU.mult), r=[PS[6], rec], w=[(AR, "otm")])
        for qt in range(4):
            for c in range(8):
                P.op("pe", lambda e, qt=qt, c=c: e.transpose(out=PSB[:, c * 128:(c + 1) * 128], in_=o_tm[:, qt, c * 128:(c + 1) * 128],
                                                             identity=ident_b[:]), r=[(AR, "otm"), ident_b], w=[PSB])
            P.op("act", lambda e, qt=qt: e.activation(out=big[:, 0:8, qt * 128:(qt + 1) * 128],
                                                      in_=PSB[:].rearrange("p (c n) -> p c n", n=128), func=AF.Copy),
                 r=[PSB], w=[big])

    def attn_sample():
        T = 128
        barrier([(AR, "vst"), (AR, "kst"), (AR, "K"), (AR, "V"), (AR, "otm"), (AR, "PT0"), (AR, "PT1"), (AR, "ssd"), fT],
                [(AR, "sa"), (AR, "Kb0"), (AR, "Kb1"), (AR, "KT0"), (AR, "KT1"), (AR, "PQ0"), (AR, "PQ1"), (AR, "qabs"), ("fTg", 0), ("fTg", 1)])
        for pn in range(4):
            bres, buf = panel(w_ukv, KVL, pn * 512, 512)
            for hh in range(4):
                h = pn * 4 + hh
                for kc in range(2):
                    P.op("pe", lambda e, kc=kc, hh=hh, buf=buf: e.transpose(
                        out=PSB[0:64, kc * 128:(kc + 1) * 128], in_=buf[:, kc, hh * 128:hh * 128 + 64], identity=ident_b[:]),
                        r=[bres, ident_b], w=[PSB])
                P.op("act", lambda e: e.activation(out=wukT[0:64, :], in_=PSB[0:64, 0:256], func=AF.Copy), r=[PSB], w=[(AR, "sa")])
                for kc in range(2):
                    pt = rot("pa", [PS[0], PS[1]])
                    P.op("pe", lambda e, kc=kc, h=h, pt=pt: e.matmul(
                        pt[:, 0:T], lhsT=wukT[0:64, kc * 128:(kc + 1) * 128], rhs=R1[0:64, h, 0:T], start=True, stop=True),
                        r=[(AR, "sa"), R1], w=[pt])
                    P.op("act", lambda e, kc=kc, h=h, pt=pt: e.activation(
                        out=qabs[:, kc, :].rearrange("p (st h) -> p st h", h=NH)[:, :, h], in_=pt[:, 0:T], func=AF.Copy),
                        r=[pt], w=[(AR, "qabs")])
                pt = rot("pa", [PS[0], PS[1]])
                P.op("pe", lambda e, h=h, pt=pt: e.matmul(pt[0:32, 0:T], lhsT=ident_b[64:96, 64:96], rhs=R1[64:96, h, 0:T],
                                                         start=True, stop=True), r=[ident_b, R1], w=[pt])
                P.op("act", lambda e, h=h, pt=pt: e.activation(
                    out=qabs[0:32, 2, :].rearrange("p (st h) -> p st h", h=NH)[:, :, h], in_=pt[0:32, 0:T], func=AF.Copy),
                    r=[pt], w=[(AR, "qabs")])
        gf, pf = tA[0][:, 0:256], tA[1][:, 0:256]
        P.op("pool", lambda e: e.iota(pti2[:].rearrange("p (a b) -> p a b", b=16), pattern=[[0, NSS], [1, 16]], base=0,
                                      channel_multiplier=0), w=[pti2])
        P.op("dve", lambda e: e.tensor_copy(out=gf, in_=pti2[:]), r=[pti2], w=[tA[0]])
        P.op("dve", lambda e: e.tensor_copy(out=pf[:, 0:NSS], in_=pti[:]), r=[pti], w=[tA[1]])
        P.op("dve", lambda e: e.scalar_tensor_tensor(
            out=gf.rearrange("p (a b) -> p a b", b=16), in0=pf[:, 0:NSS].unsqueeze(2).to_broadcast([128, NSS, 16]), scalar=16.0,
            in1=gf.rearrange("p (a b) -> p a b", b=16), op0=ALU.mult, op1=ALU.add), r=[tA[0], tA[1]], w=[tA[0]])
        P.op("dve", lambda e: e.tensor_copy(out=pti2[:], in_=gf), r=[tA[0]], w=[pti2])
        ckv_view = cache_kv.rearrange("n (g x) -> (n g) x", x=2048)
        cpe_view = cache_pe.rearrange("n (g x) -> (n g) x", x=256)
        for s in range(NSS):
            qs = lambda kc, K, s=s: qabs[0:K, kc, s * 128:(s + 1) * 128]
            def gather(g, s=s):
                gi = g % 2
                P.dma("pool", lambda e, g=g, s=s, gi=gi: e.indirect_dma_start(
                    out=Gk[gi][:], out_offset=None, in_=ckv_view,
                    in_offset=bass.IndirectOffsetOnAxis(ap=pti2[:, s * 16 + g:s * 16 + g + 1], axis=0)), r=[pti2], w=[GkK[gi]])
                P.dma("pool", lambda e, g=g, s=s, gi=gi: e.indirect_dma_start(
                    out=Gp[gi][:], out_offset=None, in_=cpe_view,
                    in_offset=bass.IndirectOffsetOnAxis(ap=pti2[:, s * 16 + g:s * 16 + g + 1], axis=0)), r=[pti2], w=[tA[gi]])

            def cast(g):
                gi = g % 2
                Kb = Kbs[gi]
                kbk = (AR, "Kb%d" % gi)
                P.op("act", lambda e, gi=gi, Kb=Kb: e.activation(out=Kb[:, :, 0:256], in_=Gk[gi][:].rearrange("p (r x) -> p r x", x=256),
                                                                 func=AF.Copy), r=[GkK[gi]], w=[kbk])
                P.op("dve", lambda e, gi=gi, Kb=Kb: e.tensor_copy(out=Kb[:, :, 257:289], in_=Gp[gi][:].rearrange("p (r x) -> p r x", x=32)),
                     r=[tA[gi]], w=[kbk])
                P.op("dve", lambda e, Kb=Kb: e.memset(Kb[:, :, 256:257], 1.0), w=[kbk])

            def stA(t):
                g, r_ = t // 8, t % 8
                if r_ == 0:
                    if g + 1 < 16:
                        gather(g + 1)
                    cast(g)
                gi, ti = g % 2, t % 2
                Kb, kbk = Kbs[gi], (AR, "Kb%d" % gi)
                TB, tbk = (PSB, PSB) if ti == 0 else (PSB2, PS[5])
                KT, ktk = KTs[ti], (AR, "KT%d" % ti)
                P.op("pe", lambda e: e.transpose(out=TB[:, 0:128], in_=Kb[:, r_, 0:128], identity=ident_b[:]), r=[kbk, ident_b], w=[tbk])
                P.op("pe", lambda e: e.transpose(out=TB[:, 128:256], in_=Kb[:, r_, 128:256], identity=ident_b[:]), r=[kbk, ident_b], w=[tbk])
                P.op("pe", lambda e: e.transpose(out=TB[0:32, 256:384], in_=Kb[:, r_, 257:289], identity=ident_b[:]), r=[kbk, ident_b], w=[tbk])
                P.op("act", lambda e: e.activation(out=KT[:, 0:2, :], in_=TB[:, 0:256].rearrange("p (c k) -> p c k", k=128), func=AF.Copy),
                     r=[tbk], w=[ktk])
                P.op("dve", lambda e: e.tensor_copy(out=KT[0:32, 2, :], in_=TB[0:32, 256:384]), r=[tbk], w=[ktk])

            def stC(t, qs=qs):
                ti = t % 2
                KT, ktk = KTs[ti], (AR, "KT%d" % ti)
                pS = PS[ti]
                for kc, K in ((0, 128), (1, 128), (2, 32)):
                    P.op("pe", lambda e, kc=kc, K=K: e.matmul(pS[:, 0:128], lhsT=KT[0:K, kc, :], rhs=qs(kc, K), start=(kc == 0), stop=(kc == 2)),
                         r=[ktk, (AR, "qabs")], w=[pS])
                PQ = PTq[ti]
                P.op("act", lambda e: e.activation(out=PQ[:], in_=pS[:, 0:128], func=AF.Exp), r=[pS], w=[(AR, "PQ%d" % ti)])

            def stE(t):
                g, r_, ti = t // 8, t % 8, t % 2
                Kb, kbk = Kbs[g % 2], (AR, "Kb%d" % (g % 2))
                PQ = PTq[ti]
                P.op("pe", lambda e: e.matmul(PS[6][:, 0:257], lhsT=PQ[:], rhs=Kb[:, r_, 0:257], start=(t == 0), stop=False),
                     r=[(AR, "PQ%d" % ti), kbk], w=[PS[6]])
            gather(0)
            NT_ = 128
            for i in range(-2, NT_):
                if i + 2 < NT_:
                    stA(i + 2)
                if 0 <= i + 1 < NT_:
                    stC(i + 1)
                if i >= 0:
                    stE(i)
            pS = PS[0]
            P.op("pe", lambda e, pS=pS, qs=qs: e.matmul(pS[:, 0:128], lhsT=ckv_b[:, 0, 0:128], rhs=qs(0, 128), start=True, stop=False),
                 r=[ckv_b, (AR, "qabs")], w=[pS])
            P.op("pe", lambda e, pS=pS, qs=qs: e.matmul(pS[:, 0:128], lhsT=ckv_b[:, 1, 0:128], rhs=qs(1, 128), start=False, stop=False),
                 r=[ckv_b, (AR, "qabs")], w=[pS])
            P.op("pe", lambda e, pS=pS, qs=qs: e.matmul(pS[:, 0:128], lhsT=kpe_b[0:32, 0:128], rhs=qs(2, 32), start=False, stop=True),
                 r=[kpe_b, (AR, "qabs")], w=[pS])
            pi = 0
            PQ = PTq[pi]
            P.op("act", lambda e, pS=pS, PQ=PQ: e.activation(out=PQ[:], in_=pS[:, 0:128], func=AF.Exp), r=[pS], w=[(AR, "PQ%d" % pi)])
            P.op("dve", lambda e, PQ=PQ, s=s: e.tensor_tensor(
                out=PQ[:].rearrange("p (t h) -> p t h", h=NH), in0=PQ[:].rearrange("p (t h) -> p t h", h=NH),
                in1=triS_b[:, s * 8:(s + 1) * 8].unsqueeze(2).to_broadcast([128, 8, NH]), op=ALU.mult),
                r=[(AR, "PQ%d" % pi), triS_b], w=[(AR, "PQ%d" % pi)])
            P.op("pe", lambda e, PQ=PQ: e.matmul(PS[6][:, 0:257], lhsT=PQ[:], rhs=Knew[:, 0:257], start=False, stop=True),
                 r=[(AR, "PQ%d" % pi), (AR, "Knew")], w=[PS[6]])
            P.op("dve", lambda e: e.reciprocal(out=rec[:, 0:1], in_=PS[6][:, 256:257]), r=[PS[6]], w=[rec])
            P.op("dve", lambda e: e.tensor_scalar(out=olat[:], in0=PS[6][:, 0:256], scalar1=rec[:, 0:1], scalar2=None, op0=ALU.mult),
                 r=[PS[6], rec], w=[(AR, "sa")])
            for kc in range(2):
                P.op("pe", lambda e, kc=kc: e.transpose(out=PSB[:, kc * 128:(kc + 1) * 128], in_=olat[:, kc * 128:(kc + 1) * 128],
                                                        identity=ident_b[:]), r=[(AR, "sa"), ident_b], w=[PSB])
            P.op("act", lambda e: e.activation(out=olatT[:], in_=PSB[:, 0:256].rearrange("p (c q) -> p c q", q=128), func=AF.Copy),
                 r=[PSB], w=[(AR, "sa")])
            for h in range(NH):
                for kc in range(2):
                    c0 = h * 128 + 64 if h % 2 == 0 else h * 128
                    P.op("pe", lambda e, kc=kc, h=h, c0=c0: e.matmul(
                        PS[3][:, h * 8:(h + 1) * 8], lhsT=wuv_res[:, kc, c0:c0 + 128],
                        rhs=olatT[:, kc, :].rearrange("p (t h) -> p t h", h=NH)[:, :, h], start=(kc == 0), stop=(kc == 1),
                        skip_group_check=True), r=[wuv_key, (AR, "sa")], w=[PS[3]])
            pv = PS[3][:, 0:128].rearrange("p (f two t) -> p f two t", two=2, t=8)
            P.op("dve", lambda e, s=s, pv=pv: e.tensor_copy(out=big[0:64, 0:8, s * 8:(s + 1) * 8], in_=pv[0:64, :, 0, :]),
                 r=[PS[3]], w=[big])
            P.op("dve", lambda e, s=s, pv=pv: e.tensor_copy(out=big[64:128, 0:8, s * 8:(s + 1) * 8], in_=pv[64:128, :, 1, :]),
                 r=[PS[3]], w=[big])

    wuv_res = big[:, 16:24, :].rearrange("p a b -> p (a b)").rearrange("p (kc n) -> p kc n", n=2048)
    wuv_key = big

    def load_wuv():
        src = w_ukv.rearrange("(kc p) n -> p kc n", p=128)
        P.dma("pool", lambda e: e.dma_start(out=wuv_res, in_=src), w=[big])

    def ssd(T, grp):
        nseq = 1 if grp == "p" else NSS
        LS = 128 // nseq
        TRI = triU if grp == "p" else triS
        BLK = ones_f if grp == "p" else blkS
        nch = T // 128
        xc = big
        barrier([(AR, "K"), (AR, "V"), (AR, "otm"), (AR, "PT0"), (AR, "PT1"), (AR, "sa"), (AR, "Kb0"), (AR, "Kb1"), (AR, "KT0"), (AR, "KT1"),
                 (AR, "PQ0"), (AR, "PQ1"), (AR, "qabs"), (AR, "Knew"), (AR, "vst"), (AR, "kst")], [(AR, "ssd")])
        SK = (AR, "ssd")
        bres, buf = panel(w_in, D, O_DT, 32)
        for t in range(nch):
            for kc in range(8):
                P.op("pe", lambda e, kc=kc, t=t, buf=buf: e.matmul(PS[2][:, t * 32:(t + 1) * 32], lhsT=hT[:, kc, t * 128:(t + 1) * 128],
                                                                  rhs=buf[:, kc, 0:32], start=(kc == 0), stop=(kc == 7)),
                     r=[bres, hT], w=[PS[2]])
            P.op("dve", lambda e, t=t: e.tensor_tensor(out=dtt[:, t, :], in0=PS[2][:, t * 32:(t + 1) * 32], in1=hv[:, 0:32], op=ALU.add),
                 r=[PS[2], hv], w=[dtt])
        P.op("act", lambda e: e.activation(out=dtt[:, 0:nch, :], in_=dtt[:, 0:nch, :], func=AF.Exp), r=[dtt], w=[dtt])
        P.op("act", lambda e: e.activation(out=dtt[:, 0:nch, :], in_=dtt[:, 0:nch, :], func=AF.Ln, bias=one_t[:, 0:1], scale=1.0),
             r=[dtt, one_t], w=[dtt])
        if grp == "s":
            P.dma("sp", lambda e: e.dma_start(out=stg[0:48, 0:1024], in_=st_conv[:, 0:1024]), w=[stg])
            for part in range(3):
                if part:
                    P.dma("sp", lambda e, part=part: e.dma_start(out=stg[0:48, 0:1024], in_=st_conv[:, part * 1024:(part + 1) * 1024]),
                          w=[stg])
                for q in range(8):
                    P.op("pe", lambda e, q=q: e.transpose(out=PS[5][:, q * 48:(q + 1) * 48], in_=stg[0:48, q * 128:(q + 1) * 128],
                                                          identity=ident_f[0:48, 0:48]), r=[stg, ident_f], w=[PS[5]])
                P.op("dve", lambda e, part=part: e.tensor_copy(out=histf[:, part * 8:(part + 1) * 8, :],
                                                               in_=PS[5][:, 0:384].rearrange("p (q x) -> p q x", x=48)),
                     r=[PS[5]], w=[histf])

        def ev_x(fc, m, pt):
            raw = rot("raw", rawt)
            acc = rot("tA", tA)
            if grp == "p":
                P.op("act", lambda e: e.activation(out=raw[:, 3:3 + T], in_=pt[:, 0:T], func=AF.Copy), r=[pt], w=[raw])
                P.op("dve", lambda e: e.tensor_copy(out=raw[:, 0:3], in_=histb[:, fc, :]), r=[histb], w=[raw])
                P.op("dve", lambda e: e.tensor_copy(out=histf[:, fc, 0:3], in_=pt[:, T - 3:T]), r=[pt], w=[histf])
                P.op("dve", lambda e: e.tensor_copy(out=histb[:, fc, :], in_=raw[:, T:T + 3]), r=[raw], w=[histb])
                win = lambda k: raw[:, k:k + T]
                accv = acc[:, 0:T]
                outv = xc[:, fc, 0:T]
            else:
                r3 = raw[:, 0:NSS * 11].rearrange("p (s x) -> p s x", x=11)
                P.op("act", lambda e: e.activation(out=r3[:, :, 3:11], in_=pt[:, 0:T].rearrange("p (s t) -> p s t", t=TS), func=AF.Copy),
                     r=[pt], w=[raw])
                P.op("dve", lambda e: e.tensor_copy(out=r3[:, :, 0:3], in_=histf[:, fc, :].rearrange("p (s j) -> p s j", j=3)),
                     r=[histf], w=[raw])
                P.op("dve", lambda e: e.tensor_copy(out=histf[:, fc, :].rearrange("p (s j) -> p s j", j=3),
                                                    in_=pt[:, 0:T].rearrange("p (s t) -> p s t", t=TS)[:, :, 5:8]),
                     r=[pt, raw], w=[histf])
                win = lambda k: r3[:, :, k:k + TS]
                accv = acc[:, 0:T].rearrange("p (s t) -> p s t", t=TS)
                outv = xc[:, fc, 0:T].rearrange("p (s t) -> p s t", t=TS)
            P.op("dve", lambda e: e.tensor_scalar(out=accv, in0=win(0), scalar1=vB[:, 6 + fc:7 + fc], scalar2=None, op0=ALU.mult),
                 r=[raw, vB], w=[acc])
            for k in (1, 2, 3):
                P.op("dve", lambda e, k=k: e.scalar_tensor_tensor(out=accv, in0=win(k), scalar=vB[:, 6 + k * 24 + fc:7 + k * 24 + fc],
                                                                  in1=accv, op0=ALU.mult, op1=ALU.add), r=[raw, vB, acc], w=[acc])
            P.op("act", lambda e: e.activation(out=outv, in_=accv, func=AF.Silu, bias=vB[:, 102 + fc:103 + fc], scale=1.0),
                 r=[acc, vB], w=[xc])
        linear(w_in, D, O_XBC, CONV, lambda kc: hT[:, kc, 0:T], hT, T, ev_x)

        for c in range(nch):
            tok = slice(c * 128, (c + 1) * 128)
            P.op("dve", lambda e, c=c: e.tensor_tensor(out=lat[:], in0=dtt[:, c, :], in1=a_bc[:], op=ALU.mult), r=[dtt, a_bc], w=[lat])
            P.op("pe", lambda e: e.matmul(PS[2][:, 0:32], lhsT=TRI[:], rhs=lat[:], start=True, stop=True), r=[TRI, lat], w=[PS[2]])
            P.op("pe", lambda e: e.matmul(PS[2][:, 32:64], lhsT=BLK[:], rhs=lat[:], start=True, stop=True), r=[BLK, lat], w=[PS[2]])
            P.op("dve", lambda e: e.tensor_copy(out=cumt[:], in_=PS[2][:, 0:32]), r=[PS[2]], w=[cumt])
            P.op("dve", lambda e: e.tensor_scalar(out=ncum[:], in0=PS[2][:, 0:32], scalar1=-1.0, scalar2=None, op0=ALU.mult),
                 r=[PS[2]], w=[ncum])
            P.op("dve", lambda e: e.tensor_tensor(out=wdec[:], in0=PS[2][:, 32:64], in1=cumt[:], op=ALU.subtract),
                 r=[PS[2], cumt], w=[wdec])
            P.op("act", lambda e: e.activation(out=wdec[:], in_=wdec[:], func=AF.Exp), r=[wdec], w=[wdec])
            for half in range(2):
                for q in range(8):
                    fc = half * 8 + q
                    P.op("pe", lambda e, fc=fc, q=q, tok=tok: e.transpose(out=PSB[:, q * 128:(q + 1) * 128], in_=xc[:, fc, tok],
                                                                        identity=ident_b[:]), r=[xc, ident_b], w=[PSB])
                P.op("dve", lambda e, half=half, c=c: e.tensor_tensor(
                    out=xdt[:, half * 1024:(half + 1) * 1024].rearrange("p (h x) -> p h x", x=64),
                    in0=PSB[:].rearrange("p (h x) -> p h x", x=64),
                    in1=dtt[:, c, half * 16:(half + 1) * 16].unsqueeze(2).to_broadcast([128, 16, 64]), op=ALU.mult),
                    r=[PSB, dtt], w=[SK])
            P.op("dve", lambda e: e.tensor_tensor(out=xdtw.rearrange("p (h x) -> p h x", x=64), in0=xdt.rearrange("p (h x) -> p h x", x=64),
                                                  in1=wdec[:].unsqueeze(2).to_broadcast([128, 32, 64]), op=ALU.mult),
                 r=[SK, wdec], w=[SK])
            for g in range(4):
                P.op("pe", lambda e, g=g, tok=tok: e.transpose(out=PSB[:, g * 128:(g + 1) * 128], in_=xc[:, 16 + g, tok], identity=ident_b[:]),
                     r=[xc, ident_b], w=[PSB])
            P.op("act", lambda e: e.activation(out=Btm, in_=PSB[:, 0:512].rearrange("p (g n) -> p g n", n=128), func=AF.Copy),
                 r=[PSB], w=[SK])
            for g in range(4):
                P.op("pe", lambda e, g=g, tok=tok: e.matmul(PS[3][:, g * 128:(g + 1) * 128], lhsT=xc[:, 16 + g, tok], rhs=xc[:, 20 + g, tok],
                                                          start=True, stop=True), r=[xc], w=[PS[3]])
            P.op("dve", lambda e: e.tensor_tensor(out=Gm[:], in0=PS[3][:].rearrange("p (g l) -> p g l", l=128),
                                                  in1=TRI[:].unsqueeze(1).to_broadcast([128, 4, 128]), op=ALU.mult),
                 r=[PS[3], TRI], w=[Gm])
            def st1(h):
                dgh, pc = dghs[h % 2], PS[h % 2]
                P.op("dve", lambda e: e.tensor_scalar(out=dgh[:], in0=ident_f[:], scalar1=cumt[:, h:h + 1], scalar2=None, op0=ALU.mult),
                     r=[ident_f, cumt], w=[dgh])
                P.op("pe", lambda e: e.matmul(pc[:, 0:128], lhsT=ones_f[:], rhs=dgh[:], start=True, stop=True),
                     r=[ones_f, dgh], w=[pc])

            def st2(h):
                pc, segt, ecr = PS[h % 2], segts[h % 2], ecrs[h % 2]
                P.op("dve", lambda e: e.tensor_scalar(out=segt[:], in0=pc[:, 0:128], scalar1=ncum[:, h:h + 1], scalar2=0.0,
                                                      op0=ALU.add, op1=ALU.min), r=[pc, ncum], w=[segt])
                P.op("act", lambda e: e.activation(out=ecr[:], in_=pc[:, 0:128], func=AF.Exp), r=[pc], w=[ecr])
                P.op("act", lambda e: e.activation(out=segt[:], in_=segt[:], func=AF.Exp), r=[segt], w=[segt])

            def st3(h, tok=tok):
                g = h // 8
                segt, ecr = segts[h % 2], ecrs[h % 2]
                P.op("dve", lambda e: e.tensor_tensor(out=MTs[:, h, :], in0=segt[:], in1=Gm[:, g, :], op=ALU.mult),
                     r=[segt, Gm], w=[SK])
                P.op("dve", lambda e: e.tensor_tensor(out=Css[:, h, :], in0=ecr[:], in1=xc[:, 20 + g, tok], op=ALU.mult),
                     r=[ecr, xc], w=[SK])
                P.op("dve", lambda e: e.tensor_copy(out=edch[:, h, 0:nseq], in_=ecr[:, LS - 1::LS]), r=[ecr], w=[edch])
            for i in range(-2, SH):
                if i + 2 < SH:
                    st1(i + 2)
                if 0 <= i + 1 < SH:
                    st2(i + 1)
                if i >= 0:
                    st3(i)
            HG = 4 if grp == "p" else 32
            for s in range(nseq):
                cols = slice(s * LS, (s + 1) * LS)
                if grp == "s":
                    for hf in range(2):
                        P.dma("sp", lambda e, s=s, hf=hf: e.dma_start(
                            out=stg[:].rearrange("p (f n) -> p f n", n=128),
                            in_=st_ssm[s, hf * 1024:(hf + 1) * 1024, :].rearrange("(f p) n -> p f n", p=128)), w=[stg])
                        for q4 in range(2):
                            for q in range(4):
                                f = q4 * 4 + q
                                P.op("pe", lambda e, f=f, q=q: e.transpose(out=PS[5][:, q * 128:(q + 1) * 128],
                                                                           in_=stg[:, f * 128:(f + 1) * 128], identity=ident_f[:]),
                                     r=[stg, ident_f], w=[PS[5]])
                            o0 = (hf * 2 + q4) * 512
                            P.op("act", lambda e, o0=o0: e.activation(out=hstate[:, o0:o0 + 512], in_=PS[5][:], func=AF.Copy),
                                 r=[PS[5]], w=[hstate])
                            P.op("dve", lambda e, o0=o0: e.tensor_copy(out=hstate_b[:, o0:o0 + 512], in_=PS[5][:]),
                                 r=[PS[5]], w=[hstate_b])
                for hg in range(SH // HG):
                    py = rot("py", [PS[2], PS[3]])
                    for hh in range(HG):
                        h = hg * HG + hh
                        pr = (h // 2) * 128
                        reg = py[:, hh * LS:(hh + 1) * LS]
                        P.op("pe", lambda e, h=h, pr=pr, reg=reg, cols=cols, hh=hh: e.matmul(
                            reg, lhsT=xdt[:, pr:pr + 128], rhs=MTs[:, h, cols], start=(hh == 0), stop=False, skip_group_check=True),
                            r=[SK], w=[py])
                        P.op("pe", lambda e, h=h, pr=pr, reg=reg, cols=cols: e.matmul(
                            reg, lhsT=hstate_b[:, pr:pr + 128], rhs=Css[:, h, cols], start=False, stop=True, skip_group_check=True),
                            r=[SK, hstate_b], w=[py])
                    nf = HG // 2
                    f0 = hg * nf
                    pv = py[:, 0:HG * LS].rearrange("p (f two l) -> p f two l", two=2, l=LS)
                    for (rows, two) in ((slice(0, 64), 0), (slice(64, 128), 1)):
                        t_ = rot("tA", tA)
                        tv = t_[:, 0:nf * LS].rearrange("p (f l) -> p f l", l=LS)
                        xv = xc[rows, f0:f0 + nf, c * 128 + s * LS:c * 128 + (s + 1) * LS]
                        P.op("dve", lambda e, rows=rows, tv=tv, xv=xv, f0=f0, nf=nf: e.tensor_tensor(
                            out=tv[rows], in0=xv, in1=vC[rows, 16 + f0:16 + f0 + nf].unsqueeze(2).to_broadcast([64, nf, LS]), op=ALU.mult),
                            r=[xc, vC], w=[t_])
                        P.op("dve", lambda e, rows=rows, tv=tv, two=two, pv=pv, f0=f0, nf=nf, c=c, s=s: e.tensor_tensor(
                            out=R1[rows, f0:f0 + nf, c * 128 + s * LS:c * 128 + (s + 1) * LS], in0=tv[rows], in1=pv[rows, :, two, :],
                            op=ALU.add), r=[t_, py], w=[R1])
                if grp == "s":
                    P.op("dve", lambda e, s=s: e.tensor_scalar(out=Bms, in0=Btm, scalar1=blkS[:, s * 8:s * 8 + 1], scalar2=None, op0=ALU.mult),
                         r=[SK, blkS], w=[SK])
                    Bl = Bms
                else:
                    Bl = Btm
                for g in range(4):
                    pu = rot("pa", [PS[0], PS[1]])
                    P.op("pe", lambda e, g=g, pu=pu, Bl=Bl: e.matmul(pu[:, 0:512], lhsT=Bl[:, g, :], rhs=xdtw[:, g * 512:(g + 1) * 512],
                                                                  start=True, stop=True), r=[SK], w=[pu])
                    hsv = hstate[:, g * 512:(g + 1) * 512].rearrange("p (h x) -> p h x", x=64)
                    P.op("dve", lambda e, g=g, hsv=hsv, s=s: e.tensor_tensor(
                        out=hsv, in0=hsv, in1=edch[:, g * 8:(g + 1) * 8, s:s + 1].to_broadcast([128, 8, 64]), op=ALU.mult),
                        r=[hstate, edch], w=[hstate])
                    P.op("dve", lambda e, g=g, pu=pu: e.tensor_tensor(out=hstate[:, g * 512:(g + 1) * 512], in0=hstate[:, g * 512:(g + 1) * 512],
                                                                      in1=pu[:, 0:512], op=ALU.add), r=[hstate, pu], w=[hstate])
                if grp == "s":
                    store_state(ssms[s], "ssms")
                else:
                    P.op("act", lambda e: e.activation(out=hstate_b[:], in_=hstate[:], func=AF.Copy), r=[hstate], w=[hstate_b])

    rawt = sq

    def store_state(dst, dname):
        for q4 in range(4):
            for q in range(4):
                f = q4 * 4 + q
                P.op("pe", lambda e, f=f, q=q: e.transpose(out=PS[5][:, q * 128:(q + 1) * 128], in_=hstate[:, f * 128:(f + 1) * 128],
                                                           identity=ident_f[:]), r=[hstate, ident_f], w=[PS[5]])
            P.op("act", lambda e, q4=q4: e.activation(out=stg[:, 0:512], in_=PS[5][:], func=AF.Copy), r=[PS[5]], w=[stg])
            P.dma("sp", lambda e, q4=q4: e.dma_start(out=dst[q4 * 512:(q4 + 1) * 512, :].rearrange("(f p) n -> p f n", p=128),
                                                     in_=stg[:, 0:512].rearrange("p (f n) -> p f n", n=128)), r=[stg], w=[dname])

    def store_conv(grp):
        n = 3 if grp == "p" else 48
        dst = convp if grp == "p" else convs
        for part in range(6):
            for q in range(4):
                fc = part * 4 + q
                P.op("pe", lambda e, fc=fc, q=q: e.transpose(out=PS[5][0:n, q * 128:(q + 1) * 128], in_=histf[:, fc, 0:n],
                                                             identity=ident_f[:]), r=[histf, ident_f], w=[PS[5]])
            P.op("act", lambda e: e.activation(out=stg[0:n, 0:512], in_=PS[5][0:n, :], func=AF.Copy), r=[PS[5]], w=[stg])
            P.dma("sp", lambda e, part=part: e.dma_start(out=dst[:, part * 512:(part + 1) * 512], in_=stg[0:n, 0:512]),
                  r=[stg], w=["conv" + grp])

    def mix_out(T, segs):
        def ev_oa(i, m, pt):
            P.op("act", lambda e: e.activation(out=fT[:, i, 0:T], in_=pt[:, 0:T], func=AF.Copy), r=[pt], w=[fT])
        linear(w_oa, D, 0, D, lambda kc: big[:, kc, 0:T], big, T, ev_oa)

    def mix_out2(T, segs):
        yT = R1
        def ev_z(i, m, pt):
            t_ = rot("sq", sq)
            P.op("act", lambda e: e.activation(out=t_[:, 0:T], in_=pt[:, 0:T], func=AF.Silu), r=[pt], w=[t_])
            P.op("dve", lambda e: e.tensor_tensor(out=yT[:, i, 0:T], in0=yT[:, i, 0:T], in1=t_[:, 0:T], op=ALU.mult),
                 r=[yT, t_], w=[yT])
        linear(w_in, D, O_Z, SI, lambda kc: hT[:, kc, 0:T], hT, T, ev_z)
        rms_rstd(yT, 16, T, SI, rstd)
        for c in range(16):
            P.op("dve", lambda e, c=c: e.tensor_scalar(out=yT[:, c, 0:T], in0=yT[:, c, 0:T], scalar1=vC[:, c:c + 1], scalar2=None,
                                                       op0=ALU.mult), r=[yT, vC], w=[yT])
        def ev_ga(i, m, pt):
            t_ = rot("tA", tA)
            P.op("act", lambda e: e.activation(out=t_[:, 0:T], in_=pt[:, 0:T], func=AF.Sigmoid), r=[pt], w=[t_])
            P.op("dve", lambda e: e.tensor_tensor(out=fT[:, i, 0:T], in0=fT[:, i, 0:T], in1=t_[:, 0:T], op=ALU.mult),
                 r=[fT, t_], w=[fT])
        linear(w_in, D, O_GA, D, lambda kc: hT[:, kc, 0:T], hT, T, ev_ga)
        gs = big
        def ev_gs(i, m, pt):
            P.op("act", lambda e: e.activation(out=big[:, 8 + i, 0:T], in_=pt[:, 0:T], func=AF.Sigmoid), r=[pt], w=[big])
        linear(w_in, D, O_GS, D, lambda kc: hT[:, kc, 0:T], hT, T, ev_gs)
        def ev_os(i, m, pt):
            t_ = rot("tA", tA)
            P.op("dve", lambda e: e.tensor_tensor(out=t_[:, 0:T], in0=pt[:, 0:T], in1=rstd[:, 0:T], op=ALU.mult), r=[pt, rstd], w=[t_])
            P.op("dve", lambda e: e.tensor_tensor(out=t_[:, 0:T], in0=t_[:, 0:T], in1=big[:, 8 + i, 0:T], op=ALU.mult), r=[t_, big], w=[t_])
            P.op("dve", lambda e: e.tensor_tensor(out=fT[:, i, 0:T], in0=fT[:, i, 0:T], in1=t_[:, 0:T], op=ALU.add), r=[fT, t_], w=[fT])
        linear(w_os, SI, 0, D, lambda kc: yT[:, kc, 0:T], yT, T, ev_os)
        P.op("act", lambda e: e.activation(out=hT[:, :, 0:T], in_=fT[:, :, 0:T], func=AF.Copy), r=[fT], w=[hT])

        def ev_o(i, m, pt):
            P.op("act", lambda e: e.activation(out=fT[:, i, 0:T], in_=pt[:, 0:T], func=AF.Copy), r=[pt], w=[fT])
        linear(w_out, D, 0, D, lambda kc: hT[:, kc, 0:T], hT, T, ev_o)
        residual(1, T, segs)

    def dump_fm(src, nchunk, T, dst, dname, bf=False):
        for t in range(T // 128):
            for c0 in range(0, nchunk, 4):
                for q in range(4):
                    if bf:
                        P.op("pe", lambda e, c0=c0, q=q, t=t: e.transpose(out=PSB[:, q * 128:(q + 1) * 128],
                                                                          in_=src[:, c0 + q, t * 128:(t + 1) * 128], identity=ident_b[:]),
                             r=[src, ident_b], w=[PSB])
                    else:
                        P.op("pe", lambda e, c0=c0, q=q, t=t: e.transpose(out=PS[5][:, q * 128:(q + 1) * 128],
                                                                          in_=src[:, c0 + q, t * 128:(t + 1) * 128], identity=ident_f[:]),
                             r=[src, ident_f], w=[PS[5]])
                P.op("act", lambda e: e.activation(out=stg[:, 0:512], in_=(PSB[:, 0:512] if bf else PS[5][:]), func=AF.Copy),
                     r=[PSB if bf else PS[5]], w=[stg])
                P.dma("sp", lambda e, c0=c0, t=t: e.dma_start(out=dst[t * 128:(t + 1) * 128, c0 * 128:(c0 + 4) * 128], in_=stg[:, 0:512]),
                      r=[stg], w=[dname])

    def program():
        wstate["i"] = 0
        cnt.clear()
        constants()
        if not P.dry:
            convert_weights()
        adaln()
        for b in range(n_pblk):
            T = 512
            segs = [(0, 0, T)]
            load_xT(xp[b * 512:(b + 1) * 512, :], T)
            ffn(0, T, segs)
            mixer_kvq(T, segs, b * 512, "p", b * 512)
            if stage >= 5:
                kv_gen(b)
                attn_prompt(b)
                mix_out(T, segs)
                if dbg and dbg == "p%d" % b:
                    dump_fm(fT, 8, T, dbg1, "dbg1")
            if stage >= 6:
                ssd(T, "p")
                if dbg and dbg == "p%d" % b:
                    dump_fm(R1, 16, T, dbg2, "dbg2", bf=True)
            if stage >= 7:
                mix_out2(T, segs)
                ffn(2, T, segs)
                store_xT(yp[b * 512:(b + 1) * 512, :], T, "yp")
        if n_pblk and stage >= 6:
            store_state(ssmp, "ssmp")
            store_conv("p")
        if do_sample:
            T = NSS * TS
            segs = [(1 + i, i * TS, TS) for i in range(NSS)]
            load_xT(xs, T)
            ffn(0, T, segs)
            mixer_kvq(T, segs, None, "s", 0)
            if stage >= 5:
                load_wuv()
                attn_sample()
                barrier([("fTg", 0), ("fTg", 1)], [fT])
                mix_out(T, segs)
                if dbg == "s":
                    dump_fm(fT, 8, T, dbg1, "dbg1")
            if stage >= 6:
                ssd(T, "s")
                store_conv("s")
                if dbg == "s":
                    dump_fm(R1, 16, T, dbg2, "dbg2", bf=True)
            if stage >= 7:
                mix_out2(T, segs)
                ffn(2, T, segs)
                store_xT(ys, T, "ys")

    P.dry = True
    program()
    P.dry = False
    wstate["issued"] = 0
    program()
    P.emit()
    return nc


def _prep_inputs(inp, need_cache=True):
    f = lambda k: np.ascontiguousarray(np.asarray(inp[k], dtype=np.float32))
    half = ROPE // 2
    inv = (10000.0 ** (-np.arange(half, dtype=np.float32) / half)).astype(np.float32)
    invf = np.tile(inv, 8).reshape(128, 1).astype(np.float32)
    vecA = np.concatenate([f("b_ada").reshape(72, 128), f("g_pre").reshape(24, 128), f("g_post").reshape(24, 128)], 0)
    vecB = np.concatenate([f("g_q_lat").reshape(4, 128), f("g_kv_lat").reshape(2, 128),
                           f("conv_w").reshape(96, 128), f("conv_b").reshape(24, 128)], 0)
    vecC = np.concatenate([f("g_ssm_norm").reshape(16, 128), np.repeat(f("d_skip").reshape(32), 64).reshape(16, 128)], 0)
    hvec = np.tile(np.concatenate([f("dt_bias").reshape(32), f("a_log").reshape(32), f("d_skip").reshape(32)])[None, :], (128, 1))
    shared = dict(
        invf=invf, w_ada=f("w_ada")[0], vecA=np.ascontiguousarray(vecA), vecB=np.ascontiguousarray(vecB),
        vecC=np.ascontiguousarray(vecC), hvec=np.ascontiguousarray(hvec.astype(np.float32)),
        w_gate=f("w_ffn_gate")[0].reshape(2 * D, DFF), w_up=f("w_ffn_up")[0].reshape(2 * D, DFF),
        w_down=f("w_ffn_down")[0].reshape(2 * DFF, D), w_in=f("w_in")[0],
        w_uq=f("w_uq")[0], w_ukv=f("w_ukv")[0], w_oa=f("w_o_attn")[0], w_os=f("w_o_ssm")[0], w_out=f("w_out")[0],
    )
    if need_cache:
        shared["cache_kv"] = np.asarray(inp["cache_kv_latent"], dtype=np.float32).reshape(20480, 128 * KVL)
        shared["cache_pe"] = np.asarray(inp["cache_k_rope"], dtype=np.float32).reshape(20480, 128 * ROPE)
    else:
        shared["cache_kv"] = np.zeros((1, 1), np.float32)
        shared["cache_pe"] = np.zeros((1, 1), np.float32)
    xpa, xsa, cpa, csa = f("x_prompt"), f("x_sample"), f("c_prompt"), f("c_sample")
    pt = np.asarray(inp["page_table"], dtype=np.int32)
    stc, sts = f("state_conv")[0], np.asarray(inp["state_ssm"], dtype=np.float32)[0]
    maps = []
    for c in range(8):
        m = dict(shared)
        sl = slice(c * NSS, (c + 1) * NSS)
        m["xp"] = xpa[c % 4]
        m["xs"] = xsa[sl].reshape(NSS * TS, D)
        m["cc"] = np.concatenate([cpa[c % 4:c % 4 + 1], csa[sl]], 0)
        m["ptT"] = np.ascontiguousarray(pt[sl].T)
        m["st_conv"] = np.ascontiguousarray(stc[sl].reshape(NSS * 3, CONV))
        m["st_ssm"] = np.ascontiguousarray(sts[sl].reshape(NSS, SI, SN))
        maps.append(m)
    return maps


def run(inp, need_cache=True, dev_small=None, trace=False, **bk):
    nc = build(**bk)
    maps = _prep_inputs(inp, need_cache)
    if dev_small is not None:
        ckv_s, cpe_s = dev_small
        for m in maps:
            for k in ("xs", "cc", "st_conv", "st_ssm"):
                m[k] = maps[0][k] if k != "cc" else np.concatenate([m["cc"][0:1], maps[0]["cc"][1:]], 0)
            m["cache_kv"], m["cache_pe"] = ckv_s, cpe_s
            m["ptT"] = np.ascontiguousarray(np.arange(2048, dtype=np.int32).reshape(16, 128).T)
    if trace:
        res = run_bass_kernel_spmd(nc, maps, core_ids=list(range(8)), trace=True)
        print("EXEC_TIME_NS", res.exec_time_ns)
        return res.results
    res = run_bass_kernel_spmd(nc, maps, core_ids=list(range(8)))
    return res.results


def kernel(**inp):
    r = run(inp)
    cat = lambda k, cores: np.stack([r[c][k] for c in cores], 0)
    y_p = cat("yp", range(4))
    y_s = np.concatenate([r[c]["ys"].reshape(NSS, TS, D) for c in range(8)], 0)
    kv_p = cat("kvp", range(4))[None]
    pe_p = cat("pep", range(4))[None]
    cv_p = cat("convp", range(4))[None]
    ss_p = cat("ssmp", range(4)).reshape(1, 4, SH, SHD, SN)
    kv_s = np.concatenate([r[c]["kvs"].reshape(NSS, TS, KVL) for c in range(8)], 0)[None]
    pe_s = np.concatenate([r[c]["pes"].reshape(NSS, TS, ROPE) for c in range(8)], 0)[None]
    cv_s = np.concatenate([r[c]["convs"].reshape(NSS, 3, CONV) for c in range(8)], 0)[None]
    ss_s = np.concatenate([r[c]["ssms"].reshape(NSS, SH, SHD, SN) for c in range(8)], 0)[None]
    return tuple(np.ascontiguousarray(a, dtype=np.float32) for a in (y_p, y_s, kv_p, pe_p, cv_p, ss_p, kv_s, pe_s, cv_s, ss_s))
```

```python
from contextlib import ExitStack
import math
import numpy as np
import concourse.bass as bass
import concourse.mybir as mybir
from concourse.bass_utils import run_bass_kernel_spmd

F32 = mybir.dt.float32
BF16 = mybir.dt.bfloat16
I32 = mybir.dt.int32
ALU = mybir.AluOpType
AF = mybir.ActivationFunctionType

D = 1024
SEQ = 4096
NSS = 16
TS = 8
DFF = 2816
DIN = 8000
QL, KVL, ROPE = 512, 256, 32
NH, NOPE, VH = 16, 64, 64
SI, SHD, SH, SG, SN = 2048, 64, 32, 4, 128
CONV = 3072
EPS = 1e-6
ATT_SCALE = (NOPE + ROPE) ** -0.5
PAST = 16384
O_Q, O_KV, O_PE, O_Z, O_XBC, O_DT, O_GA, O_GS = 0, 512, 768, 800, 2848, 5920, 5952, 6976


class Prog:
    COMPUTE = ("pe", "act", "dve", "pool")
    NDMA = {"sp": 24, "pool": 16, "act": 8}

    def __init__(self, nc):
        self.nc = nc
        self.ops = []
        self.stack = ExitStack()
        self.dry = False

    def sb(self, name, shape, dt):
        return self.stack.enter_context(self.nc.sbuf_tensor(name, list(shape), dt))

    def ps(self, name, shape, dt=F32):
        return self.stack.enter_context(self.nc.psum_tensor(name, list(shape), dt))

    @staticmethod
    def _k(x):
        if isinstance(x, str):
            return x
        if isinstance(x, tuple):
            return tuple(Prog._k(i) if not isinstance(i, int) else i for i in x)
        return "T:" + x.name

    def op(self, eng, fn, r=(), w=()):
        if self.dry:
            return
        r = [self._k(i) for i in r]
        w = [self._k(i) for i in w]
        w += [i for i in r if isinstance(i, str) and i.startswith("T:ps") and i not in w]
        self.ops.append(dict(eng=eng, fn=fn, r=r, w=w, dma=False))

    def dma(self, q, fn, r=(), w=()):
        if self.dry:
            return
        self.ops.append(dict(eng=q, fn=fn, r=[self._k(i) for i in r], w=[self._k(i) for i in w], dma=True))

    def plan(self):
        cnt = {e: 0 for e in self.COMPUTE}
        dcount = {}
        rr = {q: 0 for q in self.NDMA}
        lastw = {}
        readers = {}
        waited = {e: {} for e in ("pe", "act", "dve", "pool", "sp")}
        for op in self.ops:
            E = op["eng"]
            need = {}

            def req(t, kind):
                if t is None:
                    return
                s, v, te = t
                if te is not None and te == E:
                    if E == "pe" or kind == "war":
                        return
                if v > need.get(s, 0):
                    need[s] = v

            for r in op["r"]:
                req(lastw.get(r), "raw")
            for w in op["w"]:
                req(lastw.get(w), "waw")
                for s, (v, te) in readers.get(w, {}).items():
                    req((s, v, te), "war")
            if op["dma"]:
                s = "d_%s_%d" % (E, rr[E])
                rr[E] = (rr[E] + 1) % self.NDMA[E]
                prev = dcount.get(s, 0)
                if prev and 16 * prev > need.get(s, 0):
                    need[s] = 16 * prev
                dcount[s] = prev + 1
                tick = (s, 16 * (prev + 1), None)
                op["inc"] = 16
            else:
                cnt[E] += 1
                tick = (E, cnt[E], E)
                op["inc"] = 1
            wl = []
            for s, v in need.items():
                if v > waited[E].get(s, 0):
                    waited[E][s] = v
                    wl.append((s, v))
            op["waits"] = wl
            op["tick"] = tick
            for r in op["r"]:
                d = readers.setdefault(r, {})
                if tick[1] > d.get(tick[0], (0, None))[0]:
                    d[tick[0]] = (tick[1], tick[2])
            for w in op["w"]:
                lastw[w] = tick
                readers[w] = {}
        self.final = {k: v for k, v in cnt.items() if v}
        self.final.update({s: 16 * c for s, c in dcount.items()})
        self.waited = waited

    def emit(self):
        self.plan()
        nc = self.nc
        sems = {}
        with ExitStack() as st:
            for s in self.final:
                sems[s] = st.enter_context(nc.semaphore("s_" + s))
            block = st.enter_context(nc.Block())

            def replay(name, e):
                for op in self.ops:
                    if op["eng"] != name:
                        continue
                    for s, v in op["waits"]:
                        e.wait_ge(sems[s], v)
                    ins = op["fn"](e)
                    ins.then_inc(sems[op["tick"][0]], op["inc"])
                if name == "sp":
                    for s, v in self.final.items():
                        if v > self.waited["sp"].get(s, 0):
                            e.wait_ge(sems[s], v)

            @block.tensor
            def _(e):
                replay("pe", e)

            @block.scalar
            def _(e):
                replay("act", e)

            @block.vector
            def _(e):
                replay("dve", e)

            @block.gpsimd
            def _(e):
                replay("pool", e)

            @block.sync
            def _(e):
                replay("sp", e)
        self.stack.close()


def build(n_pblk=8, do_sample=True, stage=9, dbg=False, n_phys=20480):
    nc = bass.Bass("TRN2", target_bir_lowering=False)
    P = Prog(nc)

    def din(name, shape, dt=F32):
        return nc.dram_tensor(name, list(shape), dt, kind="ExternalInput").ap()

    def dout(name, shape, dt=F32):
        return nc.dram_tensor(name, list(shape), dt, kind="ExternalOutput").ap()

    xp = din("xp", [SEQ, D])
    xs = din("xs", [NSS * TS, D])
    cc = din("cc", [1 + NSS, D])
    invf = din("invf", [128, 1])
    w_ada = din("w_ada", [D, 9 * D])
    vecA = din("vecA", [120, 128])
    vecB = din("vecB", [126, 128])
    vecC = din("vecC", [32, 128])
    hvec = din("hvec", [128, 96])
    w_gate = din("w_gate", [2 * D, DFF])
    w_up = din("w_up", [2 * D, DFF])
    w_down = din("w_down", [2 * DFF, D])
    w_in = din("w_in", [D, DIN])
    w_uq = din("w_uq", [QL, NH * 96])
    w_ukv = din("w_ukv", [KVL, NH * 128])
    w_oa = din("w_oa", [D, D])
    w_os = din("w_os", [SI, D])
    w_out = din("w_out", [D, D])
    cache_kv = din("cache_kv", [n_phys, 128 * KVL])
    cache_pe = din("cache_pe", [n_phys, 128 * ROPE])
    ptT = din("ptT", [128, NSS], I32)
    st_conv = din("st_conv", [NSS * 3, CONV])
    st_ssm = din("st_ssm", [NSS, SI, SN])
    yp = dout("yp", [SEQ, D])
    ys = dout("ys", [NSS * TS, D])
    kvp = dout("kvp", [SEQ, KVL])
    pep = dout("pep", [SEQ, ROPE])
    convp = dout("convp", [3, CONV])
    ssmp = dout("ssmp", [SI, SN])
    kvs = dout("kvs", [NSS * TS, KVL])
    pes = dout("pes", [NSS * TS, ROPE])
    convs = dout("convs", [NSS * 3, CONV])
    ssms = dout("ssms", [NSS, SI, SN])
    dbg1 = dout("dbg1", [512, D]) if dbg else None
    dbg2 = dout("dbg2", [512, SI]) if dbg else None
    Ksc = nc.dram_tensor("Ksc", [NH, 96, SEQ], BF16).ap()
    Vsc = nc.dram_tensor("Vsc", [NH, 128, SEQ // 128, 66], BF16).ap()

    ident_f = P.sb("ident_f", [128, 128], F32)
    ident_b = P.sb("ident_b", [128, 128], BF16)
    ones_b = P.sb("ones_b", [128, 128], BF16)
    ones_f = P.sb("ones_f", [128, 128], F32)
    triU = P.sb("triU", [128, 128], F32)
    triS = P.sb("triS", [128, 128], F32)
    blkS = P.sb("blkS", [128, 128], F32)
    triU_b = P.sb("triU_b", [128, 128], BF16)
    triS_b = P.sb("triS_b", [128, 128], BF16)
    Esel = P.sb("Esel", [16, 128], F32)
    eps_t = P.sb("eps_t", [128, 1], F32)
    one_t = P.sb("one_t", [128, 1], F32)
    npi_t = P.sb("npi_t", [128, 1], F32)
    invf_t = P.sb("invf_t", [128, 1], F32)
    Rt = P.sb("Rt", [128, 96], BF16)
    Rtmp = P.sb("Rtmp", [32, 64], F32)
    vA = P.sb("vA", [128, 120], F32)
    vB = P.sb("vB", [128, 126], F32)
    vC = P.sb("vC", [128, 32], F32)
    hv = P.sb("hv", [128, 96], F32)
    a_bc = P.sb("a_bc", [128, 32], F32)
    stg = P.sb("stg", [128, 1024], F32)
    cT = P.sb("cT", [128, 8, 17], F32)
    modT = P.sb("modT", [128, 72, 17], F32)
    At = P.sb("At", [128, 24, 17], F32)
    Gt = P.sb("Gt", [128, 24, 17], F32)
    NWB = 4
    WBE = 4096
    WB = [P.sb("wb%d" % i, [128, WBE], BF16) for i in range(NWB)]
    xT = P.sb("xT", [128, 8, 512], F32)
    fT = P.sb("fT", [128, 8, 512], F32)
    wada = [xT, fT]
    hT = P.sb("hT", [128, 8, 512], BF16)
    big = P.sb("big", [128, 24, 512], BF16)
    hid = big
    R1 = P.sb("R1", [128, 16, 512], BF16)
    rstd = P.sb("rstd", [128, 512], F32)
    tA = [P.sb("tA%d" % i, [128, 512], F32) for i in range(2)]
    sq = [P.sb("sq%d" % i, [128, 520], BF16) for i in range(2)]
    ckv = P.sb("ckv", [128, 2, 512], F32)
    ckv_b = P.sb("ckv_b", [128, 2, 512], BF16)
    kpe = P.sb("kpe", [32, 512], F32)
    kpe_b = P.sb("kpe_b", [32, 512], BF16)
    cos_t = P.sb("cos_t", [128, 512], F32)
    sin_t = P.sb("sin_t", [128, 512], F32)
    pos_i = P.sb("pos_i", [128, 512], I32)
    qn = big[:, 20:24, :]
    otm = [P.sb("otm%d" % i, [128, 288], F32) for i in range(2)]
    hstate = P.sb("hstate", [128, SI], F32)
    hstate_b = P.sb("hstate_b", [128, SI], BF16)
    histb = P.sb("histb", [128, 24, 3], BF16)
    histf = P.sb("histf", [128, 24, 48], F32)
    dtt = P.sb("dtt", [128, 4, 32], F32)
    lat = P.sb("lat", [128, 32], F32)
    cumt = P.sb("cumt", [128, 32], F32)
    ncum = P.sb("ncum", [128, 32], F32)
    wdec = P.sb("wdec", [128, 32], F32)
    edch = P.sb("edch", [128, 32, 16], F32)
    Gm = P.sb("Gm", [128, 4, 128], F32)
    dghs = [P.sb("dgh%d" % i, [128, 128], F32) for i in range(2)]
    segts = [P.sb("segt%d" % i, [128, 128], F32) for i in range(2)]
    ecrs = [P.sb("ecr%d" % i, [128, 128], F32) for i in range(2)]
    rec = P.sb("rec", [128, 4], F32)
    pti = P.sb("pti", [128, NSS], I32)
    pti2 = P.sb("pti2", [128, NSS * 16], I32)
    arena = P.sb("arena", [128, 13312], BF16)
    PS = [P.ps("ps%d" % i, [128, 512], F32) for i in range(7)]
    PSB = P.ps("psb", [128, 1024], BF16)
    PSB2 = PS[5][:, 0:512].bitcast(BF16)

    Kbuf = arena[:, 0:4096]
    Vbuf = arena[:, 4096:4096 + 32 * 66].rearrange("p (t x) -> p t x", x=66)[:, :, 0:65]
    o_tm = arena[:, 6400:6400 + 4096].rearrange("p (t x) -> p t x", x=1024)
    PTs = [arena[:, 10496 + i * 512:10496 + (i + 1) * 512] for i in range(3)]
    vst = arena[:, 0:4 * 16 * 66].rearrange("p (t h x) -> p t h x", t=4, h=16)[:, :, :, 0:65]
    kst = arena[:, 4224:4224 + 2048].rearrange("p (h x) -> p h x", x=512)
    MTs = arena[:, 0:4096].rearrange("p (h l) -> p h l", l=128)
    Css = arena[:, 4096:8192].rearrange("p (h l) -> p h l", l=128)
    xdt = arena[:, 8192:10240]
    xdtw = arena[:, 10240:12288]
    Btm = arena[:, 12288:12800].rearrange("p (g n) -> p g n", n=128)
    Bms = arena[:, 12800:13312].rearrange("p (g n) -> p g n", n=128)
    Kbs = [arena[:, i * 2432:(i + 1) * 2432].rearrange("p (r x) -> p r x", x=304) for i in range(2)]
    KTs = [arena[:, 4864 + i * 384:4864 + (i + 1) * 384].rearrange("p (c k) -> p c k", k=128) for i in range(2)]
    Knew = arena[:, 5632:5632 + 304]
    qabs = arena[:, 5936:5936 + 3 * 2048].rearrange("p (c q) -> p c q", q=2048)
    wukT = arena[:, 12080:12080 + 256]
    olat = arena[:, 12336:12336 + 256]
    olatT = arena[:, 12592:12592 + 256].rearrange("p (c q) -> p c q", q=128)
    PTq = [arena[:, 12848 + i * 128:12848 + (i + 1) * 128] for i in range(2)]
    Gk = [fT[:, 4 * i:4 * i + 4, :].rearrange("p a b -> p (a b)") for i in range(2)]
    GkK = [("fTg", 0), ("fTg", 1)]
    Gp = [tA[i][:, 0:256] for i in range(2)]

    cnt = {}

    def rot(name, lst):
        cnt[name] = cnt.get(name, -1) + 1
        return lst[cnt[name] % len(lst)]

    AR = "arena"

    def barrier(old, new):
        P.op("dve", lambda e: e.memset(rec[:, 0:1], 0.0), r=list(old), w=list(new) + [rec])

    wsched = []
    wstate = {"i": 0, "issued": 0}
    wuniq = {}
    wbf_holder = {}

    def pview(buf, KC, w):
        return buf[:, 0:KC * w].rearrange("p (kc n) -> p kc n", n=w)

    def pkey(Wd, K, c0, w):
        return (repr(Wd), K, c0, w)

    def convert_weights():
        off = 0
        for ent in wsched:
            k = pkey(*ent)
            if k not in wuniq:
                Wd, K, c0, w = ent
                wuniq[k] = (len(wuniq), off, ent)
                off += (K // 128) * w
        Wbf = nc.dram_tensor("Wbf", [128, off], BF16).ap()
        wbf_holder["ap"] = Wbf
        for k, (idx, o, ent) in wuniq.items():
            Wd, K, c0, w = ent
            KC = K // 128
            src = Wd[0:K, c0:c0 + w].rearrange("(kc p) n -> p kc n", p=128)
            dst = Wbf[:, o:o + KC * w].rearrange("p (kc n) -> p kc n", n=w)
            P.dma("pool", lambda e, src=src, dst=dst: e.dma_start(out=dst, in_=src), w=[("wbf", idx)])

    def issue_panel(i):
        ent = wsched[i]
        Wd, K, c0, w = ent
        idx, o, _ = wuniq[pkey(*ent)]
        buf = WB[i % NWB]
        n = (K // 128) * w
        Wbf = wbf_holder["ap"]
        P.dma("sp", lambda e: e.dma_start(out=buf[:, 0:n], in_=Wbf[:, o:o + n]), r=[("wbf", idx)], w=[buf])

    def panel(Wd, K, c0, w):
        i = wstate["i"]
        wstate["i"] += 1
        assert (K // 128) * w <= WBE
        if P.dry:
            wsched.append((Wd, K, c0, w))
            return WB[i % NWB], pview(WB[i % NWB], K // 128, w)
        while wstate["issued"] <= min(i + NWB - 1, len(wsched) - 1):
            issue_panel(wstate["issued"])
            wstate["issued"] += 1
        return WB[i % NWB], pview(WB[i % NWB], K // 128, w)

    def constants():
        P.op("pool", lambda e: e.memset(ident_f[:], 1.0), w=[ident_f])
        P.op("pool", lambda e: e.affine_select(out=ident_f[:], in_=ident_f[:], pattern=[[-1, 128]],
                                               compare_op=ALU.is_equal, fill=0.0, base=0, channel_multiplier=1),
             r=[ident_f], w=[ident_f])
        P.op("dve", lambda e: e.tensor_copy(out=ident_b[:], in_=ident_f[:]), r=[ident_f], w=[ident_b])
        P.op("dve", lambda e: e.memset(ones_b[:], 1.0), w=[ones_b])
        P.op("dve", lambda e: e.memset(ones_f[:], 1.0), w=[ones_f])
        P.op("dve", lambda e: e.memset(eps_t[:], EPS), w=[eps_t])
        P.op("dve", lambda e: e.memset(one_t[:], 1.0), w=[one_t])
        P.op("dve", lambda e: e.memset(npi_t[:], -math.pi), w=[npi_t])
        P.dma("sp", lambda e: e.dma_start(out=invf_t[:], in_=invf), w=[invf_t])
        P.dma("sp", lambda e: e.dma_start(out=hv[:], in_=hvec), w=[hv])
        P.dma("sp", lambda e: e.dma_start(out=pti[:], in_=ptT), w=[pti])
        P.op("act", lambda e: e.activation(out=a_bc[:], in_=hv[:, 32:64], func=AF.Exp), r=[hv], w=[a_bc])
        P.op("dve", lambda e: e.tensor_scalar(out=a_bc[:], in0=a_bc[:], scalar1=-1.0, scalar2=None, op0=ALU.mult),
             r=[a_bc], w=[a_bc])
        P.op("pool", lambda e: e.memset(triU[:], 1.0), w=[triU])
        P.op("pool", lambda e: e.affine_select(out=triU[:], in_=triU[:], pattern=[[1, 128]],
                                               compare_op=ALU.is_ge, fill=0.0, base=0, channel_multiplier=-1),
             r=[triU], w=[triU])
        P.op("pool", lambda e: e.memset(Esel[:], 1.0), w=[Esel])
        P.op("pool", lambda e: e.affine_select(out=Esel[:], in_=Esel[:], pattern=[[1, 128]],
                                               compare_op=ALU.is_ge, fill=0.0, base=0, channel_multiplier=-8),
             r=[Esel], w=[Esel])
        P.op("pool", lambda e: e.affine_select(out=Esel[:], in_=Esel[:], pattern=[[-1, 128]],
                                               compare_op=ALU.is_ge, fill=0.0, base=7, channel_multiplier=8),
             r=[Esel], w=[Esel])
        P.op("pe", lambda e: e.matmul(PS[5][:, 0:128], lhsT=Esel[:], rhs=Esel[:], start=True, stop=True), r=[Esel], w=[PS[5]])
        P.op("dve", lambda e: e.tensor_copy(out=blkS[:], in_=PS[5][:, 0:128]), r=[PS[5]], w=[blkS])
        P.op("dve", lambda e: e.tensor_tensor(out=triS[:], in0=triU[:], in1=blkS[:], op=ALU.mult), r=[triU, blkS], w=[triS])
        P.op("dve", lambda e: e.tensor_copy(out=triU_b[:], in_=triU[:]), r=[triU], w=[triU_b])
        P.op("dve", lambda e: e.tensor_copy(out=triS_b[:], in_=triS[:]), r=[triS], w=[triS_b])
        P.op("pool", lambda e: e.memset(Rtmp[:, 0:32], 1.0), w=[Rtmp])
        P.op("pool", lambda e: e.affine_select(out=Rtmp[:, 0:32], in_=Rtmp[:, 0:32], pattern=[[-1, 32]],
                                               compare_op=ALU.is_equal, fill=0.0, base=16, channel_multiplier=1),
             r=[Rtmp], w=[Rtmp])
        P.op("pool", lambda e: e.memset(Rtmp[:, 32:64], -1.0), r=[Rtmp], w=[Rtmp])
        P.op("pool", lambda e: e.affine_select(out=Rtmp[:, 32:64], in_=Rtmp[:, 32:64], pattern=[[-1, 32]],
                                               compare_op=ALU.is_equal, fill=0.0, base=-16, channel_multiplier=1),
             r=[Rtmp], w=[Rtmp])
        P.op("dve", lambda e: e.memset(Rt[:], 0.0), w=[Rt])
        P.op("dve", lambda e: e.tensor_tensor(out=Rt[0:32, 0:32], in0=Rtmp[:, 0:32], in1=Rtmp[:, 32:64], op=ALU.add),
             r=[Rtmp, Rt], w=[Rt])
        P.dma("sp", lambda e: e.dma_start(out=Rt[64:96, 64:96], in_=Rt[0:32, 0:32]), r=[Rt], w=[Rt])
        for (src, n, dst) in ((vecA, 120, vA), (vecB, 126, vB), (vecC, 32, vC)):
            P.dma("sp", lambda e, src=src, n=n: e.dma_start(out=stg[0:n, 0:128], in_=src), w=[stg])
            P.op("pe", lambda e, n=n: e.transpose(out=PS[5][:, 0:n], in_=stg[0:n, 0:128], identity=ident_f[0:n, 0:n]),
                 r=[stg, ident_f], w=[PS[5]])
            P.op("dve", lambda e, n=n, dst=dst: e.tensor_copy(out=dst[:, 0:n], in_=PS[5][:, 0:n]), r=[PS[5]], w=[dst])
        P.op("dve", lambda e: e.memset(hstate[:], 0.0), w=[hstate])
        P.op("dve", lambda e: e.memset(hstate_b[:], 0.0), w=[hstate_b])
        P.op("dve", lambda e: e.memset(histb[:], 0.0), w=[histb])

    def adaln():
        P.dma("sp", lambda e: e.dma_start(out=stg[0:17, :], in_=cc), w=[stg])
        P.op("act", lambda e: e.activation(out=stg[0:17, :], in_=stg[0:17, :], func=AF.Silu), r=[stg], w=[stg])
        for kc in range(8):
            P.op("pe", lambda e, kc=kc: e.transpose(out=PS[5][:, kc * 17:(kc + 1) * 17], in_=stg[0:17, kc * 128:(kc + 1) * 128],
                                                    identity=ident_f[0:17, 0:17]), r=[stg, ident_f], w=[PS[5]])
        P.op("dve", lambda e: e.tensor_copy(out=cT[:].rearrange("p a b -> p (a b)"), in_=PS[5][:, 0:136]), r=[PS[5]], w=[cT])
        for pn in range(18):
            wb = wada[pn % 2]
            P.dma("sp", lambda e, pn=pn, wb=wb: e.dma_start(
                out=wb[:], in_=w_ada[:, pn * 512:(pn + 1) * 512].rearrange("(kc p) n -> p kc n", p=128)), w=[wb])
            pt = PS[pn % 2]
            for o in range(4):
                for kc in range(8):
                    P.op("pe", lambda e, o=o, kc=kc, wb=wb, pt=pt: e.matmul(
                        pt[:, o * 17:(o + 1) * 17], lhsT=wb[:, kc, o * 128:(o + 1) * 128], rhs=cT[:, kc, :],
                        start=(kc == 0), stop=(kc == 7)), r=[wb, cT], w=[pt])
            for o in range(4):
                oc = pn * 4 + o
                P.op("act", lambda e, o=o, oc=oc, pt=pt: e.activation(
                    out=modT[:, oc, :], in_=pt[:, o * 17:(o + 1) * 17], func=AF.Identity, bias=vA[:, oc:oc + 1], scale=1.0),
                    r=[pt, vA], w=[modT])
        for j in range(3):
            for c in range(8):
                i = j * 8 + c
                P.op("dve", lambda e, j=j, c=c, i=i: e.tensor_scalar(
                    out=At[:, i, :], in0=modT[:, (3 * j + 1) * 8 + c, :], scalar1=1.0, scalar2=vA[:, 72 + i:73 + i],
                    op0=ALU.add, op1=ALU.mult), r=[modT, vA], w=[At])
                P.op("dve", lambda e, j=j, c=c, i=i: e.tensor_scalar(
                    out=Gt[:, i, :], in0=modT[:, (3 * j + 2) * 8 + c, :], scalar1=vA[:, 96 + i:97 + i],
                    scalar2=(1.0 if j == 1 else 0.5), op0=ALU.mult, op1=ALU.mult), r=[modT, vA], w=[Gt])

    def load_xT(src, T):
        for t in range(T // 128):
            P.dma("sp", lambda e, t=t: e.dma_start(out=stg[:], in_=src[t * 128:(t + 1) * 128, :]), w=[stg])
            for half in range(2):
                pt = PS[5]
                for q in range(4):
                    c = half * 4 + q
                    P.op("pe", lambda e, c=c, q=q, pt=pt: e.transpose(
                        out=pt[:, q * 128:(q + 1) * 128], in_=stg[:, c * 128:(c + 1) * 128], identity=ident_f[:]),
                        r=[stg, ident_f], w=[pt])
                P.op("act", lambda e, half=half, t=t, pt=pt: e.activation(
                    out=xT[:, half * 4:half * 4 + 4, t * 128:(t + 1) * 128],
                    in_=pt[:].rearrange("p (q n) -> p q n", q=4), func=AF.Copy), r=[pt], w=[xT])

    def store_xT(dst, T, dname):
        for t in range(T // 128):
            for half in range(2):
                pt = PS[5]
                for q in range(4):
                    c = half * 4 + q
                    P.op("pe", lambda e, c=c, q=q, pt=pt, t=t: e.transpose(
                        out=pt[:, q * 128:(q + 1) * 128], in_=xT[:, c, t * 128:(t + 1) * 128], identity=ident_f[:]),
                        r=[xT, ident_f], w=[pt])
                P.op("act", lambda e, half=half, pt=pt: e.activation(
                    out=stg[:, half * 512:(half + 1) * 512], in_=pt[:], func=AF.Copy), r=[pt], w=[stg])
            P.dma("sp", lambda e, t=t: e.dma_start(out=dst[t * 128:(t + 1) * 128, :], in_=stg[:]), r=[stg], w=[dname])

    def rms_rstd(src, nchunks, T, n_feat, dst, src_res=None):
        for c in range(nchunks):
            s_ = rot("sq", sq)
            P.op("act", lambda e, c=c, s_=s_: e.activation(out=s_[:, 0:T], in_=src[:, c, 0:T], func=AF.Square),
                 r=[src_res or src], w=[s_])
            P.op("pe", lambda e, c=c, s_=s_: e.matmul(PS[4][:, 0:T], lhsT=ones_b[:], rhs=s_[:, 0:T],
                                                     start=(c == 0), stop=(c == nchunks - 1)), r=[s_, ones_b], w=[PS[4]])
        P.op("act", lambda e: e.activation(out=dst[:, 0:T], in_=PS[4][:, 0:T], func=AF.Sqrt, bias=eps_t[:, 0:1],
                                           scale=1.0 / n_feat), r=[PS[4], eps_t], w=[dst])
        P.op("dve", lambda e: e.reciprocal(out=dst[:, 0:T], in_=dst[:, 0:T]), r=[dst], w=[dst])

    def modulate(j, T, segs):
        rms_rstd(xT, 8, T, D, rstd)
        for c in range(8):
            t_ = rot("tA", tA)
            P.op("dve", lambda e, c=c, t_=t_: e.tensor_tensor(out=t_[:, 0:T], in0=xT[:, c, 0:T], in1=rstd[:, 0:T], op=ALU.mult),
                 r=[xT, rstd], w=[t_])
            for (s, c0, n) in segs:
                P.op("act", lambda e, c=c, t_=t_, s=s, c0=c0, n=n: e.activation(
                    out=hT[:, c, c0:c0 + n], in_=t_[:, c0:c0 + n], func=AF.Identity,
                    bias=modT[:, (3 * j) * 8 + c, s:s + 1], scale=At[:, j * 8 + c, s:s + 1]),
                    r=[t_, modT, At], w=[hT])

    def linear(Wd, K, col0, ncols, rhs, rhs_res, T, evac, chunk=128, pw=None):
        KC = K // 128
        if pw is None:
            pw = 512 if KC * 512 <= WBE else (256 if KC * 256 <= WBE else 128)
        done = 0
        idx = 0
        while done < ncols:
            w = min(pw, ncols - done)
            bres, buf = panel(Wd, K, col0 + done, w)
            o = 0
            while o < w:
                m = min(chunk, w - o)
                pt = rot("pa", [PS[0], PS[1]])
                for kc in range(KC):
                    P.op("pe", lambda e, kc=kc, o=o, m=m, pt=pt, buf=buf: e.matmul(
                        pt[0:m, 0:T], lhsT=buf[:, kc, o:o + m], rhs=rhs(kc), start=(kc == 0), stop=(kc == KC - 1)),
                        r=[bres, rhs_res], w=[pt])
                evac(idx, m, pt)
                idx += 1
                o += m
            done += w

    def residual(j, T, segs):
        rms_rstd(fT, 8, T, D, rstd)
        for c in range(8):
            t_ = rot("tA", tA)
            P.op("dve", lambda e, c=c, t_=t_: e.tensor_tensor(out=t_[:, 0:T], in0=fT[:, c, 0:T], in1=rstd[:, 0:T], op=ALU.mult),
                 r=[fT, rstd], w=[t_])
            for (s, c0, n) in segs:
                P.op("dve", lambda e, c=c, t_=t_, s=s, c0=c0, n=n: e.scalar_tensor_tensor(
                    out=xT[:, c, c0:c0 + n], in0=t_[:, c0:c0 + n], scalar=Gt[:, j * 8 + c, s:s + 1],
                    in1=xT[:, c, c0:c0 + n], op0=ALU.mult, op1=ALU.add), r=[t_, Gt, xT], w=[xT])

    def ffn(j, T, segs):
        fi = 0 if j == 0 else 1
        modulate(j, T, segs)
        for pn in range(6):
            c0 = pn * 512
            w = min(512, DFF - c0)
            nch = w // 128
            gres, gbuf = panel(w_gate[fi * D:(fi + 1) * D, :], D, c0, w)
            for o in range(nch):
                pt = rot("pa", [PS[0], PS[1]])
                for kc in range(8):
                    P.op("pe", lambda e, kc=kc, o=o, pt=pt, gbuf=gbuf: e.matmul(
                        pt[:, 0:T], lhsT=gbuf[:, kc, o * 128:(o + 1) * 128], rhs=hT[:, kc, 0:T],
                        start=(kc == 0), stop=(kc == 7)), r=[gres, hT], w=[pt])
                P.op("act", lambda e, o=o, pt=pt, pn=pn: e.activation(
                    out=hid[:, pn * 4 + o, 0:T], in_=pt[:, 0:T], func=AF.Silu), r=[pt], w=[hid])
            ures, ubuf = panel(w_up[fi * D:(fi + 1) * D, :], D, c0, w)
            for o in range(nch):
                pt = rot("pb", [PS[2], PS[3]])
                for kc in range(8):
                    P.op("pe", lambda e, kc=kc, o=o, pt=pt, ubuf=ubuf: e.matmul(
                        pt[:, 0:T], lhsT=ubuf[:, kc, o * 128:(o + 1) * 128], rhs=hT[:, kc, 0:T],
                        start=(kc == 0), stop=(kc == 7)), r=[ures, hT], w=[pt])
                P.op("dve", lambda e, o=o, pt=pt, pn=pn: e.tensor_tensor(
                    out=hid[:, pn * 4 + o, 0:T], in0=hid[:, pn * 4 + o, 0:T], in1=pt[:, 0:T], op=ALU.mult),
                    r=[pt, hid], w=[hid])

        def ev(i, m, pt):
            P.op("act", lambda e: e.activation(out=fT[:, i, 0:T], in_=pt[:, 0:T], func=AF.Copy), r=[pt], w=[fT])
        linear(w_down[fi * DFF:(fi + 1) * DFF, :], DFF, 0, D, lambda kc: hid[:, kc, 0:T], hid, T, ev)
        residual(j, T, segs)

    def rope_tables(pos0, T):
        ang, frac = tA[0], tA[1]
        if pos0 is None:
            P.op("pool", lambda e: e.iota(pos_i[:, 0:T].rearrange("p (a b) -> p a b", b=TS), pattern=[[0, NSS], [1, TS]],
                                          base=PAST, channel_multiplier=0), w=[pos_i])
        else:
            P.op("pool", lambda e: e.iota(pos_i[:, 0:T], pattern=[[1, T]], base=pos0, channel_multiplier=0), w=[pos_i])
        P.op("dve", lambda e: e.tensor_copy(out=ang[:, 0:T], in_=pos_i[:, 0:T]), r=[pos_i], w=[ang])
        P.op("dve", lambda e: e.tensor_scalar(out=ang[:, 0:T], in0=ang[:, 0:T], scalar1=invf_t[:, 0:1], scalar2=None,
                                              op0=ALU.mult), r=[ang, invf_t], w=[ang])
        for (dst, off) in ((sin_t, 0.5), (cos_t, 0.75)):
            P.op("dve", lambda e, dst=dst, off=off: e.tensor_scalar(
                out=dst[:, 0:T], in0=ang[:, 0:T], scalar1=1.0 / (2 * math.pi), scalar2=off, op0=ALU.mult, op1=ALU.add),
                r=[ang], w=[dst])
            P.op("dve", lambda e, dst=dst: e.tensor_copy(out=pos_i[:, 0:T], in_=dst[:, 0:T]), r=[dst], w=[pos_i])
            P.op("dve", lambda e, dst=dst: e.tensor_copy(out=frac[:, 0:T], in_=pos_i[:, 0:T]), r=[pos_i], w=[frac])
            P.op("dve", lambda e, dst=dst: e.tensor_tensor(out=dst[:, 0:T], in0=dst[:, 0:T], in1=frac[:, 0:T], op=ALU.subtract),
                 r=[dst, frac], w=[dst])
            P.op("dve", lambda e, dst=dst: e.scalar_tensor_tensor(out=dst[:, 0:T], in0=dst[:, 0:T], scalar=0.0, in1=dst[:, 0:T],
                                                                  op0=ALU.is_lt, op1=ALU.add), r=[dst], w=[dst])
            P.op("act", lambda e, dst=dst: e.activation(out=dst[:, 0:T], in_=dst[:, 0:T], func=AF.Sin,
                                                        bias=npi_t[:, 0:1], scale=2 * math.pi), r=[dst, npi_t], w=[dst])

    def tm_out(srcs, T, dst, row0, dname, keep=None):
        ncol = sum(m for _, m in srcs)
        for t in range(T // 128):
            o_ = rot("otm", otm)
            c0 = 0
            for (fn, m) in srcs:
                P.op("pe", lambda e, fn=fn, m=m, c0=c0, t=t: e.transpose(
                    out=PS[5][:, c0:c0 + m], in_=fn(t), identity=ident_f[0:m, 0:m]), r=[fn.res, ident_f], w=[PS[5]])
                c0 += m
            P.op("dve", lambda e, o_=o_: e.tensor_copy(out=o_[:, 0:ncol], in_=PS[5][:, 0:ncol]), r=[PS[5]], w=[o_])
            if keep is not None:
                keep(o_)
            P.dma("sp", lambda e, o_=o_, t=t: e.dma_start(out=dst[row0 + t * 128:row0 + (t + 1) * 128, :], in_=o_[:, 0:ncol]),
                  r=[o_], w=[dname])

    def mixer_kvq(T, segs, pos0, grp, row0, part="both"):
        if part in ("both", "kv"):
            mixer_kv(T, segs, pos0, grp, row0)
        if part in ("both", "q"):
            mixer_q(T)

    def mixer_kv(T, segs, pos0, grp, row0):
        modulate(1, T, segs)

        def ev_kv(i, m, pt):
            P.op("act", lambda e: e.activation(out=ckv[:, i, 0:T], in_=pt[:, 0:T], func=AF.Copy), r=[pt], w=[ckv])

        def ev_pe(i, m, pt):
            P.op("act", lambda e: e.activation(out=kpe[:, 0:T], in_=pt[0:32, 0:T], func=AF.Copy), r=[pt], w=[kpe])
        linear(w_in, D, O_KV, KVL, lambda kc: hT[:, kc, 0:T], hT, T, ev_kv)
        linear(w_in, D, O_PE, ROPE, lambda kc: hT[:, kc, 0:T], hT, T, ev_pe)
        rms_rstd(ckv, 2, T, KVL, rstd)
        for c in range(2):
            P.op("dve", lambda e, c=c: e.scalar_tensor_tensor(
                out=ckv[:, c, 0:T], in0=ckv[:, c, 0:T], scalar=vB[:, 4 + c:5 + c], in1=rstd[:, 0:T],
                op0=ALU.mult, op1=ALU.mult), r=[ckv, vB, rstd], w=[ckv])
        P.op("act", lambda e: e.activation(out=ckv_b[:, :, 0:T], in_=ckv[:, :, 0:T], func=AF.Copy), r=[ckv], w=[ckv_b])
        rope_tables(pos0, T)
        P.op("dve", lambda e: e.tensor_copy(out=kpe_b[:, 0:T], in_=kpe[:, 0:T]), r=[kpe], w=[kpe_b])
        P.op("pe", lambda e: e.matmul(PS[6][0:32, 0:T], lhsT=Rt[0:32, 0:32], rhs=kpe_b[:, 0:T], start=True, stop=True),
             r=[Rt, kpe_b], w=[PS[6]])
        kpe_r = tA[0][0:32, :]
        P.op("dve", lambda e: e.tensor_tensor(out=kpe_r[:, 0:T], in0=PS[6][0:32, 0:T], in1=sin_t[0:32, 0:T], op=ALU.mult),
             r=[PS[6], sin_t], w=[tA[0]])
        P.op("dve", lambda e: e.tensor_tensor(out=kpe[:, 0:T], in0=kpe[:, 0:T], in1=cos_t[0:32, 0:T], op=ALU.mult),
             r=[kpe, cos_t], w=[kpe])
        P.op("dve", lambda e: e.tensor_tensor(out=kpe[:, 0:T], in0=kpe[:, 0:T], in1=kpe_r[:, 0:T], op=ALU.add),
             r=[kpe, tA[0]], w=[kpe])
        P.op("dve", lambda e: e.tensor_copy(out=kpe_b[:, 0:T], in_=kpe[:, 0:T]), r=[kpe], w=[kpe_b])
        f0 = lambda t: ckv[:, 0, t * 128:(t + 1) * 128]
        f0.res = ckv
        f1 = lambda t: ckv[:, 1, t * 128:(t + 1) * 128]
        f1.res = ckv
        f2 = lambda t: kpe[:, t * 128:(t + 1) * 128]
        f2.res = kpe
        if grp == "s":
            barrier([(AR, "K"), (AR, "V"), (AR, "otm"), (AR, "PT0"), (AR, "PT1"), (AR, "PT2"), (AR, "ssd"), (AR, "vst"), (AR, "kst")],
                    [(AR, "Knew")])
        tm_out([(f0, 128), (f1, 128)], T, kvp if grp == "p" else kvs, row0, "kv" + grp,
               keep=(lambda o_: (P.op("act", lambda e: e.activation(out=Knew[:, 0:256], in_=o_[:, 0:256], func=AF.Copy),
                                      r=[o_], w=[(AR, "Knew")]),
                                 P.op("dve", lambda e: e.memset(Knew[:, 256:257], 1.0), w=[(AR, "Knew")]))) if grp == "s" else None)
        tm_out([(f2, 32)], T, pep if grp == "p" else pes, row0, "pe" + grp,
               keep=(lambda o_: P.op("act", lambda e: e.activation(out=Knew[:, 257:289], in_=o_[:, 0:32], func=AF.Copy),
                                     r=[o_], w=[(AR, "Knew")])) if grp == "s" else None)
    def mixer_q(T):
        def ev_q(i, m, pt):
            P.op("act", lambda e: e.activation(out=fT[:, i, 0:T], in_=pt[:, 0:T], func=AF.Copy), r=[pt], w=[fT])
        linear(w_in, D, O_Q, QL, lambda kc: hT[:, kc, 0:T], hT, T, ev_q)
        rms_rstd(fT, 4, T, QL, rstd)
        for c in range(4):
            P.op("dve", lambda e, c=c: e.scalar_tensor_tensor(
                out=qn[:, c, 0:T], in0=fT[:, c, 0:T], scalar=vB[:, c:c + 1], in1=rstd[:, 0:T],
                op0=ALU.mult, op1=ALU.mult), r=[fT, vB, rstd], w=[big])

        def ev_qh(h, m, pt):
            P.op("act", lambda e: e.activation(out=R1[0:96, h, 0:T], in_=pt[0:96, 0:T], func=AF.Copy, scale=ATT_SCALE),
                 r=[pt], w=[R1])
            P.op("pe", lambda e: e.matmul(PS[6][0:96, 0:T], lhsT=Rt[64:96, 0:96], rhs=R1[64:96, h, 0:T], start=True, stop=True),
                 r=[Rt, R1], w=[PS[6]])
            t_ = rot("tA", tA)
            P.op("dve", lambda e: e.tensor_tensor(out=t_[64:96, 0:T], in0=PS[6][64:96, 0:T], in1=sin_t[64:96, 0:T], op=ALU.mult),
                 r=[PS[6], sin_t], w=[t_])
            P.op("dve", lambda e: e.tensor_tensor(out=R1[64:96, h, 0:T], in0=R1[64:96, h, 0:T], in1=cos_t[64:96, 0:T], op=ALU.mult),
                 r=[R1, cos_t], w=[R1])
            P.op("dve", lambda e: e.tensor_tensor(out=R1[64:96, h, 0:T], in0=R1[64:96, h, 0:T], in1=t_[64:96, 0:T], op=ALU.add),
                 r=[R1, t_], w=[R1])
        linear(w_uq, QL, 0, NH * 96, lambda kc: qn[:, kc, 0:T], big, T, ev_qh, chunk=96, pw=384)

    def kv_gen(b):
        T = 512
        barrier([(AR, "K"), (AR, "V"), (AR, "otm"), (AR, "PT0"), (AR, "PT1"), (AR, "PT2"), (AR, "ssd"), (AR, "sa"), (AR, "Knew")],
                [(AR, "vst"), (AR, "kst")])
        P.op("dve", lambda e: e.memset(vst[:, :, :, 64:65], 1.0), w=[(AR, "vst")])
        for pn in range(4):
            bres, buf = panel(w_ukv, KVL, pn * 512, 512)
            for hh in range(4):
                h = pn * 4 + hh
                pt = rot("pa", [PS[0], PS[1]])
                for kc in range(2):
                    P.op("pe", lambda e, kc=kc, hh=hh, pt=pt, buf=buf: e.matmul(
                        pt[0:64, 0:T], lhsT=buf[:, kc, hh * 128:hh * 128 + 64], rhs=ckv_b[:, kc, 0:T],
                        start=(kc == 0), stop=(kc == 1)), r=[bres, ckv_b], w=[pt])
                P.op("act", lambda e, hh=hh, pt=pt: e.activation(out=kst[0:64, hh, :], in_=pt[0:64, 0:T], func=AF.Copy),
                     r=[pt], w=[(AR, "kst")])
                P.dma("sp", lambda e, h=h, hh=hh: e.dma_start(out=Ksc[h, 0:64, b * 512:(b + 1) * 512], in_=kst[0:64, hh, :]),
                      r=[(AR, "kst")], w=["Ksc"])
                P.dma("sp", lambda e, h=h: e.dma_start(out=Ksc[h, 64:96, b * 512:(b + 1) * 512], in_=kpe_b[:, 0:T]),
                      r=[kpe_b], w=["Ksc"])
            for t in range(4):
                pt = rot("pb", [PS[2], PS[3]])
                for kc in range(2):
                    P.op("pe", lambda e, kc=kc, t=t, pt=pt, buf=buf: e.matmul(
                        pt[:, 0:256].rearrange("p (h x) -> p h x", x=64), lhsT=ckv_b[:, kc, t * 128:(t + 1) * 128],
                        rhs=buf[:, kc, :].rearrange("p (h x) -> p h x", x=128)[:, :, 64:128],
                        start=(kc == 0), stop=(kc == 1)), r=[bres, ckv_b], w=[pt])
                P.op("act", lambda e, t=t, pn=pn, pt=pt: e.activation(
                    out=vst[:, t, pn * 4:(pn + 1) * 4, 0:64], in_=pt[:, 0:256].rearrange("p (h x) -> p h x", x=64),
                    func=AF.Copy), r=[pt], w=[(AR, "vst")])
        for h in range(NH):
            P.dma("sp", lambda e, h=h: e.dma_start(out=Vsc[h, :, b * 4:(b + 1) * 4, 0:65], in_=vst[:, :, h, :]),
                  r=[(AR, "vst")], w=["Vsc"])

    def attn_prompt(b):
        T = 512
        nk = (b + 1) * 512
        nkt = nk // 128
        barrier([(AR, "vst"), (AR, "kst")], [(AR, "K"), (AR, "V"), (AR, "otm"), (AR, "PT0"), (AR, "PT1"), (AR, "PT2")])
        for h in range(NH):
            P.dma("sp", lambda e, h=h: e.dma_start(out=Kbuf[0:96, 0:nk], in_=Ksc[h, :, 0:nk]), r=["Ksc"], w=[(AR, "K")])
            P.dma("sp", lambda e, h=h: e.dma_start(out=Vbuf[:, 0:nkt, :], in_=Vsc[h, :, 0:nkt, 0:65]), r=["Vsc"], w=[(AR, "V")])
            def QX(kt, h=h):
                j = kt - 4 * b
                q0 = 0 if j < 0 else j * 128
                nq = 512 - q0
                pS = PS[kt % 3]
                P.op("pe", lambda e, kt=kt, q0=q0, nq=nq, pS=pS, h=h: e.matmul(
                    pS[:, 0:nq], lhsT=Kbuf[0:96, kt * 128:(kt + 1) * 128], rhs=R1[0:96, h, q0:512], start=True, stop=True),
                    r=[(AR, "K"), R1], w=[pS])
                pi = kt % 3
                PT = PTs[pi]
                pk = (AR, "PT%d" % pi)
                P.op("act", lambda e, nq=nq, pS=pS, PT=PT: e.activation(out=PT[:, 0:nq], in_=pS[:, 0:nq], func=AF.Exp),
                     r=[pS], w=[pk])
                if j >= 0:
                    P.op("dve", lambda e, PT=PT: e.tensor_tensor(out=PT[:, 0:128], in0=PT[:, 0:128], in1=triU_b[:], op=ALU.mult),
                         r=[pk, triU_b], w=[pk])
                return PT, pk, q0

            def PV(kt, st):
                PT, pk, q0 = st
                for qt in range(q0 // 128, 4):
                    last_kt = 4 * b + qt
                    P.op("pe", lambda e, kt=kt, qt=qt, q0=q0, PT=PT, last_kt=last_kt: e.matmul(
                        PS[6][:, qt * 65:(qt + 1) * 65], lhsT=PT[:, qt * 128 - q0:qt * 128 - q0 + 128], rhs=Vbuf[:, kt, :],
                        start=(kt == 0 and qt == 0), stop=(kt == last_kt), skip_group_check=True),
                        r=[pk, (AR, "V")], w=[PS[6]])
            pend = [QX(k) for k in range(min(2, nkt))]
            for kt in range(nkt):
                if kt + 2 < nkt:
                    pend.append(QX(kt + 2))
                PV(kt, pend.pop(0))
            P.op("dve", lambda e: e.reciprocal(out=rec[:, 0:4], in_=PS[6][:, 0:260].rearrange("p (t x) -> p t x", x=65)[:, :, 64]),
                 r=[PS[6]], w=[rec])
            P.op("dve", lambda e, h=h: e.tensor_tensor(
                out=o_tm[:, :, h * 64:(h + 1) * 64], in0=PS[6][:, 0:260].rearrange("p (t x) -> p t x", x=65)[:, :, 0:64],
                in1=rec[:, 0:4].unsqueeze(2).to_broadcast([128, 4, 64]), op=ALU.mult), r=[PS[6], rec], w=[(AR, "otm")])
        for qt in range(4):
            for c in range(8):
                P.op("pe", lambda e, qt=qt, c=c: e.transpose(out=PSB[:, c * 128:(c + 1) * 128], in_=o_tm[:, qt, c * 128:(c + 1) * 128],
                                                             identity=ident_b[:]), r=[(AR, "otm"), ident_b], w=[PSB])
            P.op("act", lambda e, qt=qt: e.activation(out=big[:, 0:8, qt * 128:(qt + 1) * 128],
                                                      in_=PSB[:].rearrange("p (c n) -> p c n", n=128), func=AF.Copy),
                 r=[PSB], w=[big])

    def attn_sample():
        T = 128
        barrier([(AR, "vst"), (AR, "kst"), (AR, "K"), (AR, "V"), (AR, "otm"), (AR, "PT0"), (AR, "PT1"), (AR, "PT2"), (AR, "ssd"), fT],
                [(AR, "sa"), (AR, "Kb0"), (AR, "Kb1"), (AR, "KT0"), (AR, "KT1"), (AR, "PQ0"), (AR, "PQ1"), (AR, "qabs"), ("fTg", 0), ("fTg", 1)])
        for pn in range(4):
            bres, buf = panel(w_ukv, KVL, pn * 512, 512)
            for hh in range(4):
                h = pn * 4 + hh
                for kc in range(2):
                    P.op("pe", lambda e, kc=kc, hh=hh, buf=buf: e.transpose(
                        out=PSB[0:64, kc * 128:(kc + 1) * 128], in_=buf[:, kc, hh * 128:hh * 128 + 64], identity=ident_b[:]),
                        r=[bres, ident_b], w=[PSB])
                P.op("act", lambda e: e.activation(out=wukT[0:64, :], in_=PSB[0:64, 0:256], func=AF.Copy), r=[PSB], w=[(AR, "sa")])
                for kc in range(2):
                    pt = rot("pa", [PS[0], PS[1]])
                    P.op("pe", lambda e, kc=kc, h=h, pt=pt: e.matmul(
                        pt[:, 0:T], lhsT=wukT[0:64, kc * 128:(kc + 1) * 128], rhs=R1[0:64, h, 0:T], start=True, stop=True),
                        r=[(AR, "sa"), R1], w=[pt])
                    P.op("act", lambda e, kc=kc, h=h, pt=pt: e.activation(
                        out=qabs[:, kc, :].rearrange("p (st h) -> p st h", h=NH)[:, :, h], in_=pt[:, 0:T], func=AF.Copy),
                        r=[pt], w=[(AR, "qabs")])
                pt = rot("pa", [PS[0], PS[1]])
                P.op("pe", lambda e, h=h, pt=pt: e.matmul(pt[0:32, 0:T], lhsT=ident_b[64:96, 64:96], rhs=R1[64:96, h, 0:T],
                                                         start=True, stop=True), r=[ident_b, R1], w=[pt])
                P.op("act", lambda e, h=h, pt=pt: e.activation(
                    out=qabs[0:32, 2, :].rearrange("p (st h) -> p st h", h=NH)[:, :, h], in_=pt[0:32, 0:T], func=AF.Copy),
                    r=[pt], w=[(AR, "qabs")])
        gf, pf = tA[0][:, 0:256], tA[1][:, 0:256]
        P.op("pool", lambda e: e.iota(pti2[:].rearrange("p (a b) -> p a b", b=16), pattern=[[0, NSS], [1, 16]], base=0,
                                      channel_multiplier=0), w=[pti2])
        P.op("dve", lambda e: e.tensor_copy(out=gf, in_=pti2[:]), r=[pti2], w=[tA[0]])
        P.op("dve", lambda e: e.tensor_copy(out=pf[:, 0:NSS], in_=pti[:]), r=[pti], w=[tA[1]])
        P.op("dve", lambda e: e.scalar_tensor_tensor(
            out=gf.rearrange("p (a b) -> p a b", b=16), in0=pf[:, 0:NSS].unsqueeze(2).to_broadcast([128, NSS, 16]), scalar=16.0,
            in1=gf.rearrange("p (a b) -> p a b", b=16), op0=ALU.mult, op1=ALU.add), r=[tA[0], tA[1]], w=[tA[0]])
        P.op("dve", lambda e: e.tensor_copy(out=pti2[:], in_=gf), r=[tA[0]], w=[pti2])
        ckv_view = cache_kv.rearrange("n (g x) -> (n g) x", x=2048)
        cpe_view = cache_pe.rearrange("n (g x) -> (n g) x", x=256)
        for s in range(NSS):
            qs = lambda kc, K, s=s: qabs[0:K, kc, s * 128:(s + 1) * 128]
            def gather(g, s=s):
                gi = g % 2
                P.dma("pool", lambda e, g=g, s=s, gi=gi: e.indirect_dma_start(
                    out=Gk[gi][:], out_offset=None, in_=ckv_view,
                    in_offset=bass.IndirectOffsetOnAxis(ap=pti2[:, s * 16 + g:s * 16 + g + 1], axis=0)), r=[pti2], w=[GkK[gi]])
                P.dma("pool", lambda e, g=g, s=s, gi=gi: e.indirect_dma_start(
                    out=Gp[gi][:], out_offset=None, in_=cpe_view,
                    in_offset=bass.IndirectOffsetOnAxis(ap=pti2[:, s * 16 + g:s * 16 + g + 1], axis=0)), r=[pti2], w=[tA[gi]])

            def cast(g):
                gi = g % 2
                Kb = Kbs[gi]
                kbk = (AR, "Kb%d" % gi)
                P.op("act", lambda e, gi=gi, Kb=Kb: e.activation(out=Kb[:, :, 0:256], in_=Gk[gi][:].rearrange("p (r x) -> p r x", x=256),
                                                                 func=AF.Copy), r=[GkK[gi]], w=[kbk])
                P.op("dve", lambda e, gi=gi, Kb=Kb: e.tensor_copy(out=Kb[:, :, 257:289], in_=Gp[gi][:].rearrange("p (r x) -> p r x", x=32)),
                     r=[tA[gi]], w=[kbk])
                P.op("dve", lambda e, Kb=Kb: e.memset(Kb[:, :, 256:257], 1.0), w=[kbk])

            def stA(t):
                g, r_ = t // 8, t % 8
                if r_ == 0:
                    if g + 1 < 16:
                        gather(g + 1)
                    cast(g)
                gi, ti = g % 2, t % 2
                Kb, kbk = Kbs[gi], (AR, "Kb%d" % gi)
                TB, tbk = (PSB, PSB) if ti == 0 else (PSB2, PS[5])
                KT, ktk = KTs[ti], (AR, "KT%d" % ti)
                P.op("pe", lambda e: e.transpose(out=TB[:, 0:128], in_=Kb[:, r_, 0:128], identity=ident_b[:]), r=[kbk, ident_b], w=[tbk])
                P.op("pe", lambda e: e.transpose(out=TB[:, 128:256], in_=Kb[:, r_, 128:256], identity=ident_b[:]), r=[kbk, ident_b], w=[tbk])
                P.op("pe", lambda e: e.transpose(out=TB[0:32, 256:384], in_=Kb[:, r_, 257:289], identity=ident_b[:]), r=[kbk, ident_b], w=[tbk])
                P.op("act", lambda e: e.activation(out=KT[:, 0:2, :], in_=TB[:, 0:256].rearrange("p (c k) -> p c k", k=128), func=AF.Copy),
                     r=[tbk], w=[ktk])
                P.op("dve", lambda e: e.tensor_copy(out=KT[0:32, 2, :], in_=TB[0:32, 256:384]), r=[tbk], w=[ktk])

            def stC(t, qs=qs):
                ti = t % 2
                KT, ktk = KTs[ti], (AR, "KT%d" % ti)
                pS = PS[ti]
                for kc, K in ((0, 128), (1, 128), (2, 32)):
                    P.op("pe", lambda e, kc=kc, K=K: e.matmul(pS[:, 0:128], lhsT=KT[0:K, kc, :], rhs=qs(kc, K), start=(kc == 0), stop=(kc == 2)),
                         r=[ktk, (AR, "qabs")], w=[pS])
                PQ = PTq[ti]
                P.op("act", lambda e: e.activation(out=PQ[:], in_=pS[:, 0:128], func=AF.Exp), r=[pS], w=[(AR, "PQ%d" % ti)])

            def stE(t):
                g, r_, ti = t // 8, t % 8, t % 2
                Kb, kbk = Kbs[g % 2], (AR, "Kb%d" % (g % 2))
                PQ = PTq[ti]
                P.op("pe", lambda e: e.matmul(PS[6][:, 0:257], lhsT=PQ[:], rhs=Kb[:, r_, 0:257], start=(t == 0), stop=False),
                     r=[(AR, "PQ%d" % ti), kbk], w=[PS[6]])
            gather(0)
            NT_ = 128
            for i in range(-2, NT_):
                if i + 2 < NT_:
                    stA(i + 2)
                if 0 <= i + 1 < NT_:
                    stC(i + 1)
                if i >= 0:
                    stE(i)
            pS = PS[0]
            P.op("pe", lambda e, pS=pS, qs=qs: e.matmul(pS[:, 0:128], lhsT=ckv_b[:, 0, 0:128], rhs=qs(0, 128), start=True, stop=False),
                 r=[ckv_b, (AR, "qabs")], w=[pS])
            P.op("pe", lambda e, pS=pS, qs=qs: e.matmul(pS[:, 0:128], lhsT=ckv_b[:, 1, 0:128], rhs=qs(1, 128), start=False, stop=False),
                 r=[ckv_b, (AR, "qabs")], w=[pS])
            P.op("pe", lambda e, pS=pS, qs=qs: e.matmul(pS[:, 0:128], lhsT=kpe_b[0:32, 0:128], rhs=qs(2, 32), start=False, stop=True),
                 r=[kpe_b, (AR, "qabs")], w=[pS])
            pi = 0
            PQ = PTq[pi]
            P.op("act", lambda e, pS=pS, PQ=PQ: e.activation(out=PQ[:], in_=pS[:, 0:128], func=AF.Exp), r=[pS], w=[(AR, "PQ%d" % pi)])
            P.op("dve", lambda e, PQ=PQ, s=s: e.tensor_tensor(
                out=PQ[:].rearrange("p (t h) -> p t h", h=NH), in0=PQ[:].rearrange("p (t h) -> p t h", h=NH),
                in1=triS_b[:, s * 8:(s + 1) * 8].unsqueeze(2).to_broadcast([128, 8, NH]), op=ALU.mult),
                r=[(AR, "PQ%d" % pi), triS_b], w=[(AR, "PQ%d" % pi)])
            P.op("pe", lambda e, PQ=PQ: e.matmul(PS[6][:, 0:257], lhsT=PQ[:], rhs=Knew[:, 0:257], start=False, stop=True),
                 r=[(AR, "PQ%d" % pi), (AR, "Knew")], w=[PS[6]])
            P.op("dve", lambda e: e.reciprocal(out=rec[:, 0:1], in_=PS[6][:, 256:257]), r=[PS[6]], w=[rec])
            P.op("dve", lambda e: e.tensor_scalar(out=olat[:], in0=PS[6][:, 0:256], scalar1=rec[:, 0:1], scalar2=None, op0=ALU.mult),
                 r=[PS[6], rec], w=[(AR, "sa")])
            for kc in range(2):
                P.op("pe", lambda e, kc=kc: e.transpose(out=PSB[:, kc * 128:(kc + 1) * 128], in_=olat[:, kc * 128:(kc + 1) * 128],
                                                        identity=ident_b[:]), r=[(AR, "sa"), ident_b], w=[PSB])
            P.op("act", lambda e: e.activation(out=olatT[:], in_=PSB[:, 0:256].rearrange("p (c q) -> p c q", q=128), func=AF.Copy),
                 r=[PSB], w=[(AR, "sa")])
            for h in range(NH):
                for kc in range(2):
                    c0 = h * 128 + 64 if h % 2 == 0 else h * 128
                    P.op("pe", lambda e, kc=kc, h=h, c0=c0: e.matmul(
                        PS[3][:, h * 8:(h + 1) * 8], lhsT=wuv_res[:, kc, c0:c0 + 128],
                        rhs=olatT[:, kc, :].rearrange("p (t h) -> p t h", h=NH)[:, :, h], start=(kc == 0), stop=(kc == 1),
                        skip_group_check=True), r=[wuv_key, (AR, "sa")], w=[PS[3]])
            pv = PS[3][:, 0:128].rearrange("p (f two t) -> p f two t", two=2, t=8)
            P.op("dve", lambda e, s=s, pv=pv: e.tensor_copy(out=big[0:64, 0:8, s * 8:(s + 1) * 8], in_=pv[0:64, :, 0, :]),
                 r=[PS[3]], w=[big])
            P.op("dve", lambda e, s=s, pv=pv: e.tensor_copy(out=big[64:128, 0:8, s * 8:(s + 1) * 8], in_=pv[64:128, :, 1, :]),
                 r=[PS[3]], w=[big])

    wuv_res = big[:, 16:24, :].rearrange("p a b -> p (a b)").rearrange("p (kc n) -> p kc n", n=2048)
    wuv_key = big

    def load_wuv():
        src = w_ukv.rearrange("(kc p) n -> p kc n", p=128)
        P.dma("pool", lambda e: e.dma_start(out=wuv_res, in_=src), w=[big])

    def ssd(T, grp):
        nseq = 1 if grp == "p" else NSS
        LS = 128 // nseq
        TRI = triU if grp == "p" else triS
        BLK = ones_f if grp == "p" else blkS
        nch = T // 128
        xc = big
        barrier([(AR, "K"), (AR, "V"), (AR, "otm"), (AR, "PT0"), (AR, "PT1"), (AR, "PT2"), (AR, "sa"), (AR, "Kb0"), (AR, "Kb1"), (AR, "KT0"), (AR, "KT1"),
                 (AR, "PQ0"), (AR, "PQ1"), (AR, "qabs"), (AR, "Knew"), (AR, "vst"), (AR, "kst")], [(AR, "ssd")])
        SK = (AR, "ssd")
        bres, buf = panel(w_in, D, O_DT, 32)
        for t in range(nch):
            for kc in range(8):
                P.op("pe", lambda e, kc=kc, t=t, buf=buf: e.matmul(PS[2][:, t * 32:(t + 1) * 32], lhsT=hT[:, kc, t * 128:(t + 1) * 128],
                                                                  rhs=buf[:, kc, 0:32], start=(kc == 0), stop=(kc == 7)),
                     r=[bres, hT], w=[PS[2]])
            P.op("dve", lambda e, t=t: e.tensor_tensor(out=dtt[:, t, :], in0=PS[2][:, t * 32:(t + 1) * 32], in1=hv[:, 0:32], op=ALU.add),
                 r=[PS[2], hv], w=[dtt])
        P.op("act", lambda e: e.activation(out=dtt[:, 0:nch, :], in_=dtt[:, 0:nch, :], func=AF.Exp), r=[dtt], w=[dtt])
        P.op("act", lambda e: e.activation(out=dtt[:, 0:nch, :], in_=dtt[:, 0:nch, :], func=AF.Ln, bias=one_t[:, 0:1], scale=1.0),
             r=[dtt, one_t], w=[dtt])
        if grp == "s":
            P.dma("sp", lambda e: e.dma_start(out=stg[0:48, 0:1024], in_=st_conv[:, 0:1024]), w=[stg])
            for part in range(3):
                if part:
                    P.dma("sp", lambda e, part=part: e.dma_start(out=stg[0:48, 0:1024], in_=st_conv[:, part * 1024:(part + 1) * 1024]),
                          w=[stg])
                for q in range(8):
                    P.op("pe", lambda e, q=q: e.transpose(out=PS[5][:, q * 48:(q + 1) * 48], in_=stg[0:48, q * 128:(q + 1) * 128],
                                                          identity=ident_f[0:48, 0:48]), r=[stg, ident_f], w=[PS[5]])
                P.op("dve", lambda e, part=part: e.tensor_copy(out=histf[:, part * 8:(part + 1) * 8, :],
                                                               in_=PS[5][:, 0:384].rearrange("p (q x) -> p q x", x=48)),
                     r=[PS[5]], w=[histf])

        def ev_x(fc, m, pt):
            raw = rot("raw", rawt)
            acc = rot("tA", tA)
            if grp == "p":
                P.op("act", lambda e: e.activation(out=raw[:, 3:3 + T], in_=pt[:, 0:T], func=AF.Copy), r=[pt], w=[raw])
                P.op("dve", lambda e: e.tensor_copy(out=raw[:, 0:3], in_=histb[:, fc, :]), r=[histb], w=[raw])
                P.op("dve", lambda e: e.tensor_copy(out=histf[:, fc, 0:3], in_=pt[:, T - 3:T]), r=[pt], w=[histf])
                P.op("dve", lambda e: e.tensor_copy(out=histb[:, fc, :], in_=raw[:, T:T + 3]), r=[raw], w=[histb])
                win = lambda k: raw[:, k:k + T]
                accv = acc[:, 0:T]
                outv = xc[:, fc, 0:T]
            else:
                r3 = raw[:, 0:NSS * 11].rearrange("p (s x) -> p s x", x=11)
                P.op("act", lambda e: e.activation(out=r3[:, :, 3:11], in_=pt[:, 0:T].rearrange("p (s t) -> p s t", t=TS), func=AF.Copy),
                     r=[pt], w=[raw])
                P.op("dve", lambda e: e.tensor_copy(out=r3[:, :, 0:3], in_=histf[:, fc, :].rearrange("p (s j) -> p s j", j=3)),
                     r=[histf], w=[raw])
                P.op("dve", lambda e: e.tensor_copy(out=histf[:, fc, :].rearrange("p (s j) -> p s j", j=3),
                                                    in_=pt[:, 0:T].rearrange("p (s t) -> p s t", t=TS)[:, :, 5:8]),
                     r=[pt, raw], w=[histf])
                win = lambda k: r3[:, :, k:k + TS]
                accv = acc[:, 0:T].rearrange("p (s t) -> p s t", t=TS)
                outv = xc[:, fc, 0:T].rearrange("p (s t) -> p s t", t=TS)
            P.op("dve", lambda e: e.tensor_scalar(out=accv, in0=win(0), scalar1=vB[:, 6 + fc:7 + fc], scalar2=None, op0=ALU.mult),
                 r=[raw, vB], w=[acc])
            for k in (1, 2, 3):
                P.op("dve", lambda e, k=k: e.scalar_tensor_tensor(out=accv, in0=win(k), scalar=vB[:, 6 + k * 24 + fc:7 + k * 24 + fc],
                                                                  in1=accv, op0=ALU.mult, op1=ALU.add), r=[raw, vB, acc], w=[acc])
            P.op("act", lambda e: e.activation(out=outv, in_=accv, func=AF.Silu, bias=vB[:, 102 + fc:103 + fc], scale=1.0),
                 r=[acc, vB], w=[xc])
        linear(w_in, D, O_XBC, CONV, lambda kc: hT[:, kc, 0:T], hT, T, ev_x)

        for c in range(nch):
            tok = slice(c * 128, (c + 1) * 128)
            P.op("dve", lambda e, c=c: e.tensor_tensor(out=lat[:], in0=dtt[:, c, :], in1=a_bc[:], op=ALU.mult), r=[dtt, a_bc], w=[lat])
            P.op("pe", lambda e: e.matmul(PS[2][:, 0:32], lhsT=TRI[:], rhs=lat[:], start=True, stop=True), r=[TRI, lat], w=[PS[2]])
            P.op("pe", lambda e: e.matmul(PS[2][:, 32:64], lhsT=BLK[:], rhs=lat[:], start=True, stop=True), r=[BLK, lat], w=[PS[2]])
            P.op("dve", lambda e: e.tensor_copy(out=cumt[:], in_=PS[2][:, 0:32]), r=[PS[2]], w=[cumt])
            P.op("dve", lambda e: e.tensor_scalar(out=ncum[:], in0=PS[2][:, 0:32], scalar1=-1.0, scalar2=None, op0=ALU.mult),
                 r=[PS[2]], w=[ncum])
            P.op("dve", lambda e: e.tensor_tensor(out=wdec[:], in0=PS[2][:, 32:64], in1=cumt[:], op=ALU.subtract),
                 r=[PS[2], cumt], w=[wdec])
            P.op("act", lambda e: e.activation(out=wdec[:], in_=wdec[:], func=AF.Exp), r=[wdec], w=[wdec])
            for half in range(2):
                for q in range(8):
                    fc = half * 8 + q
                    P.op("pe", lambda e, fc=fc, q=q, tok=tok: e.transpose(out=PSB[:, q * 128:(q + 1) * 128], in_=xc[:, fc, tok],
                                                                        identity=ident_b[:]), r=[xc, ident_b], w=[PSB])
                P.op("dve", lambda e, half=half, c=c: e.tensor_tensor(
                    out=xdt[:, half * 1024:(half + 1) * 1024].rearrange("p (h x) -> p h x", x=64),
                    in0=PSB[:].rearrange("p (h x) -> p h x", x=64),
                    in1=dtt[:, c, half * 16:(half + 1) * 16].unsqueeze(2).to_broadcast([128, 16, 64]), op=ALU.mult),
                    r=[PSB, dtt], w=[SK])
            P.op("dve", lambda e: e.tensor_tensor(out=xdtw.rearrange("p (h x) -> p h x", x=64), in0=xdt.rearrange("p (h x) -> p h x", x=64),
                                                  in1=wdec[:].unsqueeze(2).to_broadcast([128, 32, 64]), op=ALU.mult),
                 r=[SK, wdec], w=[SK])
            for g in range(4):
                P.op("pe", lambda e, g=g, tok=tok: e.transpose(out=PSB[:, g * 128:(g + 1) * 128], in_=xc[:, 16 + g, tok], identity=ident_b[:]),
                     r=[xc, ident_b], w=[PSB])
            P.op("act", lambda e: e.activation(out=Btm, in_=PSB[:, 0:512].rearrange("p (g n) -> p g n", n=128), func=AF.Copy),
                 r=[PSB], w=[SK])
            for g in range(4):
                P.op("pe", lambda e, g=g, tok=tok: e.matmul(PS[3][:, g * 128:(g + 1) * 128], lhsT=xc[:, 16 + g, tok], rhs=xc[:, 20 + g, tok],
                                                          start=True, stop=True), r=[xc], w=[PS[3]])
            P.op("dve", lambda e: e.tensor_tensor(out=Gm[:], in0=PS[3][:].rearrange("p (g l) -> p g l", l=128),
                                                  in1=TRI[:].unsqueeze(1).to_broadcast([128, 4, 128]), op=ALU.mult),
                 r=[PS[3], TRI], w=[Gm])
            def st1(h):
                dgh, pc = dghs[h % 2], PS[h % 2]
                P.op("dve", lambda e: e.tensor_scalar(out=dgh[:], in0=ident_f[:], scalar1=cumt[:, h:h + 1], scalar2=None, op0=ALU.mult),
                     r=[ident_f, cumt], w=[dgh])
                P.op("pe", lambda e: e.matmul(pc[:, 0:128], lhsT=ones_f[:], rhs=dgh[:], start=True, stop=True),
                     r=[ones_f, dgh], w=[pc])

            def st2(h):
                pc, segt, ecr = PS[h % 2], segts[h % 2], ecrs[h % 2]
                P.op("dve", lambda e: e.tensor_scalar(out=segt[:], in0=pc[:, 0:128], scalar1=ncum[:, h:h + 1], scalar2=0.0,
                                                      op0=ALU.add, op1=ALU.min), r=[pc, ncum], w=[segt])
                P.op("act", lambda e: e.activation(out=ecr[:], in_=pc[:, 0:128], func=AF.Exp), r=[pc], w=[ecr])
                P.op("act", lambda e: e.activation(out=segt[:], in_=segt[:], func=AF.Exp), r=[segt], w=[segt])

            def st3(h, tok=tok):
                g = h // 8
                segt, ecr = segts[h % 2], ecrs[h % 2]
                P.op("pool", lambda e: e.tensor_tensor(out=MTs[:, h, :], in0=segt[:], in1=Gm[:, g, :], op=ALU.mult),
                     r=[segt, Gm], w=[SK])
                P.op("pool", lambda e: e.tensor_tensor(out=Css[:, h, :], in0=ecr[:], in1=xc[:, 20 + g, tok], op=ALU.mult),
                     r=[ecr, xc], w=[SK])
                P.op("dve", lambda e: e.tensor_copy(out=edch[:, h, 0:nseq], in_=ecr[:, LS - 1::LS]), r=[ecr], w=[edch])
            for i in range(-2, SH):
                if i + 2 < SH:
                    st1(i + 2)
                if 0 <= i + 1 < SH:
                    st2(i + 1)
                if i >= 0:
                    st3(i)
            HG = 4 if grp == "p" else 32
            for s in range(nseq):
                cols = slice(s * LS, (s + 1) * LS)
                if grp == "s":
                    for hf in range(2):
                        P.dma("sp", lambda e, s=s, hf=hf: e.dma_start(
                            out=stg[:].rearrange("p (f n) -> p f n", n=128),
                            in_=st_ssm[s, hf * 1024:(hf + 1) * 1024, :].rearrange("(f p) n -> p f n", p=128)), w=[stg])
                        for q4 in range(2):
                            for q in range(4):
                                f = q4 * 4 + q
                                P.op("pe", lambda e, f=f, q=q: e.transpose(out=PS[5][:, q * 128:(q + 1) * 128],
                                                                           in_=stg[:, f * 128:(f + 1) * 128], identity=ident_f[:]),
                                     r=[stg, ident_f], w=[PS[5]])
                            o0 = (hf * 2 + q4) * 512
                            P.op("act", lambda e, o0=o0: e.activation(out=hstate[:, o0:o0 + 512], in_=PS[5][:], func=AF.Copy),
                                 r=[PS[5]], w=[hstate])
                            P.op("dve", lambda e, o0=o0: e.tensor_copy(out=hstate_b[:, o0:o0 + 512], in_=PS[5][:]),
                                 r=[PS[5]], w=[hstate_b])
                for hg in range(SH // HG):
                    py = rot("py", [PS[2], PS[3]])
                    for hh in range(HG):
                        h = hg * HG + hh
                        pr = (h // 2) * 128
                        reg = py[:, hh * LS:(hh + 1) * LS]
                        P.op("pe", lambda e, h=h, pr=pr, reg=reg, cols=cols, hh=hh: e.matmul(
                            reg, lhsT=xdt[:, pr:pr + 128], rhs=MTs[:, h, cols], start=(hh == 0), stop=False, skip_group_check=True),
                            r=[SK], w=[py])
                        P.op("pe", lambda e, h=h, pr=pr, reg=reg, cols=cols: e.matmul(
                            reg, lhsT=hstate_b[:, pr:pr + 128], rhs=Css[:, h, cols], start=False, stop=True, skip_group_check=True),
                            r=[SK, hstate_b], w=[py])
                    nf = HG // 2
                    f0 = hg * nf
                    pv = py[:, 0:HG * LS].rearrange("p (f two l) -> p f two l", two=2, l=LS)
                    for (rows, two) in ((slice(0, 64), 0), (slice(64, 128), 1)):
                        t_ = rot("tA", tA)
                        tv = t_[:, 0:nf * LS].rearrange("p (f l) -> p f l", l=LS)
                        xv = xc[rows, f0:f0 + nf, c * 128 + s * LS:c * 128 + (s + 1) * LS]
                        P.op("dve", lambda e, rows=rows, tv=tv, xv=xv, f0=f0, nf=nf: e.tensor_tensor(
                            out=tv[rows], in0=xv, in1=vC[rows, 16 + f0:16 + f0 + nf].unsqueeze(2).to_broadcast([64, nf, LS]), op=ALU.mult),
                            r=[xc, vC], w=[t_])
                        P.op("dve", lambda e, rows=rows, tv=tv, two=two, pv=pv, f0=f0, nf=nf, c=c, s=s: e.tensor_tensor(
                            out=R1[rows, f0:f0 + nf, c * 128 + s * LS:c * 128 + (s + 1) * LS], in0=tv[rows], in1=pv[rows, :, two, :],
                            op=ALU.add), r=[t_, py], w=[R1])
                if grp == "s":
                    P.op("dve", lambda e, s=s: e.tensor_scalar(out=Bms, in0=Btm, scalar1=blkS[:, s * 8:s * 8 + 1], scalar2=None, op0=ALU.mult),
                         r=[SK, blkS], w=[SK])
                    Bl = Bms
                else:
                    Bl = Btm
                for g in range(4):
                    pu = rot("pa", [PS[0], PS[1]])
                    P.op("pe", lambda e, g=g, pu=pu, Bl=Bl: e.matmul(pu[:, 0:512], lhsT=Bl[:, g, :], rhs=xdtw[:, g * 512:(g + 1) * 512],
                                                                  start=True, stop=True), r=[SK], w=[pu])
                    hsv = hstate[:, g * 512:(g + 1) * 512].rearrange("p (h x) -> p h x", x=64)
                    P.op("dve", lambda e, g=g, hsv=hsv, s=s: e.tensor_tensor(
                        out=hsv, in0=hsv, in1=edch[:, g * 8:(g + 1) * 8, s:s + 1].to_broadcast([128, 8, 64]), op=ALU.mult),
                        r=[hstate, edch], w=[hstate])
                    P.op("dve", lambda e, g=g, pu=pu: e.tensor_tensor(out=hstate[:, g * 512:(g + 1) * 512], in0=hstate[:, g * 512:(g + 1) * 512],
                                                                      in1=pu[:, 0:512], op=ALU.add), r=[hstate, pu], w=[hstate])
                if grp == "s":
                    store_state(ssms[s], "ssms")
                else:
                    P.op("act", lambda e: e.activation(out=hstate_b[:], in_=hstate[:], func=AF.Copy), r=[hstate], w=[hstate_b])

    rawt = sq

    def store_state(dst, dname):
        for q4 in range(4):
            for q in range(4):
                f = q4 * 4 + q
                P.op("pe", lambda e, f=f, q=q: e.transpose(out=PS[5][:, q * 128:(q + 1) * 128], in_=hstate[:, f * 128:(f + 1) * 128],
                                                           identity=ident_f[:]), r=[hstate, ident_f], w=[PS[5]])
            P.op("act", lambda e, q4=q4: e.activation(out=stg[:, 0:512], in_=PS[5][:], func=AF.Copy), r=[PS[5]], w=[stg])
            P.dma("sp", lambda e, q4=q4: e.dma_start(out=dst[q4 * 512:(q4 + 1) * 512, :].rearrange("(f p) n -> p f n", p=128),
                                                     in_=stg[:, 0:512].rearrange("p (f n) -> p f n", n=128)), r=[stg], w=[dname])

    def store_conv(grp):
        n = 3 if grp == "p" else 48
        dst = convp if grp == "p" else convs
        for part in range(6):
            for q in range(4):
                fc = part * 4 + q
                P.op("pe", lambda e, fc=fc, q=q: e.transpose(out=PS[5][0:n, q * 128:(q + 1) * 128], in_=histf[:, fc, 0:n],
                                                             identity=ident_f[:]), r=[histf, ident_f], w=[PS[5]])
            P.op("act", lambda e: e.activation(out=stg[0:n, 0:512], in_=PS[5][0:n, :], func=AF.Copy), r=[PS[5]], w=[stg])
            P.dma("sp", lambda e, part=part: e.dma_start(out=dst[:, part * 512:(part + 1) * 512], in_=stg[0:n, 0:512]),
                  r=[stg], w=["conv" + grp])

    def mix_out(T, segs):
        def ev_oa(i, m, pt):
            P.op("act", lambda e: e.activation(out=fT[:, i, 0:T], in_=pt[:, 0:T], func=AF.Copy), r=[pt], w=[fT])
        linear(w_oa, D, 0, D, lambda kc: big[:, kc, 0:T], big, T, ev_oa)

    def mix_out2(T, segs):
        yT = R1
        def ev_z(i, m, pt):
            t_ = rot("sq", sq)
            P.op("act", lambda e: e.activation(out=t_[:, 0:T], in_=pt[:, 0:T], func=AF.Silu), r=[pt], w=[t_])
            P.op("dve", lambda e: e.tensor_tensor(out=yT[:, i, 0:T], in0=yT[:, i, 0:T], in1=t_[:, 0:T], op=ALU.mult),
                 r=[yT, t_], w=[yT])
        linear(w_in, D, O_Z, SI, lambda kc: hT[:, kc, 0:T], hT, T, ev_z)
        rms_rstd(yT, 16, T, SI, rstd)
        for c in range(16):
            P.op("dve", lambda e, c=c: e.tensor_scalar(out=yT[:, c, 0:T], in0=yT[:, c, 0:T], scalar1=vC[:, c:c + 1], scalar2=None,
                                                       op0=ALU.mult), r=[yT, vC], w=[yT])
        def ev_ga(i, m, pt):
            t_ = rot("tA", tA)
            P.op("act", lambda e: e.activation(out=t_[:, 0:T], in_=pt[:, 0:T], func=AF.Sigmoid), r=[pt], w=[t_])
            P.op("dve", lambda e: e.tensor_tensor(out=fT[:, i, 0:T], in0=fT[:, i, 0:T], in1=t_[:, 0:T], op=ALU.mult),
                 r=[fT, t_], w=[fT])
        linear(w_in, D, O_GA, D, lambda kc: hT[:, kc, 0:T], hT, T, ev_ga)
        gs = big
        def ev_gs(i, m, pt):
            P.op("act", lambda e: e.activation(out=big[:, 8 + i, 0:T], in_=pt[:, 0:T], func=AF.Sigmoid), r=[pt], w=[big])
        linear(w_in, D, O_GS, D, lambda kc: hT[:, kc, 0:T], hT, T, ev_gs)
        def ev_os(i, m, pt):
            t_ = rot("tA", tA)
            P.op("dve", lambda e: e.tensor_tensor(out=t_[:, 0:T], in0=pt[:, 0:T], in1=rstd[:, 0:T], op=ALU.mult), r=[pt, rstd], w=[t_])
            P.op("dve", lambda e: e.tensor_tensor(out=t_[:, 0:T], in0=t_[:, 0:T], in1=big[:, 8 + i, 0:T], op=ALU.mult), r=[t_, big], w=[t_])
            P.op("dve", lambda e: e.tensor_tensor(out=fT[:, i, 0:T], in0=fT[:, i, 0:T], in1=t_[:, 0:T], op=ALU.add), r=[fT, t_], w=[fT])
        linear(w_os, SI, 0, D, lambda kc: yT[:, kc, 0:T], yT, T, ev_os)
        P.op("act", lambda e: e.activation(out=hT[:, :, 0:T], in_=fT[:, :, 0:T], func=AF.Copy), r=[fT], w=[hT])

        def ev_o(i, m, pt):
            P.op("act", lambda e: e.activation(out=fT[:, i, 0:T], in_=pt[:, 0:T], func=AF.Copy), r=[pt], w=[fT])
        linear(w_out, D, 0, D, lambda kc: hT[:, kc, 0:T], hT, T, ev_o)
        residual(1, T, segs)

    def dump_fm(src, nchunk, T, dst, dname, bf=False):
        for t in range(T // 128):
            for c0 in range(0, nchunk, 4):
                for q in range(4):
                    if bf:
                        P.op("pe", lambda e, c0=c0, q=q, t=t: e.transpose(out=PSB[:, q * 128:(q + 1) * 128],
                                                                          in_=src[:, c0 + q, t * 128:(t + 1) * 128], identity=ident_b[:]),
                             r=[src, ident_b], w=[PSB])
                    else:
                        P.op("pe", lambda e, c0=c0, q=q, t=t: e.transpose(out=PS[5][:, q * 128:(q + 1) * 128],
                                                                          in_=src[:, c0 + q, t * 128:(t + 1) * 128], identity=ident_f[:]),
                             r=[src, ident_f], w=[PS[5]])
                P.op("act", lambda e: e.activation(out=stg[:, 0:512], in_=(PSB[:, 0:512] if bf else PS[5][:]), func=AF.Copy),
                     r=[PSB if bf else PS[5]], w=[stg])
                P.dma("sp", lambda e, c0=c0, t=t: e.dma_start(out=dst[t * 128:(t + 1) * 128, c0 * 128:(c0 + 4) * 128], in_=stg[:, 0:512]),
                      r=[stg], w=[dname])

    def program():
        wstate["i"] = 0
        cnt.clear()
        constants()
        if not P.dry:
            convert_weights()
        adaln()
        for b in range(n_pblk):
            T = 512
            segs = [(0, 0, T)]
            load_xT(xp[b * 512:(b + 1) * 512, :], T)
            ffn(0, T, segs)
            mixer_kvq(T, segs, b * 512, "p", b * 512, part="kv")
            if stage >= 5:
                kv_gen(b)
            mixer_kvq(T, segs, b * 512, "p", b * 512, part="q")
            if stage >= 5:
                attn_prompt(b)
                mix_out(T, segs)
                if dbg and dbg == "p%d" % b:
                    dump_fm(fT, 8, T, dbg1, "dbg1")
            if stage >= 6:
                ssd(T, "p")
                if dbg and dbg == "p%d" % b:
                    dump_fm(R1, 16, T, dbg2, "dbg2", bf=True)
            if stage >= 7:
                mix_out2(T, segs)
                ffn(2, T, segs)
                store_xT(yp[b * 512:(b + 1) * 512, :], T, "yp")
        if n_pblk and stage >= 6:
            store_state(ssmp, "ssmp")
            store_conv("p")
        if do_sample:
            T = NSS * TS
            segs = [(1 + i, i * TS, TS) for i in range(NSS)]
            load_xT(xs, T)
            ffn(0, T, segs)
            mixer_kvq(T, segs, None, "s", 0)
            if stage >= 5:
                load_wuv()
                attn_sample()
                barrier([("fTg", 0), ("fTg", 1)], [fT])
                mix_out(T, segs)
                if dbg == "s":
                    dump_fm(fT, 8, T, dbg1, "dbg1")
            if stage >= 6:
                ssd(T, "s")
                store_conv("s")
                if dbg == "s":
                    dump_fm(R1, 16, T, dbg2, "dbg2", bf=True)
            if stage >= 7:
                mix_out2(T, segs)
                ffn(2, T, segs)
                store_xT(ys, T, "ys")

    P.dry = True
    program()
    P.dry = False
    wstate["issued"] = 0
    program()
    P.emit()
    return nc


def _prep_inputs(inp, need_cache=True):
    f = lambda k: np.ascontiguousarray(np.asarray(inp[k], dtype=np.float32))
    half = ROPE // 2
    inv = (10000.0 ** (-np.arange(half, dtype=np.float32) / half)).astype(np.float32)
    invf = np.tile(inv, 8).reshape(128, 1).astype(np.float32)
    vecA = np.concatenate([f("b_ada").reshape(72, 128), f("g_pre").reshape(24, 128), f("g_post").reshape(24, 128)], 0)
    vecB = np.concatenate([f("g_q_lat").reshape(4, 128), f("g_kv_lat").reshape(2, 128),
                           f("conv_w").reshape(96, 128), f("conv_b").reshape(24, 128)], 0)
    vecC = np.concatenate([f("g_ssm_norm").reshape(16, 128), np.repeat(f("d_skip").reshape(32), 64).reshape(16, 128)], 0)
    hvec = np.tile(np.concatenate([f("dt_bias").reshape(32), f("a_log").reshape(32), f("d_skip").reshape(32)])[None, :], (128, 1))
    shared = dict(
        invf=invf, w_ada=f("w_ada")[0], vecA=np.ascontiguousarray(vecA), vecB=np.ascontiguousarray(vecB),
        vecC=np.ascontiguousarray(vecC), hvec=np.ascontiguousarray(hvec.astype(np.float32)),
        w_gate=f("w_ffn_gate")[0].reshape(2 * D, DFF), w_up=f("w_ffn_up")[0].reshape(2 * D, DFF),
        w_down=f("w_ffn_down")[0].reshape(2 * DFF, D), w_in=f("w_in")[0],
        w_uq=f("w_uq")[0], w_ukv=f("w_ukv")[0], w_oa=f("w_o_attn")[0], w_os=f("w_o_ssm")[0], w_out=f("w_out")[0],
    )
    if need_cache:
        shared["cache_kv"] = np.asarray(inp["cache_kv_latent"], dtype=np.float32).reshape(20480, 128 * KVL)
        shared["cache_pe"] = np.asarray(inp["cache_k_rope"], dtype=np.float32).reshape(20480, 128 * ROPE)
    else:
        shared["cache_kv"] = np.zeros((1, 1), np.float32)
        shared["cache_pe"] = np.zeros((1, 1), np.float32)
    xpa, xsa, cpa, csa = f("x_prompt"), f("x_sample"), f("c_prompt"), f("c_sample")
    pt = np.asarray(inp["page_table"], dtype=np.int32)
    stc, sts = f("state_conv")[0], np.asarray(inp["state_ssm"], dtype=np.float32)[0]
    maps = []
    for c in range(8):
        m = dict(shared)
        sl = slice(c * NSS, (c + 1) * NSS)
        m["xp"] = xpa[c % 4]
        m["xs"] = xsa[sl].reshape(NSS * TS, D)
        m["cc"] = np.concatenate([cpa[c % 4:c % 4 + 1], csa[sl]], 0)
        m["ptT"] = np.ascontiguousarray(pt[sl].T)
        m["st_conv"] = np.ascontiguousarray(stc[sl].reshape(NSS * 3, CONV))
        m["st_ssm"] = np.ascontiguousarray(sts[sl].reshape(NSS, SI, SN))
        maps.append(m)
    return maps


def run(inp, need_cache=True, dev_small=None, trace=False, **bk):
    nc = build(**bk)
    maps = _prep_inputs(inp, need_cache)
    if dev_small is not None:
        ckv_s, cpe_s = dev_small
        for m in maps:
            for k in ("xs", "cc", "st_conv", "st_ssm"):
                m[k] = maps[0][k] if k != "cc" else np.concatenate([m["cc"][0:1], maps[0]["cc"][1:]], 0)
            m["cache_kv"], m["cache_pe"] = ckv_s, cpe_s
            m["ptT"] = np.ascontiguousarray(np.arange(2048, dtype=np.int32).reshape(16, 128).T)
    if trace:
        res = run_bass_kernel_spmd(nc, maps, core_ids=list(range(8)), trace=True)
        print("EXEC_TIME_NS", res.exec_time_ns)
        return res.results
    res = run_bass_kernel_spmd(nc, maps, core_ids=list(range(8)))
    return res.results


def kernel(**inp):
    r = run(inp)
    cat = lambda k, cores: np.stack([r[c][k] for c in cores], 0)
    y_p = cat("yp", range(4))
    y_s = np.concatenate([r[c]["ys"].reshape(NSS, TS, D) for c in range(8)], 0)
    kv_p = cat("kvp", range(4))[None]
    pe_p = cat("pep", range(4))[None]
    cv_p = cat("convp", range(4))[None]
    ss_p = cat("ssmp", range(4)).reshape(1, 4, SH, SHD, SN)
    kv_s = np.concatenate([r[c]["kvs"].reshape(NSS, TS, KVL) for c in range(8)], 0)[None]
    pe_s = np.concatenate([r[c]["pes"].reshape(NSS, TS, ROPE) for c in range(8)], 0)[None]
    cv_s = np.concatenate([r[c]["convs"].reshape(NSS, 3, CONV) for c in range(8)], 0)[None]
    ss_s = np.concatenate([r[c]["ssms"].reshape(NSS, SH, SHD, SN) for c in range(8)], 0)[None]
    return tuple(np.ascontiguousarray(a, dtype=np.float32) for a in (y_p, y_s, kv_p, pe_p, cv_p, ss_p, kv_s, pe_s, cv_s, ss_s))
```

```python
from contextlib import ExitStack
import math
import numpy as np
import concourse.bass as bass
import concourse.mybir as mybir
from concourse.bass_utils import run_bass_kernel_spmd

F32 = mybir.dt.float32
BF16 = mybir.dt.bfloat16
I32 = mybir.dt.int32
ALU = mybir.AluOpType
AF = mybir.ActivationFunctionType

D = 1024
SEQ = 4096
NSS = 16
TS = 8
DFF = 2816
DIN = 8000
QL, KVL, ROPE = 512, 256, 32
NH, NOPE, VH = 16, 64, 64
SI, SHD, SH, SG, SN = 2048, 64, 32, 4, 128
CONV = 3072
EPS = 1e-6
ATT_SCALE = (NOPE + ROPE) ** -0.5
PAST = 16384
O_Q, O_KV, O_PE, O_Z, O_XBC, O_DT, O_GA, O_GS = 0, 512, 768, 800, 2848, 5920, 5952, 6976


class Prog:
    COMPUTE = ("pe", "act", "dve", "pool")
    NDMA = {"sp": 24, "pool": 16, "act": 8}

    def __init__(self, nc):
        self.nc = nc
        self.ops = []
        self.stack = ExitStack()
        self.dry = False

    def sb(self, name, shape, dt):
        return self.stack.enter_context(self.nc.sbuf_tensor(name, list(shape), dt))

    def ps(self, name, shape, dt=F32):
        return self.stack.enter_context(self.nc.psum_tensor(name, list(shape), dt))

    @staticmethod
    def _k(x):
        if isinstance(x, str):
            return x
        if isinstance(x, tuple):
            return tuple(Prog._k(i) if not isinstance(i, int) else i for i in x)
        return "T:" + x.name

    def op(self, eng, fn, r=(), w=()):
        if self.dry:
            return
        r = [self._k(i) for i in r]
        w = [self._k(i) for i in w]
        w += [i for i in r if isinstance(i, str) and i.startswith("T:ps") and i not in w]
        self.ops.append(dict(eng=eng, fn=fn, r=r, w=w, dma=False))

    def dma(self, q, fn, r=(), w=()):
        if self.dry:
            return
        self.ops.append(dict(eng=q, fn=fn, r=[self._k(i) for i in r], w=[self._k(i) for i in w], dma=True))

    def plan(self):
        cnt = {e: 0 for e in self.COMPUTE}
        dcount = {}
        rr = {q: 0 for q in self.NDMA}
        lastw = {}
        readers = {}
        waited = {e: {} for e in ("pe", "act", "dve", "pool", "sp")}
        for op in self.ops:
            E = op["eng"]
            need = {}

            def req(t, kind):
                if t is None:
                    return
                s, v, te = t
                if te is not None and te == E:
                    if E == "pe" or kind == "war":
                        return
                if v > need.get(s, 0):
                    need[s] = v

            for r in op["r"]:
                req(lastw.get(r), "raw")
            for w in op["w"]:
                req(lastw.get(w), "waw")
                for s, (v, te) in readers.get(w, {}).items():
                    req((s, v, te), "war")
            if op["dma"]:
                s = "d_%s_%d" % (E, rr[E])
                rr[E] = (rr[E] + 1) % self.NDMA[E]
                prev = dcount.get(s, 0)
                if prev and 16 * prev > need.get(s, 0):
                    need[s] = 16 * prev
                dcount[s] = prev + 1
                tick = (s, 16 * (prev + 1), None)
                op["inc"] = 16
            else:
                cnt[E] += 1
                tick = (E, cnt[E], E)
                op["inc"] = 1
            wl = []
            for s, v in need.items():
                if v > waited[E].get(s, 0):
                    waited[E][s] = v
                    wl.append((s, v))
            op["waits"] = wl
            op["tick"] = tick
            for r in op["r"]:
                d = readers.setdefault(r, {})
                if tick[1] > d.get(tick[0], (0, None))[0]:
                    d[tick[0]] = (tick[1], tick[2])
            for w in op["w"]:
                lastw[w] = tick
                readers[w] = {}
        self.final = {k: v for k, v in cnt.items() if v}
        self.final.update({s: 16 * c for s, c in dcount.items()})
        self.waited = waited

    def emit(self):
        self.plan()
        nc = self.nc
        sems = {}
        with ExitStack() as st:
            for s in self.final:
                sems[s] = st.enter_context(nc.semaphore("s_" + s))
            block = st.enter_context(nc.Block())

            def replay(name, e):
                for op in self.ops:
                    if op["eng"] != name:
                        continue
                    for s, v in op["waits"]:
                        e.wait_ge(sems[s], v)
                    ins = op["fn"](e)
                    ins.then_inc(sems[op["tick"][0]], op["inc"])
                if name == "sp":
                    for s, v in self.final.items():
                        if v > self.waited["sp"].get(s, 0):
                            e.wait_ge(sems[s], v)

            @block.tensor
            def _(e):
                replay("pe", e)

            @block.scalar
            def _(e):
                replay("act", e)

            @block.vector
            def _(e):
                replay("dve", e)

            @block.gpsimd
            def _(e):
                replay("pool", e)

            @block.sync
            def _(e):
                replay("sp", e)
        self.stack.close()


def build(n_pblk=8, do_sample=True, stage=9, dbg=False, n_phys=20480):
    nc = bass.Bass("TRN2", target_bir_lowering=False)
    P = Prog(nc)

    def din(name, shape, dt=F32):
        return nc.dram_tensor(name, list(shape), dt, kind="ExternalInput").ap()

    def dout(name, shape, dt=F32):
        return nc.dram_tensor(name, list(shape), dt, kind="ExternalOutput").ap()

    xp = din("xp", [SEQ, D])
    xs = din("xs", [NSS * TS, D])
    cc = din("cc", [1 + NSS, D])
    invf = din("invf", [128, 1])
    w_ada = din("w_ada", [D, 9 * D])
    vecA = din("vecA", [120, 128])
    vecB = din("vecB", [126, 128])
    vecC = din("vecC", [32, 128])
    hvec = din("hvec", [128, 96])
    w_gate = din("w_gate", [2 * D, DFF])
    w_up = din("w_up", [2 * D, DFF])
    w_down = din("w_down", [2 * DFF, D])
    w_in = din("w_in", [D, DIN])
    w_uq = din("w_uq", [QL, NH * 96])
    w_ukv = din("w_ukv", [KVL, NH * 128])
    w_oa = din("w_oa", [D, D])
    w_os = din("w_os", [SI, D])
    w_out = din("w_out", [D, D])
    cache_kv = din("cache_kv", [n_phys, 128 * KVL])
    cache_pe = din("cache_pe", [n_phys, 128 * ROPE])
    ptT = din("ptT", [128, NSS], I32)
    st_conv = din("st_conv", [NSS * 3, CONV])
    st_ssm = din("st_ssm", [NSS, SI, SN])
    yp = dout("yp", [SEQ, D])
    ys = dout("ys", [NSS * TS, D])
    kvp = dout("kvp", [SEQ, KVL])
    pep = dout("pep", [SEQ, ROPE])
    convp = dout("convp", [3, CONV])
    ssmp = dout("ssmp", [SI, SN])
    kvs = dout("kvs", [NSS * TS, KVL])
    pes = dout("pes", [NSS * TS, ROPE])
    convs = dout("convs", [NSS * 3, CONV])
    ssms = dout("ssms", [NSS, SI, SN])
    dbg1 = dout("dbg1", [512, D]) if dbg else None
    dbg2 = dout("dbg2", [512, SI]) if dbg else None
    Ksc = nc.dram_tensor("Ksc", [NH, 96, SEQ], BF16).ap()
    Vsc = nc.dram_tensor("Vsc", [NH, 128, SEQ // 128, 66], BF16).ap()

    ident_f = P.sb("ident_f", [128, 128], F32)
    ident_b = P.sb("ident_b", [128, 128], BF16)
    ones_b = P.sb("ones_b", [128, 128], BF16)
    ones_f = P.sb("ones_f", [128, 128], F32)
    triU = P.sb("triU", [128, 128], F32)
    triS = P.sb("triS", [128, 128], F32)
    blkS = P.sb("blkS", [128, 128], F32)
    triU_b = P.sb("triU_b", [128, 128], BF16)
    triS_b = P.sb("triS_b", [128, 128], BF16)
    Esel = P.sb("Esel", [16, 128], F32)
    eps_t = P.sb("eps_t", [128, 1], F32)
    one_t = P.sb("one_t", [128, 1], F32)
    npi_t = P.sb("npi_t", [128, 1], F32)
    invf_t = P.sb("invf_t", [128, 1], F32)
    Rt = P.sb("Rt", [128, 96], BF16)
    Rtmp = P.sb("Rtmp", [32, 64], F32)
    vA = P.sb("vA", [128, 120], F32)
    vB = P.sb("vB", [128, 126], F32)
    vC = P.sb("vC", [128, 32], F32)
    hv = P.sb("hv", [128, 96], F32)
    a_bc = P.sb("a_bc", [128, 32], F32)
    stg = P.sb("stg", [128, 1024], F32)
    cT = P.sb("cT", [128, 8, 17], F32)
    modT = P.sb("modT", [128, 72, 17], F32)
    At = P.sb("At", [128, 24, 17], F32)
    Gt = P.sb("Gt", [128, 24, 17], F32)
    NWB = 4
    WBE = 4096
    WB = [P.sb("wb%d" % i, [128, WBE], BF16) for i in range(NWB)]
    xT = P.sb("xT", [128, 8, 512], F32)
    fT = P.sb("fT", [128, 8, 512], F32)
    wada = [xT, fT]
    hT = P.sb("hT", [128, 8, 512], BF16)
    big = P.sb("big", [128, 24, 512], BF16)
    hid = big
    R1 = P.sb("R1", [128, 16, 512], BF16)
    rstd = P.sb("rstd", [128, 512], F32)
    tA = [P.sb("tA%d" % i, [128, 512], F32) for i in range(2)]
    sq = [P.sb("sq%d" % i, [128, 520], BF16) for i in range(2)]
    ckv = P.sb("ckv", [128, 2, 512], F32)
    ckv_b = P.sb("ckv_b", [128, 2, 512], BF16)
    kpe = P.sb("kpe", [32, 512], F32)
    kpe_b = P.sb("kpe_b", [32, 512], BF16)
    cos_t = P.sb("cos_t", [128, 512], F32)
    sin_t = P.sb("sin_t", [128, 512], F32)
    pos_i = P.sb("pos_i", [128, 512], I32)
    qn = big[:, 20:24, :]
    otm = [P.sb("otm%d" % i, [128, 288], F32) for i in range(2)]
    hstate = P.sb("hstate", [128, SI], F32)
    hstate_b = P.sb("hstate_b", [128, SI], BF16)
    histb = P.sb("histb", [128, 24, 3], BF16)
    histf = P.sb("histf", [128, 24, 48], F32)
    dtt = P.sb("dtt", [128, 4, 32], F32)
    lat = P.sb("lat", [128, 32], F32)
    cumt = P.sb("cumt", [128, 32], F32)
    ncum = P.sb("ncum", [128, 32], F32)
    wdec = P.sb("wdec", [128, 32], F32)
    edch = P.sb("edch", [128, 32, 16], F32)
    Gm = P.sb("Gm", [128, 4, 128], F32)
    dghs = [P.sb("dgh%d" % i, [128, 128], F32) for i in range(2)]
    segts = [P.sb("segt%d" % i, [128, 128], F32) for i in range(2)]
    ecrs = [P.sb("ecr%d" % i, [128, 128], F32) for i in range(2)]
    rec = P.sb("rec", [128, 4], F32)
    pti = P.sb("pti", [128, NSS], I32)
    pti2 = P.sb("pti2", [128, NSS * 16], I32)
    arena = P.sb("arena", [128, 13312], BF16)
    PS = [P.ps("ps%d" % i, [128, 512], F32) for i in range(7)]
    PSB = P.ps("psb", [128, 1024], BF16)
    PSB2 = PS[5][:, 0:512].bitcast(BF16)

    Kbuf = arena[:, 0:4096]
    Vbuf = arena[:, 4096:4096 + 32 * 66].rearrange("p (t x) -> p t x", x=66)[:, :, 0:65]
    o_tm = arena[:, 6400:6400 + 4096].rearrange("p (t x) -> p t x", x=1024)
    PTs = [arena[:, 10496 + i * 512:10496 + (i + 1) * 512] for i in range(4)]
    vst = arena[:, 0:4 * 16 * 66].rearrange("p (t h x) -> p t h x", t=4, h=16)[:, :, :, 0:65]
    kst = arena[:, 4224:4224 + 2048].rearrange("p (h x) -> p h x", x=512)
    MTs = arena[:, 0:4096].rearrange("p (h l) -> p h l", l=128)
    Css = arena[:, 4096:8192].rearrange("p (h l) -> p h l", l=128)
    xdt = arena[:, 8192:10240]
    xdtw = arena[:, 10240:12288]
    Btm = arena[:, 12288:12800].rearrange("p (g n) -> p g n", n=128)
    Bms = arena[:, 12800:13312].rearrange("p (g n) -> p g n", n=128)
    Kbs = [arena[:, i * 2432:(i + 1) * 2432].rearrange("p (r x) -> p r x", x=304) for i in range(2)]
    KTs = [arena[:, 4864 + i * 384:4864 + (i + 1) * 384].rearrange("p (c k) -> p c k", k=128) for i in range(2)]
    Knew = arena[:, 5632:5632 + 304]
    qabs = arena[:, 5936:5936 + 3 * 2048].rearrange("p (c q) -> p c q", q=2048)
    wukT = arena[:, 12080:12080 + 256]
    olat = arena[:, 12336:12336 + 256]
    olatT = arena[:, 12592:12592 + 256].rearrange("p (c q) -> p c q", q=128)
    PTq = [arena[:, 12848 + i * 128:12848 + (i + 1) * 128] for i in range(2)]
    Gk = [fT[:, 4 * i:4 * i + 4, :].rearrange("p a b -> p (a b)") for i in range(2)]
    GkK = [("fTg", 0), ("fTg", 1)]
    Gp = [tA[i][:, 0:256] for i in range(2)]

    cnt = {}

    def rot(name, lst):
        cnt[name] = cnt.get(name, -1) + 1
        return lst[cnt[name] % len(lst)]

    AR = "arena"

    def barrier(old, new):
        P.op("dve", lambda e: e.memset(rec[:, 0:1], 0.0), r=list(old), w=list(new) + [rec])

    wsched = []
    wstate = {"i": 0, "issued": 0}
    wuniq = {}
    wbf_holder = {}

    def pview(buf, KC, w):
        return buf[:, 0:KC * w].rearrange("p (kc n) -> p kc n", n=w)

    def pkey(Wd, K, c0, w):
        return (repr(Wd), K, c0, w)

    def convert_weights():
        off = 0
        for ent in wsched:
            k = pkey(*ent)
            if k not in wuniq:
                Wd, K, c0, w = ent
                wuniq[k] = (len(wuniq), off, ent)
                off += (K // 128) * w
        Wbf = nc.dram_tensor("Wbf", [128, off], BF16).ap()
        wbf_holder["ap"] = Wbf
        for k, (idx, o, ent) in wuniq.items():
            Wd, K, c0, w = ent
            KC = K // 128
            src = Wd[0:K, c0:c0 + w].rearrange("(kc p) n -> p kc n", p=128)
            dst = Wbf[:, o:o + KC * w].rearrange("p (kc n) -> p kc n", n=w)
            P.dma("pool", lambda e, src=src, dst=dst: e.dma_start(out=dst, in_=src), w=[("wbf", idx)])

    def issue_panel(i):
        ent = wsched[i]
        Wd, K, c0, w = ent
        idx, o, _ = wuniq[pkey(*ent)]
        buf = WB[i % NWB]
        n = (K // 128) * w
        Wbf = wbf_holder["ap"]
        P.dma("sp", lambda e: e.dma_start(out=buf[:, 0:n], in_=Wbf[:, o:o + n]), r=[("wbf", idx)], w=[buf])

    def panel(Wd, K, c0, w):
        i = wstate["i"]
        wstate["i"] += 1
        assert (K // 128) * w <= WBE
        if P.dry:
            wsched.append((Wd, K, c0, w))
            return WB[i % NWB], pview(WB[i % NWB], K // 128, w)
        while wstate["issued"] <= min(i + NWB - 1, len(wsched) - 1):
            issue_panel(wstate["issued"])
            wstate["issued"] += 1
        return WB[i % NWB], pview(WB[i % NWB], K // 128, w)

    def constants():
        P.op("pool", lambda e: e.memset(ident_f[:], 1.0), w=[ident_f])
        P.op("pool", lambda e: e.affine_select(out=ident_f[:], in_=ident_f[:], pattern=[[-1, 128]],
                                               compare_op=ALU.is_equal, fill=0.0, base=0, channel_multiplier=1),
             r=[ident_f], w=[ident_f])
        P.op("dve", lambda e: e.tensor_copy(out=ident_b[:], in_=ident_f[:]), r=[ident_f], w=[ident_b])
        P.op("dve", lambda e: e.memset(ones_b[:], 1.0), w=[ones_b])
        P.op("dve", lambda e: e.memset(ones_f[:], 1.0), w=[ones_f])
        P.op("dve", lambda e: e.memset(eps_t[:], EPS), w=[eps_t])
        P.op("dve", lambda e: e.memset(one_t[:], 1.0), w=[one_t])
        P.op("dve", lambda e: e.memset(npi_t[:], -math.pi), w=[npi_t])
        P.dma("sp", lambda e: e.dma_start(out=invf_t[:], in_=invf), w=[invf_t])
        P.dma("sp", lambda e: e.dma_start(out=hv[:], in_=hvec), w=[hv])
        P.dma("sp", lambda e: e.dma_start(out=pti[:], in_=ptT), w=[pti])
        P.op("act", lambda e: e.activation(out=a_bc[:], in_=hv[:, 32:64], func=AF.Exp), r=[hv], w=[a_bc])
        P.op("dve", lambda e: e.tensor_scalar(out=a_bc[:], in0=a_bc[:], scalar1=-1.0, scalar2=None, op0=ALU.mult),
             r=[a_bc], w=[a_bc])
        P.op("pool", lambda e: e.memset(triU[:], 1.0), w=[triU])
        P.op("pool", lambda e: e.affine_select(out=triU[:], in_=triU[:], pattern=[[1, 128]],
                                               compare_op=ALU.is_ge, fill=0.0, base=0, channel_multiplier=-1),
             r=[triU], w=[triU])
        P.op("pool", lambda e: e.memset(Esel[:], 1.0), w=[Esel])
        P.op("pool", lambda e: e.affine_select(out=Esel[:], in_=Esel[:], pattern=[[1, 128]],
                                               compare_op=ALU.is_ge, fill=0.0, base=0, channel_multiplier=-8),
             r=[Esel], w=[Esel])
        P.op("pool", lambda e: e.affine_select(out=Esel[:], in_=Esel[:], pattern=[[-1, 128]],
                                               compare_op=ALU.is_ge, fill=0.0, base=7, channel_multiplier=8),
             r=[Esel], w=[Esel])
        P.op("pe", lambda e: e.matmul(PS[5][:, 0:128], lhsT=Esel[:], rhs=Esel[:], start=True, stop=True), r=[Esel], w=[PS[5]])
        P.op("dve", lambda e: e.tensor_copy(out=blkS[:], in_=PS[5][:, 0:128]), r=[PS[5]], w=[blkS])
        P.op("dve", lambda e: e.tensor_tensor(out=triS[:], in0=triU[:], in1=blkS[:], op=ALU.mult), r=[triU, blkS], w=[triS])
        P.op("dve", lambda e: e.tensor_copy(out=triU_b[:], in_=triU[:]), r=[triU], w=[triU_b])
        P.op("dve", lambda e: e.tensor_copy(out=triS_b[:], in_=triS[:]), r=[triS], w=[triS_b])
        P.op("pool", lambda e: e.memset(Rtmp[:, 0:32], 1.0), w=[Rtmp])
        P.op("pool", lambda e: e.affine_select(out=Rtmp[:, 0:32], in_=Rtmp[:, 0:32], pattern=[[-1, 32]],
                                               compare_op=ALU.is_equal, fill=0.0, base=16, channel_multiplier=1),
             r=[Rtmp], w=[Rtmp])
        P.op("pool", lambda e: e.memset(Rtmp[:, 32:64], -1.0), r=[Rtmp], w=[Rtmp])
        P.op("pool", lambda e: e.affine_select(out=Rtmp[:, 32:64], in_=Rtmp[:, 32:64], pattern=[[-1, 32]],
                                               compare_op=ALU.is_equal, fill=0.0, base=-16, channel_multiplier=1),
             r=[Rtmp], w=[Rtmp])
        P.op("dve", lambda e: e.memset(Rt[:], 0.0), w=[Rt])
        P.op("dve", lambda e: e.tensor_tensor(out=Rt[0:32, 0:32], in0=Rtmp[:, 0:32], in1=Rtmp[:, 32:64], op=ALU.add),
             r=[Rtmp, Rt], w=[Rt])
        P.dma("sp", lambda e: e.dma_start(out=Rt[64:96, 64:96], in_=Rt[0:32, 0:32]), r=[Rt], w=[Rt])
        for (src, n, dst) in ((vecA, 120, vA), (vecB, 126, vB), (vecC, 32, vC)):
            P.dma("sp", lambda e, src=src, n=n: e.dma_start(out=stg[0:n, 0:128], in_=src), w=[stg])
            P.op("pe", lambda e, n=n: e.transpose(out=PS[5][:, 0:n], in_=stg[0:n, 0:128], identity=ident_f[0:n, 0:n]),
                 r=[stg, ident_f], w=[PS[5]])
            P.op("dve", lambda e, n=n, dst=dst: e.tensor_copy(out=dst[:, 0:n], in_=PS[5][:, 0:n]), r=[PS[5]], w=[dst])
        P.op("dve", lambda e: e.memset(hstate[:], 0.0), w=[hstate])
        P.op("dve", lambda e: e.memset(hstate_b[:], 0.0), w=[hstate_b])
        P.op("dve", lambda e: e.memset(histb[:], 0.0), w=[histb])

    def adaln():
        P.dma("sp", lambda e: e.dma_start(out=stg[0:17, :], in_=cc), w=[stg])
        P.op("act", lambda e: e.activation(out=stg[0:17, :], in_=stg[0:17, :], func=AF.Silu), r=[stg], w=[stg])
        for kc in range(8):
            P.op("pe", lambda e, kc=kc: e.transpose(out=PS[5][:, kc * 17:(kc + 1) * 17], in_=stg[0:17, kc * 128:(kc + 1) * 128],
                                                    identity=ident_f[0:17, 0:17]), r=[stg, ident_f], w=[PS[5]])
        P.op("dve", lambda e: e.tensor_copy(out=cT[:].rearrange("p a b -> p (a b)"), in_=PS[5][:, 0:136]), r=[PS[5]], w=[cT])
        for pn in range(18):
            wb = wada[pn % 2]
            P.dma("sp", lambda e, pn=pn, wb=wb: e.dma_start(
                out=wb[:], in_=w_ada[:, pn * 512:(pn + 1) * 512].rearrange("(kc p) n -> p kc n", p=128)), w=[wb])
            pt = PS[pn % 2]
            for o in range(4):
                for kc in range(8):
                    P.op("pe", lambda e, o=o, kc=kc, wb=wb, pt=pt: e.matmul(
                        pt[:, o * 17:(o + 1) * 17], lhsT=wb[:, kc, o * 128:(o + 1) * 128], rhs=cT[:, kc, :],
                        start=(kc == 0), stop=(kc == 7)), r=[wb, cT], w=[pt])
            for o in range(4):
                oc = pn * 4 + o
                P.op("act", lambda e, o=o, oc=oc, pt=pt: e.activation(
                    out=modT[:, oc, :], in_=pt[:, o * 17:(o + 1) * 17], func=AF.Identity, bias=vA[:, oc:oc + 1], scale=1.0),
                    r=[pt, vA], w=[modT])
        for j in range(3):
            for c in range(8):
                i = j * 8 + c
                P.op("dve", lambda e, j=j, c=c, i=i: e.tensor_scalar(
                    out=At[:, i, :], in0=modT[:, (3 * j + 1) * 8 + c, :], scalar1=1.0, scalar2=vA[:, 72 + i:73 + i],
                    op0=ALU.add, op1=ALU.mult), r=[modT, vA], w=[At])
                P.op("dve", lambda e, j=j, c=c, i=i: e.tensor_scalar(
                    out=Gt[:, i, :], in0=modT[:, (3 * j + 2) * 8 + c, :], scalar1=vA[:, 96 + i:97 + i],
                    scalar2=(1.0 if j == 1 else 0.5), op0=ALU.mult, op1=ALU.mult), r=[modT, vA], w=[Gt])

    def load_xT(src, T):
        for t in range(T // 128):
            P.dma("sp", lambda e, t=t: e.dma_start(out=stg[:], in_=src[t * 128:(t + 1) * 128, :]), w=[stg])
            for half in range(2):
                pt = PS[5]
                for q in range(4):
                    c = half * 4 + q
                    P.op("pe", lambda e, c=c, q=q, pt=pt: e.transpose(
                        out=pt[:, q * 128:(q + 1) * 128], in_=stg[:, c * 128:(c + 1) * 128], identity=ident_f[:]),
                        r=[stg, ident_f], w=[pt])
                P.op("act", lambda e, half=half, t=t, pt=pt: e.activation(
                    out=xT[:, half * 4:half * 4 + 4, t * 128:(t + 1) * 128],
                    in_=pt[:].rearrange("p (q n) -> p q n", q=4), func=AF.Copy), r=[pt], w=[xT])

    def store_xT(dst, T, dname):
        for t in range(T // 128):
            for half in range(2):
                pt = PS[5]
                for q in range(4):
                    c = half * 4 + q
                    P.op("pe", lambda e, c=c, q=q, pt=pt, t=t: e.transpose(
                        out=pt[:, q * 128:(q + 1) * 128], in_=xT[:, c, t * 128:(t + 1) * 128], identity=ident_f[:]),
                        r=[xT, ident_f], w=[pt])
                P.op("act", lambda e, half=half, pt=pt: e.activation(
                    out=stg[:, half * 512:(half + 1) * 512], in_=pt[:], func=AF.Copy), r=[pt], w=[stg])
            P.dma("sp", lambda e, t=t: e.dma_start(out=dst[t * 128:(t + 1) * 128, :], in_=stg[:]), r=[stg], w=[dname])

    def rms_rstd(src, nchunks, T, n_feat, dst, src_res=None):
        for c in range(nchunks):
            s_ = rot("sq", sq)
            P.op("act", lambda e, c=c, s_=s_: e.activation(out=s_[:, 0:T], in_=src[:, c, 0:T], func=AF.Square),
                 r=[src_res or src], w=[s_])
            P.op("pe", lambda e, c=c, s_=s_: e.matmul(PS[4][:, 0:T], lhsT=ones_b[:], rhs=s_[:, 0:T],
                                                     start=(c == 0), stop=(c == nchunks - 1)), r=[s_, ones_b], w=[PS[4]])
        P.op("act", lambda e: e.activation(out=dst[:, 0:T], in_=PS[4][:, 0:T], func=AF.Sqrt, bias=eps_t[:, 0:1],
                                           scale=1.0 / n_feat), r=[PS[4], eps_t], w=[dst])
        P.op("dve", lambda e: e.reciprocal(out=dst[:, 0:T], in_=dst[:, 0:T]), r=[dst], w=[dst])

    def modulate(j, T, segs):
        rms_rstd(xT, 8, T, D, rstd)
        for c in range(8):
            t_ = rot("tA", tA)
            P.op("dve", lambda e, c=c, t_=t_: e.tensor_tensor(out=t_[:, 0:T], in0=xT[:, c, 0:T], in1=rstd[:, 0:T], op=ALU.mult),
                 r=[xT, rstd], w=[t_])
            for (s, c0, n) in segs:
                P.op("act", lambda e, c=c, t_=t_, s=s, c0=c0, n=n: e.activation(
                    out=hT[:, c, c0:c0 + n], in_=t_[:, c0:c0 + n], func=AF.Identity,
                    bias=modT[:, (3 * j) * 8 + c, s:s + 1], scale=At[:, j * 8 + c, s:s + 1]),
                    r=[t_, modT, At], w=[hT])

    def linear(Wd, K, col0, ncols, rhs, rhs_res, T, evac, chunk=128, pw=None):
        KC = K // 128
        if pw is None:
            pw = 512 if KC * 512 <= WBE else (256 if KC * 256 <= WBE else 128)
        done = 0
        idx = 0
        while done < ncols:
            w = min(pw, ncols - done)
            bres, buf = panel(Wd, K, col0 + done, w)
            o = 0
            while o < w:
                m = min(chunk, w - o)
                pt = rot("pa", [PS[0], PS[1]])
                for kc in range(KC):
                    P.op("pe", lambda e, kc=kc, o=o, m=m, pt=pt, buf=buf: e.matmul(
                        pt[0:m, 0:T], lhsT=buf[:, kc, o:o + m], rhs=rhs(kc), start=(kc == 0), stop=(kc == KC - 1)),
                        r=[bres, rhs_res], w=[pt])
                evac(idx, m, pt)
                idx += 1
                o += m
            done += w

    def residual(j, T, segs):
        rms_rstd(fT, 8, T, D, rstd)
        for c in range(8):
            t_ = rot("tA", tA)
            P.op("dve", lambda e, c=c, t_=t_: e.tensor_tensor(out=t_[:, 0:T], in0=fT[:, c, 0:T], in1=rstd[:, 0:T], op=ALU.mult),
                 r=[fT, rstd], w=[t_])
            for (s, c0, n) in segs:
                P.op("dve", lambda e, c=c, t_=t_, s=s, c0=c0, n=n: e.scalar_tensor_tensor(
                    out=xT[:, c, c0:c0 + n], in0=t_[:, c0:c0 + n], scalar=Gt[:, j * 8 + c, s:s + 1],
                    in1=xT[:, c, c0:c0 + n], op0=ALU.mult, op1=ALU.add), r=[t_, Gt, xT], w=[xT])

    def ffn(j, T, segs):
        fi = 0 if j == 0 else 1
        modulate(j, T, segs)
        for pn in range(6):
            c0 = pn * 512
            w = min(512, DFF - c0)
            nch = w // 128
            gres, gbuf = panel(w_gate[fi * D:(fi + 1) * D, :], D, c0, w)
            for o in range(nch):
                pt = rot("pa", [PS[0], PS[1]])
                for kc in range(8):
                    P.op("pe", lambda e, kc=kc, o=o, pt=pt, gbuf=gbuf: e.matmul(
                        pt[:, 0:T], lhsT=gbuf[:, kc, o * 128:(o + 1) * 128], rhs=hT[:, kc, 0:T],
                        start=(kc == 0), stop=(kc == 7)), r=[gres, hT], w=[pt])
                P.op("act", lambda e, o=o, pt=pt, pn=pn: e.activation(
                    out=hid[:, pn * 4 + o, 0:T], in_=pt[:, 0:T], func=AF.Silu), r=[pt], w=[hid])
            ures, ubuf = panel(w_up[fi * D:(fi + 1) * D, :], D, c0, w)
            for o in range(nch):
                pt = rot("pb", [PS[2], PS[3]])
                for kc in range(8):
                    P.op("pe", lambda e, kc=kc, o=o, pt=pt, ubuf=ubuf: e.matmul(
                        pt[:, 0:T], lhsT=ubuf[:, kc, o * 128:(o + 1) * 128], rhs=hT[:, kc, 0:T],
                        start=(kc == 0), stop=(kc == 7)), r=[ures, hT], w=[pt])
                P.op("dve", lambda e, o=o, pt=pt, pn=pn: e.tensor_tensor(
                    out=hid[:, pn * 4 + o, 0:T], in0=hid[:, pn * 4 + o, 0:T], in1=pt[:, 0:T], op=ALU.mult),
                    r=[pt, hid], w=[hid])

        def ev(i, m, pt):
            P.op("act", lambda e: e.activation(out=fT[:, i, 0:T], in_=pt[:, 0:T], func=AF.Copy), r=[pt], w=[fT])
        linear(w_down[fi * DFF:(fi + 1) * DFF, :], DFF, 0, D, lambda kc: hid[:, kc, 0:T], hid, T, ev)
        residual(j, T, segs)

    def rope_tables(pos0, T):
        ang, frac = tA[0], tA[1]
        if pos0 is None:
            P.op("pool", lambda e: e.iota(pos_i[:, 0:T].rearrange("p (a b) -> p a b", b=TS), pattern=[[0, NSS], [1, TS]],
                                          base=PAST, channel_multiplier=0), w=[pos_i])
        else:
            P.op("pool", lambda e: e.iota(pos_i[:, 0:T], pattern=[[1, T]], base=pos0, channel_multiplier=0), w=[pos_i])
        P.op("dve", lambda e: e.tensor_copy(out=ang[:, 0:T], in_=pos_i[:, 0:T]), r=[pos_i], w=[ang])
        P.op("dve", lambda e: e.tensor_scalar(out=ang[:, 0:T], in0=ang[:, 0:T], scalar1=invf_t[:, 0:1], scalar2=None,
                                              op0=ALU.mult), r=[ang, invf_t], w=[ang])
        for (dst, off) in ((sin_t, 0.5), (cos_t, 0.75)):
            P.op("dve", lambda e, dst=dst, off=off: e.tensor_scalar(
                out=dst[:, 0:T], in0=ang[:, 0:T], scalar1=1.0 / (2 * math.pi), scalar2=off, op0=ALU.mult, op1=ALU.add),
                r=[ang], w=[dst])
            P.op("dve", lambda e, dst=dst: e.tensor_copy(out=pos_i[:, 0:T], in_=dst[:, 0:T]), r=[dst], w=[pos_i])
            P.op("dve", lambda e, dst=dst: e.tensor_copy(out=frac[:, 0:T], in_=pos_i[:, 0:T]), r=[pos_i], w=[frac])
            P.op("dve", lambda e, dst=dst: e.tensor_tensor(out=dst[:, 0:T], in0=dst[:, 0:T], in1=frac[:, 0:T], op=ALU.subtract),
                 r=[dst, frac], w=[dst])
            P.op("dve", lambda e, dst=dst: e.scalar_tensor_tensor(out=dst[:, 0:T], in0=dst[:, 0:T], scalar=0.0, in1=dst[:, 0:T],
                                                                  op0=ALU.is_lt, op1=ALU.add), r=[dst], w=[dst])
            P.op("act", lambda e, dst=dst: e.activation(out=dst[:, 0:T], in_=dst[:, 0:T], func=AF.Sin,
                                                        bias=npi_t[:, 0:1], scale=2 * math.pi), r=[dst, npi_t], w=[dst])

    def tm_out(srcs, T, dst, row0, dname, keep=None):
        ncol = sum(m for _, m in srcs)
        for t in range(T // 128):
            o_ = rot("otm", otm)
            c0 = 0
            for (fn, m) in srcs:
                P.op("pe", lambda e, fn=fn, m=m, c0=c0, t=t: e.transpose(
                    out=PS[5][:, c0:c0 + m], in_=fn(t), identity=ident_f[0:m, 0:m]), r=[fn.res, ident_f], w=[PS[5]])
                c0 += m
            P.op("dve", lambda e, o_=o_: e.tensor_copy(out=o_[:, 0:ncol], in_=PS[5][:, 0:ncol]), r=[PS[5]], w=[o_])
            if keep is not None:
                keep(o_)
            P.dma("sp", lambda e, o_=o_, t=t: e.dma_start(out=dst[row0 + t * 128:row0 + (t + 1) * 128, :], in_=o_[:, 0:ncol]),
                  r=[o_], w=[dname])

    def mixer_kvq(T, segs, pos0, grp, row0, part="both"):
        if part in ("both", "kv"):
            mixer_kv(T, segs, pos0, grp, row0)
        if part in ("both", "q"):
            mixer_q(T)

    def mixer_kv(T, segs, pos0, grp, row0):
        modulate(1, T, segs)

        def ev_kv(i, m, pt):
            P.op("act", lambda e: e.activation(out=ckv[:, i, 0:T], in_=pt[:, 0:T], func=AF.Copy), r=[pt], w=[ckv])

        def ev_pe(i, m, pt):
            P.op("act", lambda e: e.activation(out=kpe[:, 0:T], in_=pt[0:32, 0:T], func=AF.Copy), r=[pt], w=[kpe])
        linear(w_in, D, O_KV, KVL, lambda kc: hT[:, kc, 0:T], hT, T, ev_kv)
        linear(w_in, D, O_PE, ROPE, lambda kc: hT[:, kc, 0:T], hT, T, ev_pe)
        rms_rstd(ckv, 2, T, KVL, rstd)
        for c in range(2):
            P.op("dve", lambda e, c=c: e.scalar_tensor_tensor(
                out=ckv[:, c, 0:T], in0=ckv[:, c, 0:T], scalar=vB[:, 4 + c:5 + c], in1=rstd[:, 0:T],
                op0=ALU.mult, op1=ALU.mult), r=[ckv, vB, rstd], w=[ckv])
        P.op("act", lambda e: e.activation(out=ckv_b[:, :, 0:T], in_=ckv[:, :, 0:T], func=AF.Copy), r=[ckv], w=[ckv_b])
        rope_tables(pos0, T)
        P.op("dve", lambda e: e.tensor_copy(out=kpe_b[:, 0:T], in_=kpe[:, 0:T]), r=[kpe], w=[kpe_b])
        P.op("pe", lambda e: e.matmul(PS[6][0:32, 0:T], lhsT=Rt[0:32, 0:32], rhs=kpe_b[:, 0:T], start=True, stop=True),
             r=[Rt, kpe_b], w=[PS[6]])
        kpe_r = tA[0][0:32, :]
        P.op("dve", lambda e: e.tensor_tensor(out=kpe_r[:, 0:T], in0=PS[6][0:32, 0:T], in1=sin_t[0:32, 0:T], op=ALU.mult),
             r=[PS[6], sin_t], w=[tA[0]])
        P.op("dve", lambda e: e.tensor_tensor(out=kpe[:, 0:T], in0=kpe[:, 0:T], in1=cos_t[0:32, 0:T], op=ALU.mult),
             r=[kpe, cos_t], w=[kpe])
        P.op("dve", lambda e: e.tensor_tensor(out=kpe[:, 0:T], in0=kpe[:, 0:T], in1=kpe_r[:, 0:T], op=ALU.add),
             r=[kpe, tA[0]], w=[kpe])
        P.op("dve", lambda e: e.tensor_copy(out=kpe_b[:, 0:T], in_=kpe[:, 0:T]), r=[kpe], w=[kpe_b])
        f0 = lambda t: ckv[:, 0, t * 128:(t + 1) * 128]
        f0.res = ckv
        f1 = lambda t: ckv[:, 1, t * 128:(t + 1) * 128]
        f1.res = ckv
        f2 = lambda t: kpe[:, t * 128:(t + 1) * 128]
        f2.res = kpe
        if grp == "s":
            barrier([(AR, "K"), (AR, "V"), (AR, "otm"), (AR, "PT0"), (AR, "PT1"), (AR, "PT2"), (AR, "PT3"), (AR, "ssd"), (AR, "vst"), (AR, "kst")],
                    [(AR, "Knew")])
        tm_out([(f0, 128), (f1, 128)], T, kvp if grp == "p" else kvs, row0, "kv" + grp,
               keep=(lambda o_: (P.op("act", lambda e: e.activation(out=Knew[:, 0:256], in_=o_[:, 0:256], func=AF.Copy),
                                      r=[o_], w=[(AR, "Knew")]),
                                 P.op("dve", lambda e: e.memset(Knew[:, 256:257], 1.0), w=[(AR, "Knew")]))) if grp == "s" else None)
        tm_out([(f2, 32)], T, pep if grp == "p" else pes, row0, "pe" + grp,
               keep=(lambda o_: P.op("act", lambda e: e.activation(out=Knew[:, 257:289], in_=o_[:, 0:32], func=AF.Copy),
                                     r=[o_], w=[(AR, "Knew")])) if grp == "s" else None)
    def mixer_q(T):
        def ev_q(i, m, pt):
            P.op("act", lambda e: e.activation(out=fT[:, i, 0:T], in_=pt[:, 0:T], func=AF.Copy), r=[pt], w=[fT])
        linear(w_in, D, O_Q, QL, lambda kc: hT[:, kc, 0:T], hT, T, ev_q)
        rms_rstd(fT, 4, T, QL, rstd)
        for c in range(4):
            P.op("dve", lambda e, c=c: e.scalar_tensor_tensor(
                out=qn[:, c, 0:T], in0=fT[:, c, 0:T], scalar=vB[:, c:c + 1], in1=rstd[:, 0:T],
                op0=ALU.mult, op1=ALU.mult), r=[fT, vB, rstd], w=[big])

        def ev_qh(h, m, pt):
            P.op("act", lambda e: e.activation(out=R1[0:96, h, 0:T], in_=pt[0:96, 0:T], func=AF.Copy, scale=ATT_SCALE),
                 r=[pt], w=[R1])
            P.op("pe", lambda e: e.matmul(PS[6][0:96, 0:T], lhsT=Rt[64:96, 0:96], rhs=R1[64:96, h, 0:T], start=True, stop=True),
                 r=[Rt, R1], w=[PS[6]])
            t_ = rot("tA", tA)
            P.op("dve", lambda e: e.tensor_tensor(out=t_[64:96, 0:T], in0=PS[6][64:96, 0:T], in1=sin_t[64:96, 0:T], op=ALU.mult),
                 r=[PS[6], sin_t], w=[t_])
            P.op("dve", lambda e: e.tensor_tensor(out=R1[64:96, h, 0:T], in0=R1[64:96, h, 0:T], in1=cos_t[64:96, 0:T], op=ALU.mult),
                 r=[R1, cos_t], w=[R1])
            P.op("dve", lambda e: e.tensor_tensor(out=R1[64:96, h, 0:T], in0=R1[64:96, h, 0:T], in1=t_[64:96, 0:T], op=ALU.add),
                 r=[R1, t_], w=[R1])
        linear(w_uq, QL, 0, NH * 96, lambda kc: qn[:, kc, 0:T], big, T, ev_qh, chunk=96, pw=384)

    def kv_gen(b):
        T = 512
        barrier([(AR, "K"), (AR, "V"), (AR, "otm"), (AR, "PT0"), (AR, "PT1"), (AR, "PT2"), (AR, "PT3"), (AR, "ssd"), (AR, "sa"), (AR, "Knew")],
                [(AR, "vst"), (AR, "kst")])
        P.op("dve", lambda e: e.memset(vst[:, :, :, 64:65], 1.0), w=[(AR, "vst")])
        for pn in range(4):
            bres, buf = panel(w_ukv, KVL, pn * 512, 512)
            for hh in range(4):
                h = pn * 4 + hh
                pt = rot("pa", [PS[0], PS[1]])
                for kc in range(2):
                    P.op("pe", lambda e, kc=kc, hh=hh, pt=pt, buf=buf: e.matmul(
                        pt[0:64, 0:T], lhsT=buf[:, kc, hh * 128:hh * 128 + 64], rhs=ckv_b[:, kc, 0:T],
                        start=(kc == 0), stop=(kc == 1)), r=[bres, ckv_b], w=[pt])
                P.op("act", lambda e, hh=hh, pt=pt: e.activation(out=kst[0:64, hh, :], in_=pt[0:64, 0:T], func=AF.Copy),
                     r=[pt], w=[(AR, "kst")])
                P.dma("sp", lambda e, h=h, hh=hh: e.dma_start(out=Ksc[h, 0:64, b * 512:(b + 1) * 512], in_=kst[0:64, hh, :]),
                      r=[(AR, "kst")], w=["Ksc"])
                P.dma("sp", lambda e, h=h: e.dma_start(out=Ksc[h, 64:96, b * 512:(b + 1) * 512], in_=kpe_b[:, 0:T]),
                      r=[kpe_b], w=["Ksc"])
            for t in range(4):
                pt = rot("pb", [PS[2], PS[3]])
                for kc in range(2):
                    P.op("pe", lambda e, kc=kc, t=t, pt=pt, buf=buf: e.matmul(
                        pt[:, 0:256].rearrange("p (h x) -> p h x", x=64), lhsT=ckv_b[:, kc, t * 128:(t + 1) * 128],
                        rhs=buf[:, kc, :].rearrange("p (h x) -> p h x", x=128)[:, :, 64:128],
                        start=(kc == 0), stop=(kc == 1)), r=[bres, ckv_b], w=[pt])
                P.op("act", lambda e, t=t, pn=pn, pt=pt: e.activation(
                    out=vst[:, t, pn * 4:(pn + 1) * 4, 0:64], in_=pt[:, 0:256].rearrange("p (h x) -> p h x", x=64),
                    func=AF.Copy), r=[pt], w=[(AR, "vst")])
        for h in range(NH):
            P.dma("sp", lambda e, h=h: e.dma_start(out=Vsc[h, :, b * 4:(b + 1) * 4, 0:65], in_=vst[:, :, h, :]),
                  r=[(AR, "vst")], w=["Vsc"])

    def attn_prompt(b):
        T = 512
        nk = (b + 1) * 512
        nkt = nk // 128
        barrier([(AR, "vst"), (AR, "kst")], [(AR, "K"), (AR, "V"), (AR, "otm"), (AR, "PT0"), (AR, "PT1"), (AR, "PT2"), (AR, "PT3")])
        for h in range(NH):
            P.dma("sp", lambda e, h=h: e.dma_start(out=Kbuf[0:96, 0:nk], in_=Ksc[h, :, 0:nk]), r=["Ksc"], w=[(AR, "K")])
            P.dma("sp", lambda e, h=h: e.dma_start(out=Vbuf[:, 0:nkt, :], in_=Vsc[h, :, 0:nkt, 0:65]), r=["Vsc"], w=[(AR, "V")])
            def QX(kt, h=h):
                j = kt - 4 * b
                q0 = 0 if j < 0 else j * 128
                nq = 512 - q0
                pS = PS[kt % 4]
                P.op("pe", lambda e, kt=kt, q0=q0, nq=nq, pS=pS, h=h: e.matmul(
                    pS[:, 0:nq], lhsT=Kbuf[0:96, kt * 128:(kt + 1) * 128], rhs=R1[0:96, h, q0:512], start=True, stop=True),
                    r=[(AR, "K"), R1], w=[pS])
                pi = kt % 4
                PT = PTs[pi]
                pk = (AR, "PT%d" % pi)
                P.op("act", lambda e, nq=nq, pS=pS, PT=PT: e.activation(out=PT[:, 0:nq], in_=pS[:, 0:nq], func=AF.Exp),
                     r=[pS], w=[pk])
                if j >= 0:
                    P.op("dve", lambda e, PT=PT: e.tensor_tensor(out=PT[:, 0:128], in0=PT[:, 0:128], in1=triU_b[:], op=ALU.mult),
                         r=[pk, triU_b], w=[pk])
                return PT, pk, q0

            def PV(kt, st):
                PT, pk, q0 = st
                for qt in range(q0 // 128, 4):
                    last_kt = 4 * b + qt
                    P.op("pe", lambda e, kt=kt, qt=qt, q0=q0, PT=PT, last_kt=last_kt: e.matmul(
                        PS[6][:, qt * 65:(qt + 1) * 65], lhsT=PT[:, qt * 128 - q0:qt * 128 - q0 + 128], rhs=Vbuf[:, kt, :],
                        start=(kt == 0 and qt == 0), stop=(kt == last_kt), skip_group_check=True),
                        r=[pk, (AR, "V")], w=[PS[6]])
            pend = [QX(k) for k in range(min(3, nkt))]
            for kt in range(nkt):
                if kt + 3 < nkt:
                    pend.append(QX(kt + 3))
                PV(kt, pend.pop(0))
            P.op("dve", lambda e: e.reciprocal(out=rec[:, 0:4], in_=PS[6][:, 0:260].rearrange("p (t x) -> p t x", x=65)[:, :, 64]),
                 r=[PS[6]], w=[rec])
            P.op("dve", lambda e, h=h: e.tensor_tensor(
                out=o_tm[:, :, h * 64:(h + 1) * 64], in0=PS[6][:, 0:260].rearrange("p (t x) -> p t x", x=65)[:, :, 0:64],
                in1=rec[:, 0:4].unsqueeze(2).to_broadcast([128, 4, 64]), op=ALU.mult), r=[PS[6], rec], w=[(AR, "otm")])
        for qt in range(4):
            for c in range(8):
                P.op("pe", lambda e, qt=qt, c=c: e.transpose(out=PSB[:, c * 128:(c + 1) * 128], in_=o_tm[:, qt, c * 128:(c + 1) * 128],
                                                             identity=ident_b[:]), r=[(AR, "otm"), ident_b], w=[PSB])
            P.op("act", lambda e, qt=qt: e.activation(out=big[:, 0:8, qt * 128:(qt + 1) * 128],
                                                      in_=PSB[:].rearrange("p (c n) -> p c n", n=128), func=AF.Copy),
                 r=[PSB], w=[big])

    def attn_sample():
        T = 128
        barrier([(AR, "vst"), (AR, "kst"), (AR, "K"), (AR, "V"), (AR, "otm"), (AR, "PT0"), (AR, "PT1"), (AR, "PT2"), (AR, "PT3"), (AR, "ssd"), fT],
                [(AR, "sa"), (AR, "Kb0"), (AR, "Kb1"), (AR, "KT0"), (AR, "KT1"), (AR, "PQ0"), (AR, "PQ1"), (AR, "qabs"), ("fTg", 0), ("fTg", 1)])
        for pn in range(4):
            bres, buf = panel(w_ukv, KVL, pn * 512, 512)
            for hh in range(4):
                h = pn * 4 + hh
                for kc in range(2):
                    P.op("pe", lambda e, kc=kc, hh=hh, buf=buf: e.transpose(
                        out=PSB[0:64, kc * 128:(kc + 1) * 128], in_=buf[:, kc, hh * 128:hh * 128 + 64], identity=ident_b[:]),
                        r=[bres, ident_b], w=[PSB])
                P.op("act", lambda e: e.activation(out=wukT[0:64, :], in_=PSB[0:64, 0:256], func=AF.Copy), r=[PSB], w=[(AR, "sa")])
                for kc in range(2):
                    pt = rot("pa", [PS[0], PS[1]])
                    P.op("pe", lambda e, kc=kc, h=h, pt=pt: e.matmul(
                        pt[:, 0:T], lhsT=wukT[0:64, kc * 128:(kc + 1) * 128], rhs=R1[0:64, h, 0:T], start=True, stop=True),
                        r=[(AR, "sa"), R1], w=[pt])
                    P.op("act", lambda e, kc=kc, h=h, pt=pt: e.activation(
                        out=qabs[:, kc, :].rearrange("p (st h) -> p st h", h=NH)[:, :, h], in_=pt[:, 0:T], func=AF.Copy),
                        r=[pt], w=[(AR, "qabs")])
                pt = rot("pa", [PS[0], PS[1]])
                P.op("pe", lambda e, h=h, pt=pt: e.matmul(pt[0:32, 0:T], lhsT=ident_b[64:96, 64:96], rhs=R1[64:96, h, 0:T],
                                                         start=True, stop=True), r=[ident_b, R1], w=[pt])
                P.op("act", lambda e, h=h, pt=pt: e.activation(
                    out=qabs[0:32, 2, :].rearrange("p (st h) -> p st h", h=NH)[:, :, h], in_=pt[0:32, 0:T], func=AF.Copy),
                    r=[pt], w=[(AR, "qabs")])
        gf, pf = tA[0][:, 0:256], tA[1][:, 0:256]
        P.op("pool", lambda e: e.iota(pti2[:].rearrange("p (a b) -> p a b", b=16), pattern=[[0, NSS], [1, 16]], base=0,
                                      channel_multiplier=0), w=[pti2])
        P.op("dve", lambda e: e.tensor_copy(out=gf, in_=pti2[:]), r=[pti2], w=[tA[0]])
        P.op("dve", lambda e: e.tensor_copy(out=pf[:, 0:NSS], in_=pti[:]), r=[pti], w=[tA[1]])
        P.op("dve", lambda e: e.scalar_tensor_tensor(
            out=gf.rearrange("p (a b) -> p a b", b=16), in0=pf[:, 0:NSS].unsqueeze(2).to_broadcast([128, NSS, 16]), scalar=16.0,
            in1=gf.rearrange("p (a b) -> p a b", b=16), op0=ALU.mult, op1=ALU.add), r=[tA[0], tA[1]], w=[tA[0]])
        P.op("dve", lambda e: e.tensor_copy(out=pti2[:], in_=gf), r=[tA[0]], w=[pti2])
        ckv_view = cache_kv.rearrange("n (g x) -> (n g) x", x=2048)
        cpe_view = cache_pe.rearrange("n (g x) -> (n g) x", x=256)
        for s in range(NSS):
            qs = lambda kc, K, s=s: qabs[0:K, kc, s * 128:(s + 1) * 128]
            def gather(g, s=s):
                gi = g % 2
                P.dma("pool", lambda e, g=g, s=s, gi=gi: e.indirect_dma_start(
                    out=Gk[gi][:], out_offset=None, in_=ckv_view,
                    in_offset=bass.IndirectOffsetOnAxis(ap=pti2[:, s * 16 + g:s * 16 + g + 1], axis=0)), r=[pti2], w=[GkK[gi]])
                P.dma("pool", lambda e, g=g, s=s, gi=gi: e.indirect_dma_start(
                    out=Gp[gi][:], out_offset=None, in_=cpe_view,
                    in_offset=bass.IndirectOffsetOnAxis(ap=pti2[:, s * 16 + g:s * 16 + g + 1], axis=0)), r=[pti2], w=[tA[gi]])

            def cast(g):
                gi = g % 2
                Kb = Kbs[gi]
                kbk = (AR, "Kb%d" % gi)
                P.op("act", lambda e, gi=gi, Kb=Kb: e.activation(out=Kb[:, :, 0:256], in_=Gk[gi][:].rearrange("p (r x) -> p r x", x=256),
                                                                 func=AF.Copy), r=[GkK[gi]], w=[kbk])
                P.op("dve", lambda e, gi=gi, Kb=Kb: e.tensor_copy(out=Kb[:, :, 257:289], in_=Gp[gi][:].rearrange("p (r x) -> p r x", x=32)),
                     r=[tA[gi]], w=[kbk])
                P.op("dve", lambda e, Kb=Kb: e.memset(Kb[:, :, 256:257], 1.0), w=[kbk])

            def stA(t):
                g, r_ = t // 8, t % 8
                if r_ == 0:
                    if g + 1 < 16:
                        gather(g + 1)
                    cast(g)
                gi, ti = g % 2, t % 2
                Kb, kbk = Kbs[gi], (AR, "Kb%d" % gi)
                TB, tbk = (PSB, PSB) if ti == 0 else (PSB2, PS[5])
                KT, ktk = KTs[ti], (AR, "KT%d" % ti)
                P.op("pe", lambda e: e.transpose(out=TB[:, 0:128], in_=Kb[:, r_, 0:128], identity=ident_b[:]), r=[kbk, ident_b], w=[tbk])
                P.op("pe", lambda e: e.transpose(out=TB[:, 128:256], in_=Kb[:, r_, 128:256], identity=ident_b[:]), r=[kbk, ident_b], w=[tbk])
                P.op("pe", lambda e: e.transpose(out=TB[0:32, 256:384], in_=Kb[:, r_, 257:289], identity=ident_b[:]), r=[kbk, ident_b], w=[tbk])
                P.op("act", lambda e: e.activation(out=KT[:, 0:2, :], in_=TB[:, 0:256].rearrange("p (c k) -> p c k", k=128), func=AF.Copy),
                     r=[tbk], w=[ktk])
                P.op("dve", lambda e: e.tensor_copy(out=KT[0:32, 2, :], in_=TB[0:32, 256:384]), r=[tbk], w=[ktk])

            def stC(t, qs=qs):
                ti = t % 2
                KT, ktk = KTs[ti], (AR, "KT%d" % ti)
                pS = PS[ti]
                for kc, K in ((0, 128), (1, 128), (2, 32)):
                    P.op("pe", lambda e, kc=kc, K=K: e.matmul(pS[:, 0:128], lhsT=KT[0:K, kc, :], rhs=qs(kc, K), start=(kc == 0), stop=(kc == 2)),
                         r=[ktk, (AR, "qabs")], w=[pS])
                PQ = PTq[ti]
                P.op("act", lambda e: e.activation(out=PQ[:], in_=pS[:, 0:128], func=AF.Exp), r=[pS], w=[(AR, "PQ%d" % ti)])

            def stE(t):
                g, r_, ti = t // 8, t % 8, t % 2
                Kb, kbk = Kbs[g % 2], (AR, "Kb%d" % (g % 2))
                PQ = PTq[ti]
                P.op("pe", lambda e: e.matmul(PS[6][:, 0:257], lhsT=PQ[:], rhs=Kb[:, r_, 0:257], start=(t == 0), stop=False),
                     r=[(AR, "PQ%d" % ti), kbk], w=[PS[6]])
            gather(0)
            NT_ = 128
            for i in range(-2, NT_):
                if i + 2 < NT_:
                    stA(i + 2)
                if 0 <= i + 1 < NT_:
                    stC(i + 1)
                if i >= 0:
                    stE(i)
            pS = PS[0]
            P.op("pe", lambda e, pS=pS, qs=qs: e.matmul(pS[:, 0:128], lhsT=ckv_b[:, 0, 0:128], rhs=qs(0, 128), start=True, stop=False),
                 r=[ckv_b, (AR, "qabs")], w=[pS])
            P.op("pe", lambda e, pS=pS, qs=qs: e.matmul(pS[:, 0:128], lhsT=ckv_b[:, 1, 0:128], rhs=qs(1, 128), start=False, stop=False),
                 r=[ckv_b, (AR, "qabs")], w=[pS])
            P.op("pe", lambda e, pS=pS, qs=qs: e.matmul(pS[:, 0:128], lhsT=kpe_b[0:32, 0:128], rhs=qs(2, 32), start=False, stop=True),
                 r=[kpe_b, (AR, "qabs")], w=[pS])
            pi = 0
            PQ = PTq[pi]
            P.op("act", lambda e, pS=pS, PQ=PQ: e.activation(out=PQ[:], in_=pS[:, 0:128], func=AF.Exp), r=[pS], w=[(AR, "PQ%d" % pi)])
            P.op("dve", lambda e, PQ=PQ, s=s: e.tensor_tensor(
                out=PQ[:].rearrange("p (t h) -> p t h", h=NH), in0=PQ[:].rearrange("p (t h) -> p t h", h=NH),
                in1=triS_b[:, s * 8:(s + 1) * 8].unsqueeze(2).to_broadcast([128, 8, NH]), op=ALU.mult),
                r=[(AR, "PQ%d" % pi), triS_b], w=[(AR, "PQ%d" % pi)])
            P.op("pe", lambda e, PQ=PQ: e.matmul(PS[6][:, 0:257], lhsT=PQ[:], rhs=Knew[:, 0:257], start=False, stop=True),
                 r=[(AR, "PQ%d" % pi), (AR, "Knew")], w=[PS[6]])
            P.op("dve", lambda e: e.reciprocal(out=rec[:, 0:1], in_=PS[6][:, 256:257]), r=[PS[6]], w=[rec])
            P.op("dve", lambda e: e.tensor_scalar(out=olat[:], in0=PS[6][:, 0:256], scalar1=rec[:, 0:1], scalar2=None, op0=ALU.mult),
                 r=[PS[6], rec], w=[(AR, "sa")])
            for kc in range(2):
                P.op("pe", lambda e, kc=kc: e.transpose(out=PSB[:, kc * 128:(kc + 1) * 128], in_=olat[:, kc * 128:(kc + 1) * 128],
                                                        identity=ident_b[:]), r=[(AR, "sa"), ident_b], w=[PSB])
            P.op("act", lambda e: e.activation(out=olatT[:], in_=PSB[:, 0:256].rearrange("p (c q) -> p c q", q=128), func=AF.Copy),
                 r=[PSB], w=[(AR, "sa")])
            for h in range(NH):
                for kc in range(2):
                    c0 = h * 128 + 64 if h % 2 == 0 else h * 128
                    P.op("pe", lambda e, kc=kc, h=h, c0=c0: e.matmul(
                        PS[3][:, h * 8:(h + 1) * 8], lhsT=wuv_res[:, kc, c0:c0 + 128],
                        rhs=olatT[:, kc, :].rearrange("p (t h) -> p t h", h=NH)[:, :, h], start=(kc == 0), stop=(kc == 1),
                        skip_group_check=True), r=[wuv_key, (AR, "sa")], w=[PS[3]])
            pv = PS[3][:, 0:128].rearrange("p (f two t) -> p f two t", two=2, t=8)
            P.op("dve", lambda e, s=s, pv=pv: e.tensor_copy(out=big[0:64, 0:8, s * 8:(s + 1) * 8], in_=pv[0:64, :, 0, :]),
                 r=[PS[3]], w=[big])
            P.op("dve", lambda e, s=s, pv=pv: e.tensor_copy(out=big[64:128, 0:8, s * 8:(s + 1) * 8], in_=pv[64:128, :, 1, :]),
                 r=[PS[3]], w=[big])

    wuv_res = big[:, 16:24, :].rearrange("p a b -> p (a b)").rearrange("p (kc n) -> p kc n", n=2048)
    wuv_key = big

    def load_wuv():
        src = w_ukv.rearrange("(kc p) n -> p kc n", p=128)
        P.dma("pool", lambda e: e.dma_start(out=wuv_res, in_=src), w=[big])

    def ssd(T, grp):
        nseq = 1 if grp == "p" else NSS
        LS = 128 // nseq
        TRI = triU if grp == "p" else triS
        BLK = ones_f if grp == "p" else blkS
        nch = T // 128
        xc = big
        barrier([(AR, "K"), (AR, "V"), (AR, "otm"), (AR, "PT0"), (AR, "PT1"), (AR, "PT2"), (AR, "PT3"), (AR, "sa"), (AR, "Kb0"), (AR, "Kb1"), (AR, "KT0"), (AR, "KT1"),
                 (AR, "PQ0"), (AR, "PQ1"), (AR, "qabs"), (AR, "Knew"), (AR, "vst"), (AR, "kst")], [(AR, "ssd")])
        SK = (AR, "ssd")
        bres, buf = panel(w_in, D, O_DT, 32)
        for t in range(nch):
            for kc in range(8):
                P.op("pe", lambda e, kc=kc, t=t, buf=buf: e.matmul(PS[2][:, t * 32:(t + 1) * 32], lhsT=hT[:, kc, t * 128:(t + 1) * 128],
                                                                  rhs=buf[:, kc, 0:32], start=(kc == 0), stop=(kc == 7)),
                     r=[bres, hT], w=[PS[2]])
            P.op("dve", lambda e, t=t: e.tensor_tensor(out=dtt[:, t, :], in0=PS[2][:, t * 32:(t + 1) * 32], in1=hv[:, 0:32], op=ALU.add),
                 r=[PS[2], hv], w=[dtt])
        P.op("act", lambda e: e.activation(out=dtt[:, 0:nch, :], in_=dtt[:, 0:nch, :], func=AF.Exp), r=[dtt], w=[dtt])
        P.op("act", lambda e: e.activation(out=dtt[:, 0:nch, :], in_=dtt[:, 0:nch, :], func=AF.Ln, bias=one_t[:, 0:1], scale=1.0),
             r=[dtt, one_t], w=[dtt])
        if grp == "s":
            P.dma("sp", lambda e: e.dma_start(out=stg[0:48, 0:1024], in_=st_conv[:, 0:1024]), w=[stg])
            for part in range(3):
                if part:
                    P.dma("sp", lambda e, part=part: e.dma_start(out=stg[0:48, 0:1024], in_=st_conv[:, part * 1024:(part + 1) * 1024]),
                          w=[stg])
                for q in range(8):
                    P.op("pe", lambda e, q=q: e.transpose(out=PS[5][:, q * 48:(q + 1) * 48], in_=stg[0:48, q * 128:(q + 1) * 128],
                                                          identity=ident_f[0:48, 0:48]), r=[stg, ident_f], w=[PS[5]])
                P.op("dve", lambda e, part=part: e.tensor_copy(out=histf[:, part * 8:(part + 1) * 8, :],
                                                               in_=PS[5][:, 0:384].rearrange("p (q x) -> p q x", x=48)),
                     r=[PS[5]], w=[histf])

        def ev_x(fc, m, pt):
            raw = rot("raw", rawt)
            acc = rot("tA", tA)
            if grp == "p":
                P.op("act", lambda e: e.activation(out=raw[:, 3:3 + T], in_=pt[:, 0:T], func=AF.Copy), r=[pt], w=[raw])
                P.op("dve", lambda e: e.tensor_copy(out=raw[:, 0:3], in_=histb[:, fc, :]), r=[histb], w=[raw])
                P.op("dve", lambda e: e.tensor_copy(out=histf[:, fc, 0:3], in_=pt[:, T - 3:T]), r=[pt], w=[histf])
                P.op("dve", lambda e: e.tensor_copy(out=histb[:, fc, :], in_=raw[:, T:T + 3]), r=[raw], w=[histb])
                win = lambda k: raw[:, k:k + T]
                accv = acc[:, 0:T]
                outv = xc[:, fc, 0:T]
            else:
                r3 = raw[:, 0:NSS * 11].rearrange("p (s x) -> p s x", x=11)
                P.op("act", lambda e: e.activation(out=r3[:, :, 3:11], in_=pt[:, 0:T].rearrange("p (s t) -> p s t", t=TS), func=AF.Copy),
                     r=[pt], w=[raw])
                P.op("dve", lambda e: e.tensor_copy(out=r3[:, :, 0:3], in_=histf[:, fc, :].rearrange("p (s j) -> p s j", j=3)),
                     r=[histf], w=[raw])
                P.op("dve", lambda e: e.tensor_copy(out=histf[:, fc, :].rearrange("p (s j) -> p s j", j=3),
                                                    in_=pt[:, 0:T].rearrange("p (s t) -> p s t", t=TS)[:, :, 5:8]),
                     r=[pt, raw], w=[histf])
                win = lambda k: r3[:, :, k:k + TS]
                accv = acc[:, 0:T].rearrange("p (s t) -> p s t", t=TS)
                outv = xc[:, fc, 0:T].rearrange("p (s t) -> p s t", t=TS)
            P.op("dve", lambda e: e.tensor_scalar(out=accv, in0=win(0), scalar1=vB[:, 6 + fc:7 + fc], scalar2=None, op0=ALU.mult),
                 r=[raw, vB], w=[acc])
            for k in (1, 2, 3):
                P.op("dve", lambda e, k=k: e.scalar_tensor_tensor(out=accv, in0=win(k), scalar=vB[:, 6 + k * 24 + fc:7 + k * 24 + fc],
                                                                  in1=accv, op0=ALU.mult, op1=ALU.add), r=[raw, vB, acc], w=[acc])
            P.op("act", lambda e: e.activation(out=outv, in_=accv, func=AF.Silu, bias=vB[:, 102 + fc:103 + fc], scale=1.0),
                 r=[acc, vB], w=[xc])
        linear(w_in, D, O_XBC, CONV, lambda kc: hT[:, kc, 0:T], hT, T, ev_x)

        for c in range(nch):
            tok = slice(c * 128, (c + 1) * 128)
            P.op("dve", lambda e, c=c: e.tensor_tensor(out=lat[:], in0=dtt[:, c, :], in1=a_bc[:], op=ALU.mult), r=[dtt, a_bc], w=[lat])
            P.op("pe", lambda e: e.matmul(PS[2][:, 0:32], lhsT=TRI[:], rhs=lat[:], start=True, stop=True), r=[TRI, lat], w=[PS[2]])
            P.op("pe", lambda e: e.matmul(PS[2][:, 32:64], lhsT=BLK[:], rhs=lat[:], start=True, stop=True), r=[BLK, lat], w=[PS[2]])
            P.op("dve", lambda e: e.tensor_copy(out=cumt[:], in_=PS[2][:, 0:32]), r=[PS[2]], w=[cumt])
            P.op("dve", lambda e: e.tensor_scalar(out=ncum[:], in0=PS[2][:, 0:32], scalar1=-1.0, scalar2=None, op0=ALU.mult),
                 r=[PS[2]], w=[ncum])
            P.op("dve", lambda e: e.tensor_tensor(out=wdec[:], in0=PS[2][:, 32:64], in1=cumt[:], op=ALU.subtract),
                 r=[PS[2], cumt], w=[wdec])
            P.op("act", lambda e: e.activation(out=wdec[:], in_=wdec[:], func=AF.Exp), r=[wdec], w=[wdec])
            for half in range(2):
                for q in range(8):
                    fc = half * 8 + q
                    P.op("pe", lambda e, fc=fc, q=q, tok=tok: e.transpose(out=PSB[:, q * 128:(q + 1) * 128], in_=xc[:, fc, tok],
                                                                        identity=ident_b[:]), r=[xc, ident_b], w=[PSB])
                P.op("dve", lambda e, half=half, c=c: e.tensor_tensor(
                    out=xdt[:, half * 1024:(half + 1) * 1024].rearrange("p (h x) -> p h x", x=64),
                    in0=PSB[:].rearrange("p (h x) -> p h x", x=64),
                    in1=dtt[:, c, half * 16:(half + 1) * 16].unsqueeze(2).to_broadcast([128, 16, 64]), op=ALU.mult),
                    r=[PSB, dtt], w=[SK])
            P.op("dve", lambda e: e.tensor_tensor(out=xdtw.rearrange("p (h x) -> p h x", x=64), in0=xdt.rearrange("p (h x) -> p h x", x=64),
                                                  in1=wdec[:].unsqueeze(2).to_broadcast([128, 32, 64]), op=ALU.mult),
                 r=[SK, wdec], w=[SK])
            for g in range(4):
                P.op("pe", lambda e, g=g, tok=tok: e.transpose(out=PSB[:, g * 128:(g + 1) * 128], in_=xc[:, 16 + g, tok], identity=ident_b[:]),
                     r=[xc, ident_b], w=[PSB])
            P.op("act", lambda e: e.activation(out=Btm, in_=PSB[:, 0:512].rearrange("p (g n) -> p g n", n=128), func=AF.Copy),
                 r=[PSB], w=[SK])
            for g in range(4):
                P.op("pe", lambda e, g=g, tok=tok: e.matmul(PS[3][:, g * 128:(g + 1) * 128], lhsT=xc[:, 16 + g, tok], rhs=xc[:, 20 + g, tok],
                                                          start=True, stop=True), r=[xc], w=[PS[3]])
            P.op("dve", lambda e: e.tensor_tensor(out=Gm[:], in0=PS[3][:].rearrange("p (g l) -> p g l", l=128),
                                                  in1=TRI[:].unsqueeze(1).to_broadcast([128, 4, 128]), op=ALU.mult),
                 r=[PS[3], TRI], w=[Gm])
            def st1(h):
                dgh, pc = dghs[h % 2], PS[h % 2]
                P.op("dve", lambda e: e.tensor_scalar(out=dgh[:], in0=ident_f[:], scalar1=cumt[:, h:h + 1], scalar2=None, op0=ALU.mult),
                     r=[ident_f, cumt], w=[dgh])
                P.op("pe", lambda e: e.matmul(pc[:, 0:128], lhsT=ones_f[:], rhs=dgh[:], start=True, stop=True),
                     r=[ones_f, dgh], w=[pc])

            def st2(h):
                pc, segt, ecr = PS[h % 2], segts[h % 2], ecrs[h % 2]
                P.op("dve", lambda e: e.tensor_scalar(out=segt[:], in0=pc[:, 0:128], scalar1=ncum[:, h:h + 1], scalar2=0.0,
                                                      op0=ALU.add, op1=ALU.min), r=[pc, ncum], w=[segt])
                P.op("act", lambda e: e.activation(out=ecr[:], in_=pc[:, 0:128], func=AF.Exp), r=[pc], w=[ecr])
                P.op("act", lambda e: e.activation(out=segt[:], in_=segt[:], func=AF.Exp), r=[segt], w=[segt])

            def st3(h, tok=tok):
                g = h // 8
                segt, ecr = segts[h % 2], ecrs[h % 2]
                P.op("pool", lambda e: e.tensor_tensor(out=MTs[:, h, :], in0=segt[:], in1=Gm[:, g, :], op=ALU.mult),
                     r=[segt, Gm], w=[SK])
                P.op("pool", lambda e: e.tensor_tensor(out=Css[:, h, :], in0=ecr[:], in1=xc[:, 20 + g, tok], op=ALU.mult),
                     r=[ecr, xc], w=[SK])
                P.op("dve", lambda e: e.tensor_copy(out=edch[:, h, 0:nseq], in_=ecr[:, LS - 1::LS]), r=[ecr], w=[edch])
            for i in range(-2, SH):
                if i + 2 < SH:
                    st1(i + 2)
                if 0 <= i + 1 < SH:
                    st2(i + 1)
                if i >= 0:
                    st3(i)
            HG = 4 if grp == "p" else 32
            for s in range(nseq):
                cols = slice(s * LS, (s + 1) * LS)
                if grp == "s":
                    for hf in range(2):
                        P.dma("sp", lambda e, s=s, hf=hf: e.dma_start(
                            out=stg[:].rearrange("p (f n) -> p f n", n=128),
                            in_=st_ssm[s, hf * 1024:(hf + 1) * 1024, :].rearrange("(f p) n -> p f n", p=128)), w=[stg])
                        for q4 in range(2):
                            for q in range(4):
                                f = q4 * 4 + q
                                P.op("pe", lambda e, f=f, q=q: e.transpose(out=PS[5][:, q * 128:(q + 1) * 128],
                                                                           in_=stg[:, f * 128:(f + 1) * 128], identity=ident_f[:]),
                                     r=[stg, ident_f], w=[PS[5]])
                            o0 = (hf * 2 + q4) * 512
                            P.op("act", lambda e, o0=o0: e.activation(out=hstate[:, o0:o0 + 512], in_=PS[5][:], func=AF.Copy),
                                 r=[PS[5]], w=[hstate])
                            P.op("dve", lambda e, o0=o0: e.tensor_copy(out=hstate_b[:, o0:o0 + 512], in_=PS[5][:]),
                                 r=[PS[5]], w=[hstate_b])
                for hg in range(SH // HG):
                    py = rot("py", [PS[2], PS[3]])
                    for hh in range(HG):
                        h = hg * HG + hh
                        pr = (h // 2) * 128
                        reg = py[:, hh * LS:(hh + 1) * LS]
                        P.op("pe", lambda e, h=h, pr=pr, reg=reg, cols=cols, hh=hh: e.matmul(
                            reg, lhsT=xdt[:, pr:pr + 128], rhs=MTs[:, h, cols], start=(hh == 0), stop=False, skip_group_check=True),
                            r=[SK], w=[py])
                        P.op("pe", lambda e, h=h, pr=pr, reg=reg, cols=cols: e.matmul(
                            reg, lhsT=hstate_b[:, pr:pr + 128], rhs=Css[:, h, cols], start=False, stop=True, skip_group_check=True),
                            r=[SK, hstate_b], w=[py])
                    nf = HG // 2
                    f0 = hg * nf
                    pv = py[:, 0:HG * LS].rearrange("p (f two l) -> p f two l", two=2, l=LS)
                    for (rows, two) in ((slice(0, 64), 0), (slice(64, 128), 1)):
                        t_ = rot("tA", tA)
                        tv = t_[:, 0:nf * LS].rearrange("p (f l) -> p f l", l=LS)
                        xv = xc[rows, f0:f0 + nf, c * 128 + s * LS:c * 128 + (s + 1) * LS]
                        P.op("dve", lambda e, rows=rows, tv=tv, xv=xv, f0=f0, nf=nf: e.tensor_tensor(
                            out=tv[rows], in0=xv, in1=vC[rows, 16 + f0:16 + f0 + nf].unsqueeze(2).to_broadcast([64, nf, LS]), op=ALU.mult),
                            r=[xc, vC], w=[t_])
                        P.op("dve", lambda e, rows=rows, tv=tv, two=two, pv=pv, f0=f0, nf=nf, c=c, s=s: e.tensor_tensor(
                            out=R1[rows, f0:f0 + nf, c * 128 + s * LS:c * 128 + (s + 1) * LS], in0=tv[rows], in1=pv[rows, :, two, :],
                            op=ALU.add), r=[t_, py], w=[R1])
                if grp == "s":
                    P.op("dve", lambda e, s=s: e.tensor_scalar(out=Bms, in0=Btm, scalar1=blkS[:, s * 8:s * 8 + 1], scalar2=None, op0=ALU.mult),
                         r=[SK, blkS], w=[SK])
                    Bl = Bms
                else:
                    Bl = Btm
                for g in range(4):
                    pu = rot("pa", [PS[0], PS[1]])
                    P.op("pe", lambda e, g=g, pu=pu, Bl=Bl: e.matmul(pu[:, 0:512], lhsT=Bl[:, g, :], rhs=xdtw[:, g * 512:(g + 1) * 512],
                                                                  start=True, stop=True), r=[SK], w=[pu])
                    hsv = hstate[:, g * 512:(g + 1) * 512].rearrange("p (h x) -> p h x", x=64)
                    P.op("dve", lambda e, g=g, hsv=hsv, s=s: e.tensor_tensor(
                        out=hsv, in0=hsv, in1=edch[:, g * 8:(g + 1) * 8, s:s + 1].to_broadcast([128, 8, 64]), op=ALU.mult),
                        r=[hstate, edch], w=[hstate])
                    P.op("dve", lambda e, g=g, pu=pu: e.tensor_tensor(out=hstate[:, g * 512:(g + 1) * 512], in0=hstate[:, g * 512:(g + 1) * 512],
                                                                      in1=pu[:, 0:512], op=ALU.add), r=[hstate, pu], w=[hstate])
                if grp == "s":
                    store_state(ssms[s], "ssms")
                else:
                    P.op("act", lambda e: e.activation(out=hstate_b[:], in_=hstate[:], func=AF.Copy), r=[hstate], w=[hstate_b])

    rawt = sq

    def store_state(dst, dname):
        for q4 in range(4):
            for q in range(4):
                f = q4 * 4 + q
                P.op("pe", lambda e, f=f, q=q: e.transpose(out=PS[5][:, q * 128:(q + 1) * 128], in_=hstate[:, f * 128:(f + 1) * 128],
                                                           identity=ident_f[:]), r=[hstate, ident_f], w=[PS[5]])
            P.op("act", lambda e, q4=q4: e.activation(out=stg[:, 0:512], in_=PS[5][:], func=AF.Copy), r=[PS[5]], w=[stg])
            P.dma("sp", lambda e, q4=q4: e.dma_start(out=dst[q4 * 512:(q4 + 1) * 512, :].rearrange("(f p) n -> p f n", p=128),
                                                     in_=stg[:, 0:512].rearrange("p (f n) -> p f n", n=128)), r=[stg], w=[dname])

    def store_conv(grp):
        n = 3 if grp == "p" else 48
        dst = convp if grp == "p" else convs
        for part in range(6):
            for q in range(4):
                fc = part * 4 + q
                P.op("pe", lambda e, fc=fc, q=q: e.transpose(out=PS[5][0:n, q * 128:(q + 1) * 128], in_=histf[:, fc, 0:n],
                                                             identity=ident_f[:]), r=[histf, ident_f], w=[PS[5]])
            P.op("act", lambda e: e.activation(out=stg[0:n, 0:512], in_=PS[5][0:n, :], func=AF.Copy), r=[PS[5]], w=[stg])
            P.dma("sp", lambda e, part=part: e.dma_start(out=dst[:, part * 512:(part + 1) * 512], in_=stg[0:n, 0:512]),
                  r=[stg], w=["conv" + grp])

    def mix_out(T, segs):
        def ev_oa(i, m, pt):
            P.op("act", lambda e: e.activation(out=fT[:, i, 0:T], in_=pt[:, 0:T], func=AF.Copy), r=[pt], w=[fT])
        linear(w_oa, D, 0, D, lambda kc: big[:, kc, 0:T], big, T, ev_oa)

    def mix_out2(T, segs):
        yT = R1
        def ev_z(i, m, pt):
            t_ = rot("sq", sq)
            P.op("act", lambda e: e.activation(out=t_[:, 0:T], in_=pt[:, 0:T], func=AF.Silu), r=[pt], w=[t_])
            P.op("dve", lambda e: e.tensor_tensor(out=yT[:, i, 0:T], in0=yT[:, i, 0:T], in1=t_[:, 0:T], op=ALU.mult),
                 r=[yT, t_], w=[yT])
        linear(w_in, D, O_Z, SI, lambda kc: hT[:, kc, 0:T], hT, T, ev_z)
        rms_rstd(yT, 16, T, SI, rstd)
        for c in range(16):
            P.op("dve", lambda e, c=c: e.tensor_scalar(out=yT[:, c, 0:T], in0=yT[:, c, 0:T], scalar1=vC[:, c:c + 1], scalar2=None,
                                                       op0=ALU.mult), r=[yT, vC], w=[yT])
        def ev_ga(i, m, pt):
            t_ = rot("tA", tA)
            P.op("act", lambda e: e.activation(out=t_[:, 0:T], in_=pt[:, 0:T], func=AF.Sigmoid), r=[pt], w=[t_])
            P.op("dve", lambda e: e.tensor_tensor(out=fT[:, i, 0:T], in0=fT[:, i, 0:T], in1=t_[:, 0:T], op=ALU.mult),
                 r=[fT, t_], w=[fT])
        linear(w_in, D, O_GA, D, lambda kc: hT[:, kc, 0:T], hT, T, ev_ga)
        gs = big
        def ev_gs(i, m, pt):
            P.op("act", lambda e: e.activation(out=big[:, 8 + i, 0:T], in_=pt[:, 0:T], func=AF.Sigmoid), r=[pt], w=[big])
        linear(w_in, D, O_GS, D, lambda kc: hT[:, kc, 0:T], hT, T, ev_gs)
        def ev_os(i, m, pt):
            t_ = rot("tA", tA)
            P.op("dve", lambda e: e.tensor_tensor(out=t_[:, 0:T], in0=pt[:, 0:T], in1=rstd[:, 0:T], op=ALU.mult), r=[pt, rstd], w=[t_])
            P.op("dve", lambda e: e.tensor_tensor(out=t_[:, 0:T], in0=t_[:, 0:T], in1=big[:, 8 + i, 0:T], op=ALU.mult), r=[t_, big], w=[t_])
            P.op("dve", lambda e: e.tensor_tensor(out=fT[:, i, 0:T], in0=fT[:, i, 0:T], in1=t_[:, 0:T], op=ALU.add), r=[fT, t_], w=[fT])
        linear(w_os, SI, 0, D, lambda kc: yT[:, kc, 0:T], yT, T, ev_os)
        P.op("act", lambda e: e.activation(out=hT[:, :, 0:T], in_=fT[:, :, 0:T], func=AF.Copy), r=[fT], w=[hT])

        def ev_o(i, m, pt):
            P.op("act", lambda e: e.activation(out=fT[:, i, 0:T], in_=pt[:, 0:T], func=AF.Copy), r=[pt], w=[fT])
        linear(w_out, D, 0, D, lambda kc: hT[:, kc, 0:T], hT, T, ev_o)
        residual(1, T, segs)

    def dump_fm(src, nchunk, T, dst, dname, bf=False):
        for t in range(T // 128):
            for c0 in range(0, nchunk, 4):
                for q in range(4):
                    if bf:
                        P.op("pe", lambda e, c0=c0, q=q, t=t: e.transpose(out=PSB[:, q * 128:(q + 1) * 128],
                                                                          in_=src[:, c0 + q, t * 128:(t + 1) * 128], identity=ident_b[:]),
                             r=[src, ident_b], w=[PSB])
                    else:
                        P.op("pe", lambda e, c0=c0, q=q, t=t: e.transpose(out=PS[5][:, q * 128:(q + 1) * 128],
                                                                          in_=src[:, c0 + q, t * 128:(t + 1) * 128], identity=ident_f[:]),
                             r=[src, ident_f], w=[PS[5]])
                P.op("act", lambda e: e.activation(out=stg[:, 0:512], in_=(PSB[:, 0:512] if bf else PS[5][:]), func=AF.Copy),
                     r=[PSB if bf else PS[5]], w=[stg])
                P.dma("sp", lambda e, c0=c0, t=t: e.dma_start(out=dst[t * 128:(t + 1) * 128, c0 * 128:(c0 + 4) * 128], in_=stg[:, 0:512]),
                      r=[stg], w=[dname])

    def program():
        wstate["i"] = 0
        cnt.clear()
        constants()
        if not P.dry:
            convert_weights()
        adaln()
        for b in range(n_pblk):
            T = 512
            segs = [(0, 0, T)]
            load_xT(xp[b * 512:(b + 1) * 512, :], T)
            ffn(0, T, segs)
            mixer_kvq(T, segs, b * 512, "p", b * 512, part="kv")
            if stage >= 5:
                kv_gen(b)
            mixer_kvq(T, segs, b * 512, "p", b * 512, part="q")
            if stage >= 5:
                attn_prompt(b)
                mix_out(T, segs)
                if dbg and dbg == "p%d" % b:
                    dump_fm(fT, 8, T, dbg1, "dbg1")
            if stage >= 6:
                ssd(T, "p")
                if dbg and dbg == "p%d" % b:
                    dump_fm(R1, 16, T, dbg2, "dbg2", bf=True)
            if stage >= 7:
                mix_out2(T, segs)
                ffn(2, T, segs)
                store_xT(yp[b * 512:(b + 1) * 512, :], T, "yp")
        if n_pblk and stage >= 6:
            store_state(ssmp, "ssmp")
            store_conv("p")
        if do_sample:
            T = NSS * TS
            segs = [(1 + i, i * TS, TS) for i in range(NSS)]
            load_xT(xs, T)
            ffn(0, T, segs)
            mixer_kvq(T, segs, None, "s", 0)
            if stage >= 5:
                load_wuv()
                attn_sample()
                barrier([("fTg", 0), ("fTg", 1)], [fT])
                mix_out(T, segs)
                if dbg == "s":
                    dump_fm(fT, 8, T, dbg1, "dbg1")
            if stage >= 6:
                ssd(T, "s")
                store_conv("s")
                if dbg == "s":
                    dump_fm(R1, 16, T, dbg2, "dbg2", bf=True)
            if stage >= 7:
                mix_out2(T, segs)
                ffn(2, T, segs)
                store_xT(ys, T, "ys")

    P.dry = True
    program()
    P.dry = False
    wstate["issued"] = 0
    program()
    P.emit()
    return nc


def _prep_inputs(inp, need_cache=True):
    f = lambda k: np.ascontiguousarray(np.asarray(inp[k], dtype=np.float32))
    half = ROPE // 2
    inv = (10000.0 ** (-np.arange(half, dtype=np.float32) / half)).astype(np.float32)
    invf = np.tile(inv, 8).reshape(128, 1).astype(np.float32)
    vecA = np.concatenate([f("b_ada").reshape(72, 128), f("g_pre").reshape(24, 128), f("g_post").reshape(24, 128)], 0)
    vecB = np.concatenate([f("g_q_lat").reshape(4, 128), f("g_kv_lat").reshape(2, 128),
                           f("conv_w").reshape(96, 128), f("conv_b").reshape(24, 128)], 0)
    vecC = np.concatenate([f("g_ssm_norm").reshape(16, 128), np.repeat(f("d_skip").reshape(32), 64).reshape(16, 128)], 0)
    hvec = np.tile(np.concatenate([f("dt_bias").reshape(32), f("a_log").reshape(32), f("d_skip").reshape(32)])[None, :], (128, 1))
    shared = dict(
        invf=invf, w_ada=f("w_ada")[0], vecA=np.ascontiguousarray(vecA), vecB=np.ascontiguousarray(vecB),
        vecC=np.ascontiguousarray(vecC), hvec=np.ascontiguousarray(hvec.astype(np.float32)),
        w_gate=f("w_ffn_gate")[0].reshape(2 * D, DFF), w_up=f("w_ffn_up")[0].reshape(2 * D, DFF),
        w_down=f("w_ffn_down")[0].reshape(2 * DFF, D), w_in=f("w_in")[0],
        w_uq=f("w_uq")[0], w_ukv=f("w_ukv")[0], w_oa=f("w_o_attn")[0], w_os=f("w_o_ssm")[0], w_out=f("w_out")[0],
    )
    if need_cache:
        shared["cache_kv"] = np.asarray(inp["cache_kv_latent"], dtype=np.float32).reshape(20480, 128 * KVL)
        shared["cache_pe"] = np.asarray(inp["cache_k_rope"], dtype=np.float32).reshape(20480, 128 * ROPE)
    else:
        shared["cache_kv"] = np.zeros((1, 1), np.float32)
        shared["cache_pe"] = np.zeros((1, 1), np.float32)
    xpa, xsa, cpa, csa = f("x_prompt"), f("x_sample"), f("c_prompt"), f("c_sample")
    pt = np.asarray(inp["page_table"], dtype=np.int32)
    stc, sts = f("state_conv")[0], np.asarray(inp["state_ssm"], dtype=np.float32)[0]
    maps = []
    for c in range(8):
        m = dict(shared)
        sl = slice(c * NSS, (c + 1) * NSS)
        m["xp"] = xpa[c % 4]
        m["xs"] = xsa[sl].reshape(NSS * TS, D)
        m["cc"] = np.concatenate([cpa[c % 4:c % 4 + 1], csa[sl]], 0)
        m["ptT"] = np.ascontiguousarray(pt[sl].T)
        m["st_conv"] = np.ascontiguousarray(stc[sl].reshape(NSS * 3, CONV))
        m["st_ssm"] = np.ascontiguousarray(sts[sl].reshape(NSS, SI, SN))
        maps.append(m)
    return maps


def run(inp, need_cache=True, dev_small=None, trace=False, **bk):
    nc = build(**bk)
    maps = _prep_inputs(inp, need_cache)
    if dev_small is not None:
        ckv_s, cpe_s = dev_small
        for m in maps:
            for k in ("xs", "cc", "st_conv", "st_ssm"):
                m[k] = maps[0][k] if k != "cc" else np.concatenate([m["cc"][0:1], maps[0]["cc"][1:]], 0)
            m["cache_kv"], m["cache_pe"] = ckv_s, cpe_s
            m["ptT"] = np.ascontiguousarray(np.arange(2048, dtype=np.int32).reshape(16, 128).T)
    if trace:
        res = run_bass_kernel_spmd(nc, maps, core_ids=list(range(8)), trace=True)
        print("EXEC_TIME_NS", res.exec_time_ns)
        return res.results
    res = run_bass_kernel_spmd(nc, maps, core_ids=list(range(8)))
    return res.results


def kernel(**inp):
    r = run(inp)
    cat = lambda k, cores: np.stack([r[c][k] for c in cores], 0)
    y_p = cat("yp", range(4))
    y_s = np.concatenate([r[c]["ys"].reshape(NSS, TS, D) for c in range(8)], 0)
    kv_p = cat("kvp", range(4))[None]
    pe_p = cat("pep", range(4))[None]
    cv_p = cat("convp", range(4))[None]
    ss_p = cat("ssmp", range(4)).reshape(1, 4, SH, SHD, SN)
    kv_s = np.concatenate([r[c]["kvs"].reshape(NSS, TS, KVL) for c in range(8)], 0)[None]
    pe_s = np.concatenate([r[c]["pes"].reshape(NSS, TS, ROPE) for c in range(8)], 0)[None]
    cv_s = np.concatenate([r[c]["convs"].reshape(NSS, 3, CONV) for c in range(8)], 0)[None]
    ss_s = np.concatenate([r[c]["ssms"].reshape(NSS, SH, SHD, SN) for c in range(8)], 0)[None]
    return tuple(np.ascontiguousarray(a, dtype=np.float32) for a in (y_p, y_s, kv_p, pe_p, cv_p, ss_p, kv_s, pe_s, cv_s, ss_s))
```
